# Optimizing a Trainium2 kernel written in Bass

```python
import jax
import jax.numpy as jnp
from jax import lax
import numpy as np

D_MODEL = 2048
BATCH = 4
SEQ = 4096
DEPTH = 2

GRID_W = 64
CTX_LEN = 256
HEAD_DIM = 128
BLOCK = 128
NA_HEADS = 4
NA_WIN_R = 8
NA_WIN_C = 16
SWA_HEADS = 4
SWA_KV_HEADS = 2
SWA_WINDOW = 128
MLA_HEADS = 4
MLA_Q_LORA = 384
MLA_KV_LORA = 128
MLA_NOPE = 128
MLA_ROPE = 64
MLA_V = 128
GQA_HEADS = 4
GQA_KV_HEADS = 2
MIX_WIDTH = (NA_HEADS + SWA_HEADS + GQA_HEADS) * HEAD_DIM + MLA_HEADS * MLA_V
D_FF = 5632
CONV_W = 3
ROPE_THETA = 10000.0
EPS = 1e-6
NEG = -1e30
DEEPNORM_ALPHA = (2 * DEPTH) ** 0.25
DEEPNORM_BETA = (8 * DEPTH) ** -0.25
IN_SIZES = (
    NA_HEADS * HEAD_DIM, NA_HEADS * HEAD_DIM, NA_HEADS * HEAD_DIM,
    SWA_HEADS * HEAD_DIM, SWA_KV_HEADS * HEAD_DIM, SWA_KV_HEADS * HEAD_DIM,
    MLA_Q_LORA, MLA_KV_LORA, MLA_ROPE,
    GQA_HEADS * HEAD_DIM, GQA_KV_HEADS * HEAD_DIM, GQA_KV_HEADS * HEAD_DIM,
)
IN_COLS = sum(IN_SIZES)

kernel_name = 'hybrid_dit_parallel_heads_deepnorm'


def layer_norm(x, g=None, b=None):
    xf = x.astype(jnp.float32)
    mu = jnp.mean(xf, axis=-1, keepdims=True)
    var = jnp.mean(jnp.square(xf - mu), axis=-1, keepdims=True)
    y = (xf - mu) * lax.rsqrt(var + EPS)
    if g is not None:
        y = y * g + b
    return y.astype(x.dtype)


def rms_norm(x, g):
    xf = x.astype(jnp.float32)
    y = xf * lax.rsqrt(jnp.mean(xf * xf, axis=-1, keepdims=True) + EPS) * g
    return y.astype(x.dtype)


def modulate(x, shift, scale):
    return layer_norm(x) * (1 + scale) + shift


def rope_1d(x, pos):
    half = x.shape[-1] // 2
    inv_freq = ROPE_THETA ** (-jnp.arange(half, dtype=jnp.float32) / half)
    ang = pos.astype(jnp.float32)[:, None] * inv_freq[None, :]
    cos, sin = jnp.cos(ang), jnp.sin(ang)
    xf = x.astype(jnp.float32)
    x1, x2 = xf[..., :half], xf[..., half:]
    return jnp.concatenate([x1 * cos - x2 * sin, x2 * cos + x1 * sin], axis=-1).astype(x.dtype)


def rope_2d(x, row, col):
    h = x.shape[-1] // 2
    return jnp.concatenate([rope_1d(x[..., :h], row), rope_1d(x[..., h:], col)], axis=-1)


def split_cols(p):
    out, o = [], 0
    for n in IN_SIZES:
        out.append(p[..., o:o + n])
        o += n
    return out


def to_heads(t, n):
    B, S, _ = t.shape
    return t.reshape(B, S, n, -1).transpose(0, 2, 1, 3)


def from_heads(o):
    B, n, S, d = o.shape
    return o.transpose(0, 2, 1, 3).reshape(B, S, n * d)


def full_attention(q, k, v, scale, sink=None):
    s = jnp.einsum('bhgqd,bhkd->bhgqk', q, k, preferred_element_type=jnp.float32) * scale
    if sink is not None:
        s_sink = jnp.broadcast_to(sink.astype(jnp.float32)[None, :, :, None, None], s.shape[:-1] + (1,))
        s = jnp.concatenate([s, s_sink], axis=-1)
    p = jax.nn.softmax(s, axis=-1)[..., :k.shape[2]]
    return jnp.einsum('bhgqk,bhkd->bhgqd', p.astype(v.dtype), v)


def mixer_neighbourhood(px, pc, rpb, need_ctx):
    q, k, v = (to_heads(t, NA_HEADS) for t in px)
    kc, vc = to_heads(pc[1], NA_HEADS), to_heads(pc[2], NA_HEADS)
    scale = HEAD_DIM ** -0.5
    B, H, S, d = q.shape
    rows = S // GRID_W
    wr = min(NA_WIN_R, rows)
    qg = q.reshape(B, H, rows, GRID_W, d)
    kg = k.reshape(B, H, rows, GRID_W, d)
    vg = v.reshape(B, H, rows, GRID_W, d)
    r = jnp.arange(rows)
    r0 = jnp.clip(r - wr // 2, 0, rows - wr)
    krow = r0[:, None] + jnp.arange(wr)[None, :]
    k_band = kg[:, :, krow]
    v_band = vg[:, :, krow]
    cq = jnp.arange(GRID_W)
    c0 = jnp.clip(cq - NA_WIN_C // 2, 0, GRID_W - NA_WIN_C)
    col_in = (cq[None, :] >= c0[:, None]) & (cq[None, :] < c0[:, None] + NA_WIN_C)
    drow_idx = krow - r[:, None] + NA_WIN_R - 1
    dcol_idx = jnp.clip(cq[None, :] - cq[:, None] + NA_WIN_C - 1, 0, 2 * NA_WIN_C - 2)
    bias = rpb[:, drow_idx[:, None, :, None], dcol_idx[None, :, None, :]]
    s = jnp.einsum('bhrqd,bhrwkd->bhrqwk', qg, k_band, preferred_element_type=jnp.float32) * scale
    s = jnp.where(col_in[:, None, :], s + bias[None], NEG)
    nwin = wr * GRID_W
    s = s.reshape(B, H, rows, GRID_W, nwin)
    s_ctx = jnp.einsum('bhrqd,bhcd->bhrqc', qg, kc, preferred_element_type=jnp.float32) * scale
    p = jax.nn.softmax(jnp.concatenate([s, s_ctx], axis=-1), axis=-1)
    o = (jnp.einsum('bhrqn,bhrnd->bhrqd', p[..., :nwin].astype(v.dtype), v_band.reshape(B, H, rows, nwin, d))
         + jnp.einsum('bhrqc,bhcd->bhrqd', p[..., nwin:].astype(v.dtype), vc))
    out_x = from_heads(o.reshape(B, H, S, d))
    out_c = None
    if need_ctx:
        qc = to_heads(pc[0], NA_HEADS)
        out_c = from_heads(full_attention(qc[:, :, None], kc, vc, scale)[:, :, 0])
    return out_x, out_c


def mixer_sliding(px, pc, row, col, sink, need_ctx):
    G = SWA_HEADS // SWA_KV_HEADS
    q = rope_2d(to_heads(px[0], SWA_HEADS), row, col)
    k = rope_2d(to_heads(px[1], SWA_KV_HEADS), row, col)
    v = to_heads(px[2], SWA_KV_HEADS)
    kc, vc = to_heads(pc[1], SWA_KV_HEADS), to_heads(pc[2], SWA_KV_HEADS)
    sink = sink.reshape(SWA_KV_HEADS, G)
    scale = HEAD_DIM ** -0.5
    B, _, S, d = q.shape
    nb = S // BLOCK
    qb = q.reshape(B, SWA_KV_HEADS, G, nb, BLOCK, d)
    pad = ((0, 0), (0, 0), (BLOCK, BLOCK), (0, 0))
    idx = jnp.arange(nb)[:, None] * BLOCK + jnp.arange(3 * BLOCK)[None, :]
    kb = jnp.pad(k, pad)[:, :, idx]
    vb = jnp.pad(v, pad)[:, :, idx]
    kpos = idx - BLOCK
    qpos = jnp.arange(S).reshape(nb, BLOCK)
    valid = ((jnp.abs(kpos[:, None, :] - qpos[:, :, None]) <= SWA_WINDOW)
             & (kpos >= 0)[:, None, :] & (kpos < S)[:, None, :])
    s = jnp.einsum('bhgnqd,bhnkd->bhgnqk', qb, kb, preferred_element_type=jnp.float32) * scale
    s = jnp.where(valid, s, NEG)
    s_ctx = jnp.einsum('bhgnqd,bhcd->bhgnqc', qb, kc, preferred_element_type=jnp.float32) * scale
    s_sink = jnp.broadcast_to(sink.astype(jnp.float32)[None, :, :, None, None, None], s.shape[:-1] + (1,))
    p = jax.nn.softmax(jnp.concatenate([s, s_ctx, s_sink], axis=-1), axis=-1)
    nk, C = 3 * BLOCK, kc.shape[2]
    o = (jnp.einsum('bhgnqk,bhnkd->bhgnqd', p[..., :nk].astype(v.dtype), vb)
         + jnp.einsum('bhgnqc,bhcd->bhgnqd', p[..., nk:nk + C].astype(v.dtype), vc))
    out_x = from_heads(o.reshape(B, SWA_HEADS, S, d))
    out_c = None
    if need_ctx:
        qc = to_heads(pc[0], SWA_HEADS)
        qc = qc.reshape(B, SWA_KV_HEADS, G, qc.shape[2], d)
        oc = full_attention(qc, kc, vc, scale, sink)
        out_c = from_heads(oc.reshape(B, SWA_HEADS, qc.shape[3], d))
    return out_x, out_c


def mixer_mla(px, pc, row, col, q_norm, kv_norm, w_uq, w_ukv, need_ctx):
    def proj_q(cq):
        qh = to_heads(rms_norm(cq, q_norm) @ w_uq, MLA_HEADS)
        return qh[..., :MLA_NOPE], qh[..., MLA_NOPE:]

    def proj_kv(ckv):
        kvh = to_heads(rms_norm(ckv, kv_norm) @ w_ukv, MLA_HEADS)
        return kvh[..., :MLA_NOPE], kvh[..., MLA_NOPE:]

    scale = (MLA_NOPE + MLA_ROPE) ** -0.5

    def attend(qn, qpe, kn, kpe, v):
        s = (jnp.einsum('bhqd,bhkd->bhqk', qn, kn, preferred_element_type=jnp.float32)
             + jnp.einsum('bhqr,bkr->bhqk', qpe, kpe, preferred_element_type=jnp.float32)) * scale
        p = jax.nn.softmax(s, axis=-1)
        return jnp.einsum('bhqk,bhkd->bhqd', p.astype(v.dtype), v)

    qn, qpe = proj_q(px[0])
    qpe = rope_2d(qpe, row, col)
    kn, v = proj_kv(px[1])
    kpe = rope_2d(px[2], row, col)
    kn_c, v_c = proj_kv(pc[1])
    kpe_c = pc[2]
    kn_all = jnp.concatenate([kn, kn_c], axis=2)
    kpe_all = jnp.concatenate([kpe, kpe_c], axis=1)
    v_all = jnp.concatenate([v, v_c], axis=2)
    B, H, S, _ = qn.shape
    nb = S // BLOCK
    blocks = lambda t: jnp.moveaxis(t.reshape(B, H, nb, BLOCK, t.shape[-1]), 2, 0)
    o = lax.map(lambda qs: attend(qs[0], qs[1], kn_all, kpe_all, v_all), (blocks(qn), blocks(qpe)))
    out_x = from_heads(jnp.moveaxis(o, 0, 2).reshape(B, H, S, MLA_V))
    out_c = None
    if need_ctx:
        qn_c, qpe_c = proj_q(pc[0])
        out_c = from_heads(attend(qn_c, qpe_c, kn_c, kpe_c, v_c))
    return out_x, out_c


def mixer_gqa(px, pc, row, col, q_norm, k_norm, need_ctx):
    G = GQA_HEADS // GQA_KV_HEADS
    q = rope_2d(rms_norm(to_heads(px[0], GQA_HEADS), q_norm), row, col)
    k = rope_2d(rms_norm(to_heads(px[1], GQA_KV_HEADS), k_norm), row, col)
    v = to_heads(px[2], GQA_KV_HEADS)
    kc = rms_norm(to_heads(pc[1], GQA_KV_HEADS), k_norm)
    vc = to_heads(pc[2], GQA_KV_HEADS)
    k_all = jnp.concatenate([k, kc], axis=2)
    v_all = jnp.concatenate([v, vc], axis=2)
    scale = HEAD_DIM ** -0.5
    B, _, S, d = q.shape
    nb = S // BLOCK
    qb = jnp.moveaxis(q.reshape(B, GQA_KV_HEADS, G, nb, BLOCK, d), 3, 0)
    o = lax.map(lambda qi: full_attention(qi, k_all, v_all, scale), qb)
    out_x = from_heads(jnp.moveaxis(o, 0, 3).reshape(B, GQA_HEADS, S, d))
    out_c = None
    if need_ctx:
        qc = rms_norm(to_heads(pc[0], GQA_HEADS), q_norm)
        C = qc.shape[2]
        oc = full_attention(qc.reshape(B, GQA_KV_HEADS, G, C, d), kc, vc, scale)
        out_c = from_heads(oc.reshape(B, GQA_HEADS, C, d))
    return out_x, out_c


def depthwise_conv(h, w, b):
    S = h.shape[1]
    hp = jnp.pad(h, ((0, 0), (CONV_W // 2, CONV_W // 2), (0, 0)))
    return hp[:, 0:S] * w[0] + hp[:, 1:S + 1] * w[1] + hp[:, 2:S + 2] * w[2] + b


def conv_ffn(h, w_gate, w_up, conv_w, conv_b, w_down):
    a = depthwise_conv(h @ w_gate, conv_w, conv_b)
    return (jax.nn.silu(a) * (h @ w_up)) @ w_down


def setup_inputs(seed: int = 0) -> dict:
    key = jax.random.key(seed)
    ks = jax.random.split(key, 25)
    L, D = DEPTH, D_MODEL

    def nrm(k, shape, s):
        return jax.random.normal(k, shape, jnp.float32) * s

    return {
        'x': nrm(ks[0], (BATCH, SEQ, D), 1.0),
        'c': nrm(ks[1], (BATCH, D), 1.0),
        'ctx': nrm(ks[2], (BATCH, CTX_LEN, D), 1.0),
        'c_ctx': nrm(ks[3], (D,), 1.0),
        'w_ada': nrm(ks[4], (L, D, 6 * D), D ** -0.5),
        'b_ada': nrm(ks[5], (L, 6 * D), 0.02),
        'w_in': nrm(ks[6], (L, D, IN_COLS), D ** -0.5),
        'na_rpb': nrm(ks[7], (L, NA_HEADS, 2 * NA_WIN_R - 1, 2 * NA_WIN_C - 1), 0.5),
        'swa_sink': nrm(ks[8], (L, SWA_HEADS), 0.5),
        'mla_q_norm': 1.0 + nrm(ks[9], (L, MLA_Q_LORA), 0.05),
        'mla_kv_norm': 1.0 + nrm(ks[10], (L, MLA_KV_LORA), 0.05),
        'mla_w_uq': nrm(ks[11], (L, MLA_Q_LORA, MLA_HEADS * (MLA_NOPE + MLA_ROPE)), MLA_Q_LORA ** -0.5),
        'mla_w_ukv': nrm(ks[12], (L, MLA_KV_LORA, MLA_HEADS * (MLA_NOPE + MLA_V)), MLA_KV_LORA ** -0.5),
        'gqa_q_norm': 1.0 + nrm(ks[13], (L, HEAD_DIM), 0.05),
        'gqa_k_norm': 1.0 + nrm(ks[14], (L, HEAD_DIM), 0.05),
        'w_out': nrm(ks[15], (L, MIX_WIDTH, D), DEEPNORM_BETA * MIX_WIDTH ** -0.5),
        'ln1_g': 1.0 + nrm(ks[16], (L, D), 0.05),
        'ln1_b': nrm(ks[17], (L, D), 0.02),
        'ffn_w_gate': nrm(ks[18], (L, D, D_FF), D ** -0.5),
        'ffn_w_up': nrm(ks[19], (L, D, D_FF), D ** -0.5),
        'ffn_conv_w': nrm(ks[20], (L, CONV_W, D_FF), CONV_W ** -0.5),
        'ffn_conv_b': nrm(ks[21], (L, D_FF), 0.02),
        'ffn_w_down': nrm(ks[22], (L, D_FF, D), DEEPNORM_BETA * D_FF ** -0.5),
        'ln2_g': 1.0 + nrm(ks[23], (L, D), 0.05),
        'ln2_b': nrm(ks[24], (L, D), 0.02),
    }


def reference(x, c, ctx, c_ctx, w_ada, b_ada, w_in, na_rpb, swa_sink, mla_q_norm, mla_kv_norm,
              mla_w_uq, mla_w_ukv, gqa_q_norm, gqa_k_norm, w_out, ln1_g, ln1_b,
              ffn_w_gate, ffn_w_up, ffn_conv_w, ffn_conv_b, ffn_w_down, ln2_g, ln2_b):
    t = jnp.arange(x.shape[1])
    row, col = t // GRID_W, t % GRID_W
    for i in range(DEPTH):
        need_ctx = i < DEPTH - 1
        mod_x = jnp.split((jax.nn.silu(c) @ w_ada[i] + b_ada[i])[:, None, :], 6, axis=-1)
        mod_c = jnp.split(jax.nn.silu(c_ctx) @ w_ada[i] + b_ada[i], 6, axis=-1)

        px = split_cols(modulate(x, mod_x[0], mod_x[1]) @ w_in[i])
        pc = split_cols(modulate(ctx, mod_c[0], mod_c[1]) @ w_in[i])
        oa_x, oa_c = mixer_neighbourhood(px[0:3], pc[0:3], na_rpb[i], need_ctx)
        ob_x, ob_c = mixer_sliding(px[3:6], pc[3:6], row, col, swa_sink[i], need_ctx)
        oc_x, oc_c = mixer_mla(px[6:9], pc[6:9], row, col, mla_q_norm[i], mla_kv_norm[i],
                               mla_w_uq[i], mla_w_ukv[i], need_ctx)
        od_x, od_c = mixer_gqa(px[9:12], pc[9:12], row, col, gqa_q_norm[i], gqa_k_norm[i], need_ctx)
        mix_x = jnp.concatenate([oa_x, ob_x, oc_x, od_x], axis=-1)
        x = layer_norm(DEEPNORM_ALPHA * x + mod_x[2] * (mix_x @ w_out[i]), ln1_g[i], ln1_b[i])
        if need_ctx:
            mix_c = jnp.concatenate([oa_c, ob_c, oc_c, od_c], axis=-1)
            ctx = layer_norm(DEEPNORM_ALPHA * ctx + mod_c[2] * (mix_c @ w_out[i]), ln1_g[i], ln1_b[i])

        ffn_args = (ffn_w_gate[i], ffn_w_up[i], ffn_conv_w[i], ffn_conv_b[i], ffn_w_down[i])
        x = layer_norm(DEEPNORM_ALPHA * x + mod_x[5] * conv_ffn(modulate(x, mod_x[3], mod_x[4]), *ffn_args),
                       ln2_g[i], ln2_b[i])
        if need_ctx:
            ctx = layer_norm(DEEPNORM_ALPHA * ctx + mod_c[5] * conv_ffn(modulate(ctx, mod_c[3], mod_c[4]), *ffn_args),
                             ln2_g[i], ln2_b[i])
    return x
```

```python
import numpy as np
import ml_dtypes
from contextlib import ExitStack
import concourse.bass as bass
import concourse.mybir as mybir
from concourse.bass_utils import run_bass_kernel_spmd

F32 = mybir.dt.float32
BF16 = mybir.dt.bfloat16
AF = mybir.ActivationFunctionType
ALU = mybir.AluOpType

D = 2048
KC = 16
SEQ = 4096
CTX = 256
TT = SEQ + CTX
NKT = TT // 128
L = 2
DFF = 5632
FC = 44
GRID_W = 64
ALPHA = (2 * L) ** 0.25
EPS = 1e-6
NEG = -1e30
SC128 = 128 ** -0.5
SC192 = 192 ** -0.5
NCORES = 4

QA, KA, QB, KB, QD, KD, QNC, QPEC, KNC, KPEC = 0, 4, 8, 12, 14, 18, 20, 24, 26, 30
NFM = 31
VA, VB, VD, VC = 0, 4, 6, 8
NVT = 12
H2W = SEQ + CTX + 4
H2_CTX0 = SEQ + 2

VEC = {}
_o = 0
for _l in range(L):
    for _n, _w in (("b_ada", 96), ("ln1_g", 16), ("ln1_b", 16), ("ln2_g", 16), ("ln2_b", 16),
                   ("conv_w", FC * 3), ("conv_b", FC), ("gqa_qn", 1), ("gqa_kn", 1),
                   ("mla_qn", 3), ("mla_kvn", 1), ("sink", 4)):
        VEC[(_n, _l)] = (_o, _w)
        _o += _w
NVEC = _o


class Buf:
    __slots__ = ("lw", "rd")

    def __init__(self):
        self.lw = None
        self.rd = {}


class Tile:
    def __init__(self, t, nsub=0):
        self.t = t
        self.b = Buf()
        self.sub = [Tile(t) for _ in range(nsub)]

    def __getitem__(self, idx):
        return self.t[idx]


class Ring:
    def __init__(self, tiles):
        self.tiles = tiles
        self.i = 0

    def next(self):
        t = self.tiles[self.i % len(self.tiles)]
        self.i += 1
        return t


class Prog:
    NDMA = 8
    LIM = 28000

    def __init__(self, nc):
        self.nc = nc
        self.eng = {"pe": nc.tensor, "act": nc.scalar, "dve": nc.vector,
                    "pool": nc.gpsimd, "sp": nc.sync}
        self.sem = {}
        self.cnt = {}
        self.cur = {}
        self.nsem = 0
        for k in ("pe", "act", "dve", "pool"):
            self._fresh(k)
        self.dqn = {"sp": 0, "pool": 0, "act": 0}
        for q in self.dqn:
            for i in range(self.NDMA):
                self._fresh(("dma", q, i))
        self.seen = {k: {} for k in self.eng}
        self.ninst = 0
        self.nwait = 0
        self.cvn = 0

    def _fresh(self, base):
        ep = self.cur.get(base, (None, -1))[1] + 1
        key = (base, ep)
        self.cur[base] = key
        self.sem[key] = self.nc.alloc_semaphore("s%d" % self.nsem)
        self.nsem += 1
        self.cnt[key] = 0
        return key

    def _wait(self, e, ev):
        if ev is None:
            return
        key, c = ev
        if key[0] == e and e == "pe":
            return
        if self.seen[e].get(key, 0) >= c:
            return
        self.seen[e][key] = c
        val = c * 16 if isinstance(key[0], tuple) else c
        self.eng[e].wait_ge(self.sem[key], val)
        self.nwait += 1

    def _deps(self, e, reads, writes):
        for t in reads:
            self._wait(e, t.b.lw)
        for t in writes:
            self._wait(e, t.b.lw)
            for k, c in t.b.rd.items():
                self._wait(e, (k, c))

    def _commit(self, ev, reads, writes):
        k, c = ev
        for t in reads:
            t.b.rd[k] = c
        for t in writes:
            t.b.lw = ev
            t.b.rd = {}

    def op(self, e, fn, reads=(), writes=()):
        key = self.cur[e]
        if self.cnt[key] >= self.LIM:
            key = self._fresh(e)
        self._deps(e, reads, writes)
        ins = fn(self.eng[e])
        self.cnt[key] += 1
        ins.then_inc(self.sem[key], 1)
        self._commit((key, self.cnt[key]), reads, writes)
        self.ninst += 1
        return ins

    def dma(self, q, out, in_, reads=(), writes=(), **kw):
        i = self.dqn[q] % self.NDMA
        self.dqn[q] += 1
        base = ("dma", q, i)
        key = self.cur[base]
        if self.cnt[key] > 0:
            self._wait(q, (key, self.cnt[key]))
        if self.cnt[key] * 16 >= self.LIM:
            key = self._fresh(base)
        self._deps(q, reads, writes)
        ins = self.eng[q].dma_start(out=out, in_=in_, **kw)
        self.cnt[key] += 1
        ins.then_inc(self.sem[key], 16)
        self._commit((key, self.cnt[key]), reads, writes)
        self.ninst += 1
        return ins

    def cvt(self, out, in_, tile):
        key = (("cv", self.cvn), 0)
        self.cvn += 1
        self.sem[key] = self.nc.alloc_semaphore("cv%d" % self.cvn)
        self.nsem += 1
        ins = self.nc.gpsimd.dma_start(out=out, in_=in_)
        ins.then_inc(self.sem[key], 16)
        self.cnt[key] = 1
        tile.b.lw = (key, 1)
        self.ninst += 1

    def barrier(self):
        for e in self.eng:
            for key, c in list(self.cnt.items()):
                if c > 0 and key[0] != e and not (isinstance(key[0], tuple) and key[0][0] == "cv"):
                    self._wait(e, (key, c))


def _rope_tables():
    t = np.arange(SEQ)
    row, col = (t // GRID_W).astype(np.float64), (t % GRID_W).astype(np.float64)

    def tab(dim):
        seg = dim // 2
        half = seg // 2
        C = np.ones((dim, TT))
        S = np.zeros((dim, TT))
        perm = np.zeros(dim, np.int64)
        inv = 10000.0 ** (-np.arange(half, dtype=np.float32) / half)
        for d in range(dim):
            s, e = d // seg, d % seg
            i = e % half
            first = e < half
            pos = row if s == 0 else col
            ang = (pos.astype(np.float32) * np.float32(inv[i])).astype(np.float32)
            C[d, :SEQ] = np.cos(ang)
            S[d, :SEQ] = -np.sin(ang) if first else np.sin(ang)
            perm[d] = d + half if first else d - half
        return C.astype(np.float32), S.astype(np.float32), perm

    C128, S128, p128 = tab(128)
    C64, S64, p64 = tab(64)
    rope128 = np.stack([C128, S128])
    rope64 = np.stack([np.concatenate([C64, C64], 0), np.concatenate([S64, S64], 0)])
    Pm128 = np.zeros((128, 128), np.float32)
    Pm128[p128, np.arange(128)] = 1.0
    Pm64 = np.zeros((128, 128), np.float32)
    for hh in range(2):
        Pm64[hh * 64 + p64, hh * 64 + np.arange(64)] = 1.0
    return rope128, rope64, np.stack([Pm128, Pm64])


def _amask():
    p = np.arange(128)
    a, kcol = p // 64, p % 64
    m = np.arange(14)
    j = 6 - m
    dr = j[None, :] + a[:, None]
    qcol = np.arange(64)
    c0 = np.clip(qcol - 8, 0, 48)
    col_in = (kcol[:, None] >= c0[None, :]) & (kcol[:, None] < c0[None, :] + 16)
    v_edge = np.broadcast_to(col_in[:, None, :], (128, 14, 64))
    v_gen = v_edge & ((dr >= -4) & (dr <= 3))[:, :, None]
    out = []
    for v in (v_gen, v_edge):
        out.append((v / SC128).astype(np.float32).reshape(128, 14 * 64))
        out.append(np.where(v, 0.0, NEG).astype(np.float32).reshape(128, 14 * 64))
    return np.stack(out)


def _gtab(rpb):
    p = np.arange(128)
    a, kcol = p // 64, p % 64
    j = 6 - np.arange(14)
    dr = j[None, :] + a[:, None]
    qcol = np.arange(64)
    dc = np.clip(kcol[:, None] - qcol[None, :] + 15, 0, 30)
    g = rpb[:, :, (dr + 7)[:, :, None], dc[:, None, :]]
    return np.ascontiguousarray(g.reshape(L, 4, 128, 14 * 64))


def _chunk_w(w, kc):
    Lw, K, N = w.shape
    return np.ascontiguousarray(w.reshape(Lw, kc, 128, N // 128, 128).transpose(0, 3, 2, 1, 4).reshape(Lw, N // 128, 128, kc * 128))


_CONST_CACHE = {}


def _consts():
    if not _CONST_CACHE:
        rope128, rope64, perm = _rope_tables()
        k = np.arange(128)
        tri = np.stack([np.where(k[:, None] >= k[None, :], 0.0, NEG), np.where(k[:, None] <= k[None, :], 0.0, NEG)], 1)
        _CONST_CACHE.update(rope128=rope128, rope64=rope64, perm=perm, amask=_amask(),
                            identf=np.eye(128, dtype=np.float32),
                            identb=np.eye(128, dtype=np.float32).astype(ml_dtypes.bfloat16),
                            trimask=np.ascontiguousarray(tri.reshape(128, 256).astype(np.float32)))
    return _CONST_CACHE


def prep_shared(inp):
    sh = {}
    w_in = inp["w_in"]
    sizes = [512, 512, 512, 512, 256, 256, 384, 128, 64, 512, 256, 256]
    offs = np.cumsum([0] + sizes)
    seg = lambda i: w_in[:, :, offs[i]:offs[i + 1]]
    w_in_r = np.concatenate([seg(0), seg(1), seg(2), seg(3), seg(4), seg(5), seg(6), seg(7), seg(9), seg(10), seg(11),
                             seg(8), np.zeros((L, D, 64), np.float32)], axis=2)
    sh["w_in"] = _chunk_w(w_in_r, KC)
    sh["w_out"] = _chunk_w(inp["w_out"], KC)
    sh["w_gate"] = _chunk_w(inp["ffn_w_gate"], KC)
    sh["w_up"] = _chunk_w(inp["ffn_w_up"], KC)
    sh["w_down"] = _chunk_w(inp["ffn_w_down"], FC)
    uq = inp["mla_w_uq"].reshape(L, 384, 4, 192)
    uq = np.concatenate([uq[..., :128].reshape(L, 384, 512), uq[..., 128:].reshape(L, 384, 256)], -1)
    sh["w_uq"] = np.ascontiguousarray(uq.reshape(L, 3, 128, 768).transpose(0, 2, 1, 3).reshape(L, 128, 3 * 768))
    ukv = inp["mla_w_ukv"].reshape(L, 128, 4, 256)
    sh["w_ukv"] = np.ascontiguousarray(np.concatenate([ukv[..., :128].reshape(L, 128, 512), ukv[..., 128:].reshape(L, 128, 512)], -1))
    sh["w_ada"] = np.ascontiguousarray(inp["w_ada"].reshape(L * D, 6 * D))
    vec = np.zeros((128, NVEC), np.float32)

    def put(name, l, arr):
        o, w = VEC[(name, l)]
        vec[:, o:o + w] = arr

    for l in range(L):
        put("b_ada", l, inp["b_ada"][l].reshape(96, 128).T)
        for n in ("ln1_g", "ln1_b", "ln2_g", "ln2_b"):
            put(n, l, inp[n][l].reshape(16, 128).T)
        put("conv_w", l, inp["ffn_conv_w"][l].reshape(3, FC, 128).transpose(2, 1, 0).reshape(128, FC * 3))
        put("conv_b", l, inp["ffn_conv_b"][l].reshape(FC, 128).T)
        put("gqa_qn", l, inp["gqa_q_norm"][l].reshape(128, 1))
        put("gqa_kn", l, inp["gqa_k_norm"][l].reshape(128, 1))
        put("mla_qn", l, inp["mla_q_norm"][l].reshape(3, 128).T)
        put("mla_kvn", l, inp["mla_kv_norm"][l].reshape(128, 1))
        put("sink", l, np.broadcast_to(inp["swa_sink"][l][None, :], (128, 4)))
    sh["vecs"] = vec
    sh["gtab"] = _gtab(inp["na_rpb"])
    sh.update(_consts())
    return sh


def build(dbg=(), stop_after=None):
    nc = bass.Bass("TRN2", target_bir_lowering=False)
    P = Prog(nc)
    es = ExitStack()

    def din(name, shape, dt=F32):
        return nc.dram_tensor(name, list(shape), dt, kind="ExternalInput").ap()

    def dscr(name, shape, dt):
        kind = "ExternalOutput" if name in dbg else "Internal"
        return nc.dram_tensor(name, list(shape), dt, kind=kind).ap()

    xin = din("xin", [TT, D])
    cvec = din("cvec", [128, 32])
    w_ada = din("w_ada", [L * D, 6 * D])
    vecs_d = din("vecs", [128, NVEC])
    w_in_d = din("w_in", [L, 33, 128, 2048])
    w_out_d = din("w_out", [L, 16, 128, 2048])
    w_gate_d = din("w_gate", [L, FC, 128, 2048])
    w_up_d = din("w_up", [L, FC, 128, 2048])
    w_down_d = din("w_down", [L, 16, 128, DFF])
    w_uq_d = din("w_uq", [L, 128, 3 * 768])
    w_ukv_d = din("w_ukv", [L, 128, 1024])
    gtab_d = din("gtab", [L, 4, 128, 14 * 64])
    rope128_d = din("rope128", [2, 128, TT])
    rope64_d = din("rope64", [2, 128, TT])
    perm_d = din("perm", [2, 128, 128])
    amask_d = din("amask", [4, 128, 14 * 64])
    identf_d = din("identf", [128, 128])
    identb_d = din("identb", [128, 128], BF16)
    trimask_d = din("trimask", [128, 256])
    out_d = nc.dram_tensor("out", [SEQ, D], F32, kind="ExternalOutput").ap()

    wb = {}
    wsrc = {"w_in": w_in_d, "w_out": w_out_d, "w_gate": w_gate_d, "w_up": w_up_d, "w_down": w_down_d,
            "w_uq": w_uq_d, "w_ukv": w_ukv_d}
    for n, src in wsrc.items():
        wb[n] = nc.dram_tensor(n + "_bf", list(src.shape), BF16, kind="Internal").ap()
    wbt = {(n, l): Tile(None) for n in wsrc for l in range(L)}

    xT = dscr("xT", [KC, 128, TT], F32)
    hT = dscr("hT", [KC, 128, TT], BF16)
    h2T = dscr("h2T", [KC, 128, H2W], BF16)
    fmT = dscr("fmT", [NFM, 128, TT], BF16)
    vtok = dscr("vtok", [NVT, TT, 128], BF16)
    mixT = dscr("mixT", [KC, 128, TT], BF16)

    uid = [0]

    def sb(name, shape, dt, stack=None, nsub=0):
        uid[0] += 1
        t = (stack or es).enter_context(nc.sbuf_tensor("sb%d_%s" % (uid[0], name), list(shape), dt))
        return Tile(t, nsub)

    def ring(name, n, shape, dt, stack=None, nsub=0):
        return Ring([sb("%s%d" % (name, i), shape, dt, stack, nsub) for i in range(n)])

    PS = [Tile(es.enter_context(nc.psum_tensor("ps%d" % i, [128, 512], F32))) for i in range(8)]

    def flat2k(ap):
        n = 1
        for s in ap.shape:
            n *= s
        names = " ".join("a%d" % i for i in range(len(ap.shape)))
        return ap.rearrange("%s -> (%s)" % (names, names)).rearrange("(r c) -> r c", c=2048)

    for l in range(L):
        for n in ("w_in", "w_uq", "w_ukv", "w_out", "w_gate", "w_up", "w_down"):
            P.cvt(flat2k(wb[n][l]), flat2k(wsrc[n][l]), wbt[(n, l)])

    identf = sb("identf", [128, 128], F32)
    identb = sb("identb", [128, 128], BF16)
    perm = sb("perm", [128, 2, 128], F32)
    trimask = sb("trimask", [128, 256], F32)
    vecs = sb("vecs", [128, NVEC], F32)
    onesb = sb("onesb", [128, 128], BF16)
    ones128 = sb("ones128", [128, 128], BF16)
    ones384 = sb("ones384", [128, 128], BF16)
    modv = sb("modv", [128, L, 96, 2], F32)
    esink = sb("esink", [128, L * 4], F32)
    P.dma("sp", identf[:], identf_d, writes=[identf])
    P.dma("sp", identb[:], identb_d, writes=[identb])
    P.dma("sp", perm[:], perm_d.rearrange("a p c -> p a c"), writes=[perm])
    P.dma("sp", trimask[:], trimask_d, writes=[trimask])
    P.dma("sp", vecs[:], vecs_d, writes=[vecs])
    P.op("dve", lambda e: e.memset(onesb[:], 1.0 / D), writes=[onesb])
    P.op("dve", lambda e: e.memset(ones128[:], 1.0 / 128), writes=[ones128])
    P.op("dve", lambda e: e.memset(ones384[:], 1.0 / 384), writes=[ones384])

    zcol = sb("zcol", [128, KC, 1], BF16)
    P.op("dve", lambda e: e.memset(zcol[:], 0.0), writes=[zcol])
    for c in (0, SEQ + 1, SEQ + 2, H2W - 1):
        P.dma("sp", h2T[:, :, c:c + 1].rearrange("kc p t -> p kc t"), zcol[:], reads=[zcol], allow_slow_non_contiguous=True)

    def vcol(name, l, j=0, w=1):
        o, _ = VEC[(name, l)]
        return vecs[:, o + j:o + j + w]

    def mod(l, s, kc, which):
        return modv[:, l, s * 16 + kc, which:which + 1]

    def ln_stats(zt, N, eps, pools, psM, psV):
        sqp, zbp, stp = pools
        for kc in range(KC):
            sq = sqp.next()
            P.op("act", lambda e: e.activation(out=sq[:, :N], in_=zt[:, kc, :N], func=AF.Square), reads=[zt.sub[kc]], writes=[sq])
            zb = zbp.next()
            P.op("pool", lambda e: e.tensor_copy(out=zb[:, :N], in_=zt[:, kc, :N]), reads=[zt.sub[kc]], writes=[zb])
            P.op("pe", lambda e: e.matmul(psM[:, :N], lhsT=onesb[:], rhs=zb[:, :N], start=(kc == 0), stop=(kc == KC - 1)),
                 reads=[onesb, zb], writes=[psM])
            P.op("pe", lambda e: e.matmul(psV[:, :N], lhsT=onesb[:], rhs=sq[:, :N], start=(kc == 0), stop=(kc == KC - 1)),
                 reads=[onesb, sq], writes=[psV])
        Mt, m2, Rt = stp.next(), stp.next(), stp.next()
        P.op("act", lambda e: e.activation(out=Mt[:, :N], in_=psM[:, :N], func=AF.Copy), reads=[psM], writes=[Mt])
        P.op("act", lambda e: e.activation(out=m2[:, :N], in_=psM[:, :N], func=AF.Square), reads=[psM], writes=[m2])
        P.op("dve", lambda e: e.tensor_tensor(out=m2[:, :N], in0=psV[:, :N], in1=m2[:, :N], op=ALU.subtract), reads=[psV, m2], writes=[m2])
        P.op("dve", lambda e: e.tensor_scalar(out=m2[:, :N], in0=m2[:, :N], scalar1=float(eps), scalar2=None, op0=ALU.add), reads=[m2], writes=[m2])
        P.op("act", lambda e: e.activation(out=m2[:, :N], in_=m2[:, :N], func=AF.Sqrt), reads=[m2], writes=[m2])
        P.op("dve", lambda e: e.reciprocal(out=Rt[:, :N], in_=m2[:, :N]), reads=[m2], writes=[Rt])
        return Mt, Rt

    def ln_apply(zt, kc, N, Mt, Rt, scale_ap, bias_ap, out_ap, out_tile, tp, extra_reads=()):
        t = tp.next()
        P.op("dve", lambda e: e.tensor_tensor(out=t[:, :N], in0=zt[:, kc, :N], in1=Mt[:, :N], op=ALU.subtract), reads=[zt.sub[kc], Mt], writes=[t])
        P.op("pool", lambda e: e.tensor_tensor(out=t[:, :N], in0=t[:, :N], in1=Rt[:, :N], op=ALU.mult), reads=[t, Rt], writes=[t])
        P.op("act", lambda e: e.activation(out=out_ap, in_=t[:, :N], func=AF.Identity, scale=scale_ap, bias=bias_ap),
             reads=[t, vecs, modv] + list(extra_reads), writes=[out_tile])

    groups = [(g * 512, 512, 0) for g in range(8)] + [(SEQ, CTX, 1)]

    def h2col(t0, which):
        return (t0 if which == 0 else H2_CTX0)

    with ExitStack() as st:
        cv = sb("cv", [128, 32], F32, st)
        scv = sb("scv", [128, 32], F32, st)
        wblk = ring("wblk", 2, [128, KC, 512], F32, st)
        P.dma("sp", cv[:], cvec, writes=[cv])
        P.op("act", lambda e: e.activation(out=scv[:], in_=cv[:], func=AF.Silu), reads=[cv], writes=[scv])
        for l in range(L):
            psA = PS[l]
            psv = psA[:, 0:192].rearrange("p (j w) -> p j w", w=2)
            wl = w_ada[l * D:(l + 1) * D, :].rearrange("(kc p) n -> p kc n", p=128)
            for blk in range(24):
                wt = wblk.next()
                P.dma("sp", wt[:], wl[:, :, blk * 512:(blk + 1) * 512], writes=[wt])
                for j in range(4):
                    for kc in range(KC):
                        P.op("pe", lambda e: e.matmul(psv[:, blk * 4 + j, :], lhsT=wt[:, kc, j * 128:(j + 1) * 128],
                                                      rhs=scv[:, 2 * kc:2 * kc + 2], start=(kc == 0), stop=(kc == KC - 1)),
                             reads=[wt, scv], writes=[psA])
            o, w = VEC[("b_ada", l)]
            for which in range(2):
                P.op("dve", lambda e: e.tensor_tensor(out=modv[:, l, :, which], in0=psv[:, :, which], in1=vecs[:, o:o + 96], op=ALU.add),
                     reads=[psA, vecs], writes=[modv])
            for s in (1, 4):
                P.op("dve", lambda e: e.tensor_scalar(out=modv[:, l, s * 16:(s + 1) * 16, :], in0=modv[:, l, s * 16:(s + 1) * 16, :],
                                                      scalar1=1.0, scalar2=None, op0=ALU.add), reads=[modv], writes=[modv])
            for s in (2, 5):
                P.op("dve", lambda e: e.tensor_scalar(out=modv[:, l, s * 16:(s + 1) * 16, :], in0=modv[:, l, s * 16:(s + 1) * 16, :],
                                                      scalar1=1.0 / ALPHA, scalar2=None, op0=ALU.mult), reads=[modv], writes=[modv])
            o, w = VEC[("sink", l)]
            P.op("act", lambda e: e.activation(out=esink[:, l * 4:(l + 1) * 4], in_=vecs[:, o:o + 4], func=AF.Exp), reads=[vecs], writes=[esink])
        P.barrier()
    if "modv" in dbg:
        modv_o = nc.dram_tensor("modv_o", [128, L * 96 * 2], F32, kind="ExternalOutput").ap()
        P.dma("sp", modv_o, modv[:].rearrange("p l j w -> p (l j w)"), reads=[modv])

    def finish():
        P.barrier()
        es.close()
        return nc, P

    if stop_after == "ada":
        return finish()

    with ExitStack() as st:
        xr = ring("xin", 2, [128, 4, D], F32, st)
        ztr = ring("zt0", 2, [128, KC, 512], F32, st, nsub=KC)
        hbr = ring("hb0", 2, [128, KC, 512], BF16, st, nsub=KC)
        pools = (ring("sq0", 3, [128, 512], BF16, st), ring("zb0", 3, [128, 512], BF16, st), ring("st0", 6, [128, 512], F32, st))
        tp = ring("t0", 3, [128, 512], F32, st)
        psr = Ring([PS[0], PS[1], PS[2], PS[3]])
        for (t0, N, which) in groups:
            nt = N // 128
            xt = xr.next()
            P.dma("sp", xt[:, :nt, :], xin[t0:t0 + N, :].rearrange("(a p) d -> p a d", p=128), writes=[xt])
            zt = ztr.next()
            for kc in range(KC):
                ps = psr.next()
                for a in range(nt):
                    P.op("pe", lambda e: e.transpose(out=ps[:, a * 128:(a + 1) * 128], in_=xt[:, a, kc * 128:(kc + 1) * 128], identity=identf[:]),
                         reads=[xt, identf], writes=[ps])
                eng = "act" if kc % 2 == 0 else "dve"
                if eng == "act":
                    P.op("act", lambda e: e.activation(out=zt[:, kc, :N], in_=ps[:, :N], func=AF.Copy), reads=[ps], writes=[zt.sub[kc]])
                else:
                    P.op("dve", lambda e: e.tensor_copy(out=zt[:, kc, :N], in_=ps[:, :N]), reads=[ps], writes=[zt.sub[kc]])
            P.dma("pool", xT[:, :, t0:t0 + N].rearrange("kc p t -> p kc t"), zt[:, :, :N], reads=zt.sub)
            Mt, Rt = ln_stats(zt, N, EPS, pools, PS[4], PS[5])
            hb = hbr.next()
            for kc in range(KC):
                ln_apply(zt, kc, N, Mt, Rt, mod(0, 1, kc, which), mod(0, 0, kc, which), hb[:, kc, :N], hb.sub[kc], tp)
            P.dma("pool", hT[:, :, t0:t0 + N].rearrange("kc p t -> p kc t"), hb[:, :, :N], reads=hb.sub)
        P.barrier()
    if stop_after == "p0":
        return finish()


    def phase1(l):
        with ExitStack() as st:
            hr = ring("h1_", 2, [128, KC, 512], BF16, st)
            wr = ring("w1_", 6, [128, KC, 128], BF16, st)
            wv = sb("wv1", [128, 8, KC, 128], BF16, st)
            wuq = sb("wuq", [128, 3, 768], BF16, st)
            wukv = sb("wukv", [128, 1024], BF16, st)
            rc128 = ring("rc128_", 2, [128, 2, 512], F32, st)
            rc64 = ring("rc64_", 2, [128, 2, 512], F32, st)
            xsr = ring("xs1_", 4, [128, 512], F32, st)
            t1r = ring("t1_", 6, [128, 512], F32, st)
            obr = ring("ob1_", 6, [128, 512], BF16, st)
            sqr = ring("sq1_", 3, [128, 512], BF16, st)
            rsr = ring("rs1_", 3, [128, 512], F32, st)
            vobr = ring("vob1_", 3, [128, 512], BF16, st)
            cqs = sb("cqs", [128, 3, 512], F32, st)
            cqn = sb("cqn", [128, 3, 512], BF16, st)
            ckvn = sb("ckvn", [128, 512], BF16, st)
            mainr = Ring([PS[0], PS[1], PS[2]])
            ppr = Ring([PS[3], PS[4]])
            psR = PS[5]
            vr = Ring([PS[6], PS[7]])
            wt_in = wbt[("w_in", l)]
            wsrc_l = w_in_bf = wb["w_in"]

            def wchunks(c0, n):
                return wb["w_in"][l, c0:c0 + n].rearrange("c p (kc n) -> p c kc n", kc=KC)

            P.dma("sp", wv[:, 0:4], wchunks(8, 4), reads=[wt_in], writes=[wv])
            P.dma("sp", wv[:, 4:6], wchunks(18, 2), reads=[wt_in], writes=[wv])
            P.dma("sp", wv[:, 6:8], wchunks(30, 2), reads=[wt_in], writes=[wv])
            P.dma("sp", wuq[:], wb["w_uq"][l].rearrange("p (kc n) -> p kc n", kc=3), reads=[wbt[("w_uq", l)]], writes=[wuq])
            P.dma("sp", wukv[:], wb["w_ukv"][l], reads=[wbt[("w_ukv", l)]], writes=[wukv])
            cnt = [0]

            for (t0, N, which) in groups:
                skipq = (l == L - 1 and which == 1)
                nt = N // 128
                ht = hr.next()
                P.dma("sp", ht[:, :, :N], hT[:, :, t0:t0 + N].rearrange("kc p t -> p kc t"), writes=[ht])
                r128 = rc128.next()
                P.dma("sp", r128[:, :, :N], rope128_d[:, :, t0:t0 + N].rearrange("a p t -> p a t"), writes=[r128])
                r64 = rc64.next()
                P.dma("sp", r64[:, :, :N], rope64_d[:, :, t0:t0 + N].rearrange("a p t -> p a t"), writes=[r64])

                def proj(cc, M=128):
                    w = wr.next()
                    P.dma("sp", w[:], wb["w_in"][l, cc].rearrange("p (kc n) -> p kc n", kc=KC), reads=[wt_in], writes=[w])
                    ps = mainr.next()
                    for kc in range(KC):
                        P.op("pe", lambda e: e.matmul(ps[0:M, :N], lhsT=w[:, kc, 0:M], rhs=ht[:, kc, :N], start=(kc == 0), stop=(kc == KC - 1)),
                             reads=[w, ht], writes=[ps])
                    return ps

                def copy_out(dst_ap, dst_tile, src_ap, src_tile):
                    cnt[0] += 1
                    if cnt[0] % 2 == 0:
                        P.op("act", lambda e: e.activation(out=dst_ap, in_=src_ap, func=AF.Copy), reads=[src_tile], writes=[dst_tile])
                    else:
                        P.op("dve", lambda e: e.tensor_copy(out=dst_ap, in_=src_ap), reads=[src_tile], writes=[dst_tile])

                def evac_bf(ps, M=128):
                    ob = obr.next()
                    copy_out(ob[:M, :N], ob, ps[:M, :N], ps)
                    return ob

                def store_fm(idx, ob, M=128, prow=0):
                    P.dma("pool", fmT[idx, prow:prow + M, t0:t0 + N], ob[:M, :N], reads=[ob])

                def rope(xs, M, tab, pmi):
                    pp = ppr.next()
                    P.op("pe", lambda e: e.matmul(pp[:M, :N], lhsT=perm[:M, pmi, :M], rhs=xs[:M, :N], start=True, stop=True),
                         reads=[perm, xs], writes=[pp])
                    t1 = t1r.next()
                    P.op("pool", lambda e: e.tensor_tensor(out=t1[:M, :N], in0=xs[:M, :N], in1=tab[:M, 0, :N], op=ALU.mult), reads=[xs, tab], writes=[t1])
                    t2 = t1r.next()
                    P.op("dve", lambda e: e.tensor_tensor(out=t2[:M, :N], in0=pp[:M, :N], in1=tab[:M, 1, :N], op=ALU.mult), reads=[pp, tab], writes=[t2])
                    ob = obr.next()
                    P.op("dve", lambda e: e.tensor_tensor(out=ob[:M, :N], in0=t1[:M, :N], in1=t2[:M, :N], op=ALU.add), reads=[t1, t2], writes=[ob])
                    return ob

                def rstd_from(psr_tile):
                    m = rsr.next()
                    P.op("dve", lambda e: e.tensor_scalar(out=m[:, :N], in0=psr_tile[:, :N], scalar1=EPS, scalar2=None, op0=ALU.add), reads=[psr_tile], writes=[m])
                    P.op("act", lambda e: e.activation(out=m[:, :N], in_=m[:, :N], func=AF.Sqrt), reads=[m], writes=[m])
                    P.op("dve", lambda e: e.reciprocal(out=m[:, :N], in_=m[:, :N]), reads=[m], writes=[m])
                    return m

                for cc in range(0, 8):
                    if skipq and cc < 4:
                        continue
                    ps = proj(cc)
                    store_fm(QA + cc, evac_bf(ps))
                for i, cc in enumerate(range(12, 18)):
                    if skipq and i < 4:
                        continue
                    ps = proj(cc)
                    xs = xsr.next()
                    copy_out(xs[:, :N], xs, ps[:, :N], ps)
                    store_fm(QB + i, rope(xs, 128, r128, 0))
                for i, cc in enumerate(range(24, 30)):
                    if skipq and i < 4:
                        continue
                    ps = proj(cc)
                    sq = sqr.next()
                    P.op("act", lambda e: e.activation(out=sq[:, :N], in_=ps[:, :N], func=AF.Square), reads=[ps], writes=[sq])
                    P.op("pe", lambda e: e.matmul(psR[:, :N], lhsT=ones128[:], rhs=sq[:, :N], start=True, stop=True), reads=[ones128, sq], writes=[psR])
                    m = rstd_from(psR)
                    xs = xsr.next()
                    g = vcol("gqa_qn" if i < 4 else "gqa_kn", l)
                    P.op("dve", lambda e: e.scalar_tensor_tensor(out=xs[:, :N], in0=ps[:, :N], scalar=g, in1=m[:, :N], op0=ALU.mult, op1=ALU.mult),
                         reads=[ps, m, vecs], writes=[xs])
                    store_fm(QD + i, rope(xs, 128, r128, 0))
                if not skipq:
                    for i in range(3):
                        ps = proj(20 + i)
                        P.op("act", lambda e: e.activation(out=cqs[:, i, :N], in_=ps[:, :N], func=AF.Copy), reads=[ps], writes=[cqs])
                        sq = sqr.next()
                        P.op("act", lambda e: e.activation(out=sq[:, :N], in_=ps[:, :N], func=AF.Square), reads=[ps], writes=[sq])
                        P.op("pe", lambda e: e.matmul(psR[:, :N], lhsT=ones384[:], rhs=sq[:, :N], start=(i == 0), stop=(i == 2)), reads=[ones384, sq], writes=[psR])
                    m = rstd_from(psR)
                    for i in range(3):
                        P.op("dve", lambda e: e.scalar_tensor_tensor(out=cqn[:, i, :N], in0=cqs[:, i, :N], scalar=vcol("mla_qn", l, i), in1=m[:, :N],
                                                                    op0=ALU.mult, op1=ALU.mult), reads=[cqs, m, vecs], writes=[cqn])
                    for h in range(4):
                        ps = mainr.next()
                        for i in range(3):
                            P.op("pe", lambda e: e.matmul(ps[:, :N], lhsT=wuq[:, i, h * 128:(h + 1) * 128], rhs=cqn[:, i, :N], start=(i == 0), stop=(i == 2)),
                                 reads=[wuq, cqn], writes=[ps])
                        store_fm(QNC + h, evac_bf(ps))
                    for j in range(2):
                        ps = mainr.next()
                        for i in range(3):
                            P.op("pe", lambda e: e.matmul(ps[:, :N], lhsT=wuq[:, i, 512 + j * 128:512 + (j + 1) * 128], rhs=cqn[:, i, :N], start=(i == 0), stop=(i == 2)),
                                 reads=[wuq, cqn], writes=[ps])
                        xs = xsr.next()
                        copy_out(xs[:, :N], xs, ps[:, :N], ps)
                        store_fm(QPEC + j, rope(xs, 128, r64, 1))
                ps = proj(23)
                xs = xsr.next()
                P.op("act", lambda e: e.activation(out=xs[:, :N], in_=ps[:, :N], func=AF.Copy), reads=[ps], writes=[xs])
                sq = sqr.next()
                P.op("act", lambda e: e.activation(out=sq[:, :N], in_=ps[:, :N], func=AF.Square), reads=[ps], writes=[sq])
                P.op("pe", lambda e: e.matmul(psR[:, :N], lhsT=ones128[:], rhs=sq[:, :N], start=True, stop=True), reads=[ones128, sq], writes=[psR])
                m = rstd_from(psR)
                P.op("dve", lambda e: e.scalar_tensor_tensor(out=ckvn[:, :N], in0=xs[:, :N], scalar=vcol("mla_kvn", l), in1=m[:, :N], op0=ALU.mult, op1=ALU.mult),
                     reads=[xs, m, vecs], writes=[ckvn])
                for h in range(4):
                    ps = mainr.next()
                    P.op("pe", lambda e: e.matmul(ps[:, :N], lhsT=wukv[:, h * 128:(h + 1) * 128], rhs=ckvn[:, :N], start=True, stop=True), reads=[wukv, ckvn], writes=[ps])
                    store_fm(KNC + h, evac_bf(ps))
                for a in range(nt):
                    psv = vr.next()
                    P.op("pe", lambda e: e.matmul(psv[:, :], lhsT=ckvn[:, a * 128:(a + 1) * 128], rhs=wukv[:, 512:1024], start=True, stop=True), reads=[wukv, ckvn], writes=[psv])
                    vob = vobr.next()
                    copy_out(vob[:, :], vob, psv[:, :], psv)
                    P.dma("pool", vtok[VC:VC + 4, t0 + a * 128:t0 + (a + 1) * 128, :].rearrange("h t d -> t h d"),
                          vob[:, :].rearrange("t (h d) -> t h d", h=4), reads=[vob])
                ps = proj(32, M=64)
                xs = xsr.next()
                copy_out(xs[:64, :N], xs, ps[:64, :N], ps)
                ob = rope(xs, 64, r64, 1)
                store_fm(KPEC, ob, M=64, prow=0)
                store_fm(KPEC, ob, M=64, prow=64)
                for a in range(nt):
                    for blk in range(2):
                        psv = vr.next()
                        for kc in range(KC):
                            P.op("pe", lambda e: e.matmul(psv[:, :].rearrange("t (c n) -> t c n", c=4), lhsT=ht[:, kc, a * 128:(a + 1) * 128],
                                                          rhs=wv[:, blk * 4:(blk + 1) * 4, kc, :], start=(kc == 0), stop=(kc == KC - 1)),
                                 reads=[ht, wv], writes=[psv])
                        vob = vobr.next()
                        copy_out(vob[:, :], vob, psv[:, :], psv)
                        tsl = slice(t0 + a * 128, t0 + (a + 1) * 128)
                        if blk == 0:
                            P.dma("pool", vtok[VA:VA + 4, tsl, :].rearrange("h t d -> t h d"), vob[:, :].rearrange("t (h d) -> t h d", h=4), reads=[vob])
                        else:
                            P.dma("pool", vtok[VB:VB + 2, tsl, :].rearrange("h t d -> t h d"), vob[:, 0:256].rearrange("t (h d) -> t h d", h=2), reads=[vob])
                            P.dma("pool", vtok[VD:VD + 2, tsl, :].rearrange("h t d -> t h d"), vob[:, 256:512].rearrange("t (h d) -> t h d", h=2), reads=[vob])
            P.barrier()


    def phase2(l):
        need_ctx = l < L - 1
        qgroups = groups if need_ctx else groups[:8]
        with ExitStack() as st:
            ktr = ring("kt2_", 2, [128, TT], BF16, st)
            vtr = ring("vt2_", 2, [128, NKT, 130], BF16, st)
            kpe = sb("kpe2", [128, TT], BF16, st)
            qr = ring("q2_", 3, [128, 512], BF16, st)
            qpr = ring("qp2_", 3, [128, 512], BF16, st)
            ptr = ring("pt2_", 4, [128, 512], BF16, st)
            onr = ring("on2_", 4, [128, 128], BF16, st)
            mor = ring("mo2_", 3, [128, 512], BF16, st)
            rir = ring("ri2_", 8, [128, 1], F32, st)
            amask = sb("amask", [128, 4, 896], F32, st)
            gtr = ring("gt2_", 2, [128, 896], F32, st)
            ggr = ring("gg2_", 2, [128, 896], F32, st)
            ger = ring("ge2_", 2, [128, 896], F32, st)
            psT = PS[6]
            psT_bf = psT[:, :].bitcast(BF16)
            for vt in vtr.tiles:
                P.op("dve", lambda e: e.memset(vt[:, :, 128:129], 1.0), writes=[vt])
                P.op("dve", lambda e: e.memset(vt[:, :, 129:130], 0.0), writes=[vt])
            P.dma("sp", amask[:], amask_d.rearrange("a p c -> p a c"), writes=[amask])
            P.dma("sp", kpe[:], fmT[KPEC], writes=[kpe])

            def load_kv(kidx, vidx):
                kt = ktr.next()
                P.dma("sp", kt[:], fmT[kidx], writes=[kt])
                vt = vtr.next()
                P.dma("sp", vt[:, :, 0:128], vtok[vidx].rearrange("(kt p) d -> p kt d", p=128), writes=[vt])
                return kt, vt

            def load_q(idx, t0, N, r=None):
                q = (r or qr).next()
                P.dma("sp", q[:, :N], fmT[idx, :, t0:t0 + N], writes=[q])
                return q

            def fin_tile(O, col, sink_ap=None):
                ri = rir.next()
                if sink_ap is not None:
                    P.op("dve", lambda e: e.tensor_scalar(out=ri[:], in0=O[:, 128:129], scalar1=sink_ap, scalar2=None, op0=ALU.add), reads=[O, esink], writes=[ri])
                    P.op("dve", lambda e: e.reciprocal(out=ri[:], in_=ri[:]), reads=[ri], writes=[ri])
                else:
                    P.op("dve", lambda e: e.reciprocal(out=ri[:], in_=O[:, 128:129]), reads=[O], writes=[ri])
                on = onr.next()
                P.op("act", lambda e: e.activation(out=on[:], in_=O[:, 0:128], func=AF.Copy, scale=ri[:]), reads=[O, ri], writes=[on])
                P.op("pe", lambda e: e.transpose(out=psT_bf[:, col:col + 128], in_=on[:], identity=identb[:]), reads=[on, identb], writes=[psT])

            def fin_group(head, t0, N):
                mo = mor.next()
                P.op("dve", lambda e: e.tensor_copy(out=mo[:, :N], in_=psT_bf[:, :N]), reads=[psT], writes=[mo])
                P.dma("pool", mixT[head, :, t0:t0 + N], mo[:, :N], reads=[mo])

            sring = Ring([PS[0], PS[1], PS[7]])
            Od = [PS[2], PS[3], PS[4], PS[5]]

            def dense(ktiles, N, smm, scale, vt):
                nq = N // 128
                last = len(ktiles) - 1
                for i, ktile in enumerate(ktiles):
                    s_ = sring.next()
                    smm(ktile, s_)
                    pt = ptr.next()
                    P.op("act", lambda e: e.activation(out=pt[:, :N], in_=s_[:, :N], func=AF.Exp, scale=scale), reads=[s_], writes=[pt])
                    for a in range(nq):
                        P.op("pe", lambda e: e.matmul(Od[a][:, 0:130], lhsT=pt[:, a * 128:(a + 1) * 128], rhs=vt[:, ktile, :], start=(i == 0), stop=(i == last)),
                             reads=[pt, vt], writes=[Od[a]])

            def run_dense(kind):
                heads = range(4)
                kt = vt = None
                for h in heads:
                    if kind == "D":
                        if h % 2 == 0:
                            kt, vt = load_kv(KD + h // 2, VD + h // 2)
                        head = 12 + h
                    else:
                        kt, vt = load_kv(KNC + h, VC + h)
                        head = 8 + h
                    for (t0, N, which) in qgroups:
                        ktiles = list(range(NKT)) if which == 0 else [32, 33]
                        if kind == "D":
                            q = load_q(QD + h, t0, N)

                            def smm(ktile, s_):
                                P.op("pe", lambda e: e.matmul(s_[:, :N], lhsT=kt[:, ktile * 128:(ktile + 1) * 128], rhs=q[:, :N], start=True, stop=True),
                                     reads=[kt, q], writes=[s_])
                            dense(ktiles, N, smm, SC128, vt)
                        else:
                            q = load_q(QNC + h, t0, N)
                            qp = load_q(QPEC + h // 2, t0, N, qpr)
                            lo = (h % 2) * 64

                            def smm(ktile, s_):
                                P.op("pe", lambda e: e.matmul(s_[:, :N], lhsT=kt[:, ktile * 128:(ktile + 1) * 128], rhs=q[:, :N], start=True, stop=False),
                                     reads=[kt, q], writes=[s_])
                                P.op("pe", lambda e: e.matmul(s_[:, :N], lhsT=kpe[lo:lo + 64, ktile * 128:(ktile + 1) * 128], rhs=qp[lo:lo + 64, :N], start=False, stop=True),
                                     reads=[kpe, qp], writes=[s_])
                            dense(ktiles, N, smm, SC192, vt)
                        for a in range(N // 128):
                            fin_tile(Od[a], a * 128)
                        fin_group(head, t0, N)

            ssets = Ring([(PS[0], PS[1]), (PS[2], PS[3])])
            Ow = Ring([PS[4], PS[5], PS[7]])

            def window_tile(kt, vt, q, a, blocks, bias_fn, sink_ap, col):
                bA, bB = ssets.next()
                nb = len(blocks)
                nA = min(nb, 4)
                nB = nb - nA
                for i, ktile in enumerate(blocks):
                    bank, c = (bA, i * 128) if i < 4 else (bB, (i - 4) * 128)
                    P.op("pe", lambda e: e.matmul(bank[:, c:c + 128], lhsT=kt[:, ktile * 128:(ktile + 1) * 128], rhs=q[:, a * 128:(a + 1) * 128], start=True, stop=True),
                         reads=[kt, q], writes=[bank])
                bias_fn(bA, bB)
                ptA = ptr.next()
                P.op("act", lambda e: e.activation(out=ptA[:, :nA * 128], in_=bA[:, :nA * 128], func=AF.Exp, scale=SC128), reads=[bA], writes=[ptA])
                if nB:
                    ptB = ptr.next()
                    P.op("act", lambda e: e.activation(out=ptB[:, :nB * 128], in_=bB[:, :nB * 128], func=AF.Exp, scale=SC128), reads=[bB], writes=[ptB])
                O = Ow.next()
                for i, ktile in enumerate(blocks):
                    pt, c = (ptA, i * 128) if i < 4 else (ptB, (i - 4) * 128)
                    P.op("pe", lambda e: e.matmul(O[:, 0:130], lhsT=pt[:, c:c + 128], rhs=vt[:, ktile, :], start=(i == 0), stop=(i == nb - 1)),
                         reads=[pt, vt], writes=[O])
                fin_tile(O, col, sink_ap)

            def addbias(bank, c0, c1, tab, tab_ap):
                P.op("dve", lambda e: e.tensor_tensor(out=bank[:, c0:c1], in0=bank[:, c0:c1], in1=tab_ap, op=ALU.add), reads=[bank, tab], writes=[bank])

            def run_A():
                for h in range(4):
                    kt, vt = load_kv(KA + h, VA + h)
                    gt = gtr.next()
                    P.dma("sp", gt[:], gtab_d[l, h], writes=[gt])
                    gg, ge = ggr.next(), ger.next()
                    for (g_, mi) in ((gg, 0), (ge, 2)):
                        P.op("dve", lambda e: e.tensor_tensor(out=g_[:], in0=gt[:], in1=amask[:, mi, :], op=ALU.mult), reads=[gt, amask], writes=[g_])
                        P.op("dve", lambda e: e.tensor_tensor(out=g_[:], in0=g_[:], in1=amask[:, mi + 1, :], op=ALU.add), reads=[g_, amask], writes=[g_])
                    for (t0, N, which) in qgroups:
                        q = load_q(QA + h, t0, N)
                        for a in range(N // 128):
                            qt = t0 // 128 + a
                            if which == 1:
                                window_tile(kt, vt, q, a, [32, 33], lambda bA, bB: None, None, a * 128)
                                continue
                            if 2 <= qt <= 29:
                                d0, nw, tab = 2, 5, gg
                            elif qt == 0:
                                d0, nw, tab = 3, 4, ge
                            elif qt == 1:
                                d0, nw, tab = 2, 4, ge
                            elif qt == 30:
                                d0, nw, tab = 1, 4, ge
                            else:
                                d0, nw, tab = 0, 4, ge
                            m0 = 6 - 2 * d0
                            blocks = [qt + d0 - i for i in range(nw)] + [32, 33]

                            def bias_fn(bA, bB, m0=m0, nw=nw, tab=tab):
                                addbias(bA, 0, 512, tab, tab[:, m0 * 64:m0 * 64 + 512])
                                if nw == 5:
                                    addbias(bB, 0, 128, tab, tab[:, (m0 + 8) * 64:(m0 + 10) * 64])
                            window_tile(kt, vt, q, a, blocks, bias_fn, None, a * 128)
                        fin_group(h, t0, N)

            def run_B():
                kt = vt = None
                for h in range(4):
                    if h % 2 == 0:
                        kt, vt = load_kv(KB + h // 2, VB + h // 2)
                    sink_ap = esink[:, l * 4 + h:l * 4 + h + 1]
                    for (t0, N, which) in qgroups:
                        q = load_q(QB + h, t0, N)
                        for a in range(N // 128):
                            qt = t0 // 128 + a
                            if which == 1:
                                window_tile(kt, vt, q, a, [32, 33], lambda bA, bB: None, sink_ap, a * 128)
                                continue
                            masked = ([qt - 1] if qt > 0 else []) + ([qt + 1] if qt < 31 else [])
                            blocks = masked + [qt, 32, 33]

                            def bias_fn(bA, bB, qt=qt):
                                if 0 < qt < 31:
                                    addbias(bA, 0, 256, trimask, trimask[:, 0:256])
                                elif qt == 0:
                                    addbias(bA, 0, 128, trimask, trimask[:, 128:256])
                                else:
                                    addbias(bA, 0, 128, trimask, trimask[:, 0:128])
                            window_tile(kt, vt, q, a, blocks, bias_fn, sink_ap, a * 128)
                        fin_group(4 + h, t0, N)

            run_A()
            run_B()
            run_dense("C")
            run_dense("D")
            P.barrier()


    def phase3(l):
        grp = groups if l < L - 1 else groups[:8]
        with ExitStack() as st:
            mr = ring("m3_", 2, [128, KC, 512], BF16, st)
            ztr = ring("z3_", 2, [128, KC, 512], F32, st, nsub=KC)
            wr = ring("w3_", 4, [128, KC, 128], BF16, st)
            hbr = ring("hb3_", 2, [128, KC, 512], BF16, st, nsub=KC)
            pools = (ring("sq3_", 3, [128, 512], BF16, st), ring("zb3_", 3, [128, 512], BF16, st), ring("st3_", 6, [128, 512], F32, st))
            tp = ring("t3_", 3, [128, 512], F32, st)
            mainr = Ring([PS[0], PS[1], PS[2], PS[3]])
            wt_ = wbt[("w_out", l)]
            for (t0, N, which) in grp:
                mt = mr.next()
                P.dma("sp", mt[:, :, :N], mixT[:, :, t0:t0 + N].rearrange("h p t -> p h t"), writes=[mt])
                zt = ztr.next()
                P.dma("sp", zt[:, :, :N], xT[:, :, t0:t0 + N].rearrange("kc p t -> p kc t"), writes=zt.sub)
                for cc in range(KC):
                    w = wr.next()
                    P.dma("sp", w[:], wb["w_out"][l, cc].rearrange("p (kc n) -> p kc n", kc=KC), reads=[wt_], writes=[w])
                    ps = mainr.next()
                    for kc in range(KC):
                        P.op("pe", lambda e: e.matmul(ps[:, :N], lhsT=w[:, kc, :], rhs=mt[:, kc, :N], start=(kc == 0), stop=(kc == KC - 1)),
                             reads=[w, mt], writes=[ps])
                    P.op("dve", lambda e: e.scalar_tensor_tensor(out=zt[:, cc, :N], in0=ps[:, :N], scalar=mod(l, 2, cc, which), in1=zt[:, cc, :N],
                                                                op0=ALU.mult, op1=ALU.add), reads=[ps, zt.sub[cc], modv], writes=[zt.sub[cc]])
                Mt, Rt = ln_stats(zt, N, EPS / ALPHA ** 2, pools, PS[4], PS[5])
                for kc in range(KC):
                    ln_apply(zt, kc, N, Mt, Rt, vcol("ln1_g", l, kc), vcol("ln1_b", l, kc), zt[:, kc, :N], zt.sub[kc], tp)
                P.dma("pool", xT[:, :, t0:t0 + N].rearrange("kc p t -> p kc t"), zt[:, :, :N], reads=zt.sub)
                Mt, Rt = ln_stats(zt, N, EPS, pools, PS[4], PS[5])
                hb = hbr.next()
                for kc in range(KC):
                    ln_apply(zt, kc, N, Mt, Rt, mod(l, 4, kc, which), mod(l, 3, kc, which), hb[:, kc, :N], hb.sub[kc], tp)
                c0 = h2col(t0, which) + 1
                P.dma("pool", h2T[:, :, c0:c0 + N].rearrange("kc p t -> p kc t"), hb[:, :, :N], reads=hb.sub)
            P.barrier()

    def phase4(l):
        last = (l == L - 1)
        grp = groups if not last else groups[:8]
        with ExitStack() as st:
            h2 = sb("h4", [128, KC, 514], BF16, st)
            zt = sb("z4", [128, KC, 512], F32, st, nsub=KC)
            act = sb("act4", [128, FC, 512], BF16, st)
            hb = Tile(act.t, nsub=KC)
            for s_ in hb.sub:
                s_.b = act.b
            wgr = ring("wg4_", 3, [128, KC, 128], BF16, st)
            wur = ring("wu4_", 3, [128, KC, 128], BF16, st)
            wdr = ring("wd4_", 2, [128, FC, 128], BF16, st)
            gbr = ring("gb4_", 3, [128, 514], F32, st)
            a1r = ring("a14_", 3, [128, 512], F32, st)
            sr = ring("s4_", 2, [128, 512], F32, st)
            pools = (ring("sq4_", 2, [128, 512], BF16, st), ring("zb4_", 2, [128, 512], BF16, st), ring("st4_", 6, [128, 512], F32, st))
            tp = ring("t4_", 2, [128, 512], F32, st)
            ost = sb("ost4", [128, D], F32, st) if last else None
            gr = Ring([PS[0], PS[1]])
            ur = Ring([PS[2], PS[3]])
            hr_ = Ring([PS[4], PS[7]])
            dr = Ring([PS[0], PS[1], PS[2], PS[3]])
            o_w, _ = VEC[("conv_w", l)]
            o_b, _ = VEC[("conv_b", l)]
            for (t0, N, which) in grp:
                c0 = h2col(t0, which)
                P.dma("sp", h2[:, :, :N + 2], h2T[:, :, c0:c0 + N + 2].rearrange("kc p t -> p kc t"), writes=[h2])
                P.dma("sp", zt[:, :, :N], xT[:, :, t0:t0 + N].rearrange("kc p t -> p kc t"), writes=zt.sub)
                for j in range(FC):
                    wg = wgr.next()
                    P.dma("sp", wg[:], wb["w_gate"][l, j].rearrange("p (kc n) -> p kc n", kc=KC), reads=[wbt[("w_gate", l)]], writes=[wg])
                    wu = wur.next()
                    P.dma("sp", wu[:], wb["w_up"][l, j].rearrange("p (kc n) -> p kc n", kc=KC), reads=[wbt[("w_up", l)]], writes=[wu])
                    psG, psU, psH = gr.next(), ur.next(), hr_.next()
                    for kc in range(KC):
                        P.op("pe", lambda e: e.matmul(psG[:, :N], lhsT=wg[:, kc, :], rhs=h2[:, kc, 1:N + 1], start=(kc == 0), stop=(kc == KC - 1)),
                             reads=[wg, h2], writes=[psG])
                    for kc in range(KC):
                        P.op("pe", lambda e: e.matmul(psH[:, 0:2], lhsT=wg[:, kc, :], rhs=h2[:, kc, 0:N + 2:N + 1], start=(kc == 0), stop=(kc == KC - 1)),
                             reads=[wg, h2], writes=[psH])
                    for kc in range(KC):
                        P.op("pe", lambda e: e.matmul(psU[:, :N], lhsT=wu[:, kc, :], rhs=h2[:, kc, 1:N + 1], start=(kc == 0), stop=(kc == KC - 1)),
                             reads=[wu, h2], writes=[psU])
                    gb = gbr.next()
                    P.op("act", lambda e: e.activation(out=gb[:, 1:N + 1], in_=psG[:, :N], func=AF.Copy), reads=[psG], writes=[gb])
                    P.op("act", lambda e: e.activation(out=gb[:, 0:N + 2:N + 1], in_=psH[:, 0:2], func=AF.Copy), reads=[psH], writes=[gb])
                    a1 = a1r.next()
                    w0 = vecs[:, o_w + j * 3 + 0:o_w + j * 3 + 1]
                    w1 = vecs[:, o_w + j * 3 + 1:o_w + j * 3 + 2]
                    w2 = vecs[:, o_w + j * 3 + 2:o_w + j * 3 + 3]
                    cb = vecs[:, o_b + j:o_b + j + 1]
                    P.op("act", lambda e: e.activation(out=a1[:, :N], in_=gb[:, 1:N + 1], func=AF.Identity, scale=w1, bias=cb), reads=[gb, vecs], writes=[a1])
                    P.op("dve", lambda e: e.scalar_tensor_tensor(out=a1[:, :N], in0=gb[:, 0:N], scalar=w0, in1=a1[:, :N], op0=ALU.mult, op1=ALU.add),
                         reads=[gb, a1, vecs], writes=[a1])
                    P.op("dve", lambda e: e.scalar_tensor_tensor(out=a1[:, :N], in0=gb[:, 2:N + 2], scalar=w2, in1=a1[:, :N], op0=ALU.mult, op1=ALU.add),
                         reads=[gb, a1, vecs], writes=[a1])
                    sg = sr.next()
                    P.op("act", lambda e: e.activation(out=sg[:, :N], in_=a1[:, :N], func=AF.Silu), reads=[a1], writes=[sg])
                    P.op("dve", lambda e: e.tensor_tensor(out=act[:, j, :N], in0=sg[:, :N], in1=psU[:, :N], op=ALU.mult), reads=[sg, psU], writes=[act])
                for cc in range(KC):
                    wd = wdr.next()
                    P.dma("sp", wd[:], wb["w_down"][l, cc].rearrange("p (j n) -> p j n", j=FC), reads=[wbt[("w_down", l)]], writes=[wd])
                    ps = dr.next()
                    for j in range(FC):
                        P.op("pe", lambda e: e.matmul(ps[:, :N], lhsT=wd[:, j, :], rhs=act[:, j, :N], start=(j == 0), stop=(j == FC - 1)),
                             reads=[wd, act], writes=[ps])
                    P.op("dve", lambda e: e.scalar_tensor_tensor(out=zt[:, cc, :N], in0=ps[:, :N], scalar=mod(l, 5, cc, which), in1=zt[:, cc, :N],
                                                                op0=ALU.mult, op1=ALU.add), reads=[ps, zt.sub[cc], modv], writes=[zt.sub[cc]])
                Mt, Rt = ln_stats(zt, N, EPS / ALPHA ** 2, pools, PS[5], PS[6])
                for kc in range(KC):
                    ln_apply(zt, kc, N, Mt, Rt, vcol("ln2_g", l, kc), vcol("ln2_b", l, kc), zt[:, kc, :N], zt.sub[kc], tp)
                if not last:
                    P.dma("pool", xT[:, :, t0:t0 + N].rearrange("kc p t -> p kc t"), zt[:, :, :N], reads=zt.sub)
                    Mt, Rt = ln_stats(zt, N, EPS, pools, PS[5], PS[6])
                    for kc in range(KC):
                        ln_apply(zt, kc, N, Mt, Rt, mod(l + 1, 1, kc, which), mod(l + 1, 0, kc, which), hb[:, kc, :N], hb.sub[kc], tp)
                    P.dma("pool", hT[:, :, t0:t0 + N].rearrange("kc p t -> p kc t"), hb[:, 0:KC, :N], reads=[act])
                else:
                    for a in range(N // 128):
                        for k4 in range(4):
                            ps = dr.next()
                            for i in range(4):
                                kc = k4 * 4 + i
                                P.op("pe", lambda e: e.transpose(out=ps[:, i * 128:(i + 1) * 128], in_=zt[:, kc, a * 128:(a + 1) * 128], identity=identf[:]),
                                     reads=[zt.sub[kc], identf], writes=[ps])
                            if k4 % 2 == 0:
                                P.op("act", lambda e: e.activation(out=ost[:, k4 * 512:(k4 + 1) * 512], in_=ps[:, :], func=AF.Copy), reads=[ps], writes=[ost])
                            else:
                                P.op("dve", lambda e: e.tensor_copy(out=ost[:, k4 * 512:(k4 + 1) * 512], in_=ps[:, :]), reads=[ps], writes=[ost])
                        P.dma("pool", out_d[t0 + a * 128:t0 + (a + 1) * 128, :], ost[:, :], reads=[ost])
            P.barrier()

    for l in range(L):
        for nm, ph in (("p1", phase1), ("p2", phase2), ("p3", phase3), ("p4", phase4)):
            ph(l)
            if stop_after == "%s_%d" % (nm, l):
                return finish()
    return finish()


_NC_CACHE = {}


def kernel(**inp):
    inp = {k: np.asarray(v) for k, v in inp.items()}
    sh = prep_shared(inp)
    if "nc" not in _NC_CACHE:
        _NC_CACHE["nc"] = build()[0]
    nc = _NC_CACHE["nc"]
    B = inp["x"].shape[0]
    in_maps = []
    for b in range(B):
        m = dict(sh)
        m["xin"] = np.ascontiguousarray(np.concatenate([inp["x"][b], inp["ctx"][b]], 0))
        cv = np.stack([inp["c"][b].reshape(16, 128).T, inp["c_ctx"].reshape(16, 128).T], -1).reshape(128, 32)
        m["cvec"] = np.ascontiguousarray(cv)
        in_maps.append(m)
    res = run_bass_kernel_spmd(nc, in_maps, core_ids=list(range(B)))
    return np.stack([np.asarray(r["out"]) for r in res.results], 0).astype(np.float32)
```

```python
import numpy as np
import ml_dtypes
from contextlib import ExitStack
import concourse.bass as bass
import concourse.mybir as mybir
from concourse.bass_utils import run_bass_kernel_spmd

F32 = mybir.dt.float32
BF16 = mybir.dt.bfloat16
AF = mybir.ActivationFunctionType
ALU = mybir.AluOpType

D = 2048
KC = 16
SEQ = 4096
CTX = 256
TT = SEQ + CTX
NKT = TT // 128
L = 2
DFF = 5632
FC = 44
GRID_W = 64
ALPHA = (2 * L) ** 0.25
EPS = 1e-6
NEG = -1e30
SC128 = 128 ** -0.5
SC192 = 192 ** -0.5
NCORES = 4

QA, KA, QB, KB, QD, KD, QNC, QPEC, KNC, KPEC = 0, 4, 8, 12, 14, 18, 20, 24, 26, 30
NFM = 31
VA, VB, VD, VC = 0, 4, 6, 8
NVT = 12
H2W = SEQ + CTX + 4
H2_CTX0 = SEQ + 2

VEC = {}
_o = 0
for _l in range(L):
    for _n, _w in (("b_ada", 96), ("ln1_g", 16), ("ln1_b", 16), ("ln2_g", 16), ("ln2_b", 16),
                   ("conv_w", FC * 3), ("conv_b", FC), ("gqa_qn", 1), ("gqa_kn", 1),
                   ("mla_qn", 3), ("mla_kvn", 1), ("sink", 4)):
        VEC[(_n, _l)] = (_o, _w)
        _o += _w
NVEC = _o


class Buf:
    __slots__ = ("lw", "rd")

    def __init__(self):
        self.lw = None
        self.rd = {}


class Tile:
    def __init__(self, t, nsub=0):
        self.t = t
        self.b = Buf()
        self.sub = [Tile(t) for _ in range(nsub)]

    def __getitem__(self, idx):
        return self.t[idx]


class Ring:
    def __init__(self, tiles):
        self.tiles = tiles
        self.i = 0

    def next(self):
        t = self.tiles[self.i % len(self.tiles)]
        self.i += 1
        return t


class Prog:
    NDMA = 8
    LIM = 28000

    def __init__(self, nc):
        self.nc = nc
        self.eng = {"pe": nc.tensor, "act": nc.scalar, "dve": nc.vector,
                    "pool": nc.gpsimd, "sp": nc.sync}
        self.sem = {}
        self.cnt = {}
        self.cur = {}
        self.nsem = 0
        for k in ("pe", "act", "dve", "pool"):
            self._fresh(k)
        self.dqn = {"sp": 0, "pool": 0, "act": 0}
        for q in self.dqn:
            for i in range(self.NDMA):
                self._fresh(("dma", q, i))
        self.seen = {k: {} for k in self.eng}
        self.ninst = 0
        self.nwait = 0
        self.cvn = 0

    def _fresh(self, base):
        ep = self.cur.get(base, (None, -1))[1] + 1
        key = (base, ep)
        self.cur[base] = key
        self.sem[key] = self.nc.alloc_semaphore("s%d" % self.nsem)
        self.nsem += 1
        self.cnt[key] = 0
        return key

    def _wait(self, e, ev):
        if ev is None:
            return
        key, c = ev
        if key[0] == e and e == "pe":
            return
        if self.seen[e].get(key, 0) >= c:
            return
        self.seen[e][key] = c
        val = c * 16 if isinstance(key[0], tuple) else c
        self.eng[e].wait_ge(self.sem[key], val)
        self.nwait += 1

    def _deps(self, e, reads, writes):
        for t in reads:
            self._wait(e, t.b.lw)
        for t in writes:
            self._wait(e, t.b.lw)
            for k, c in t.b.rd.items():
                self._wait(e, (k, c))

    def _commit(self, ev, reads, writes):
        k, c = ev
        for t in reads:
            t.b.rd[k] = c
        for t in writes:
            t.b.lw = ev
            t.b.rd = {}

    def op(self, e, fn, reads=(), writes=()):
        key = self.cur[e]
        if self.cnt[key] >= self.LIM:
            key = self._fresh(e)
        self._deps(e, reads, writes)
        ins = fn(self.eng[e])
        self.cnt[key] += 1
        ins.then_inc(self.sem[key], 1)
        self._commit((key, self.cnt[key]), reads, writes)
        self.ninst += 1
        return ins

    def dma(self, q, out, in_, reads=(), writes=(), **kw):
        i = self.dqn[q] % self.NDMA
        self.dqn[q] += 1
        base = ("dma", q, i)
        key = self.cur[base]
        if self.cnt[key] > 0:
            self._wait(q, (key, self.cnt[key]))
        if self.cnt[key] * 16 >= self.LIM:
            key = self._fresh(base)
        self._deps(q, reads, writes)
        ins = self.eng[q].dma_start(out=out, in_=in_, **kw)
        self.cnt[key] += 1
        ins.then_inc(self.sem[key], 16)
        self._commit((key, self.cnt[key]), reads, writes)
        self.ninst += 1
        return ins

    def cvt(self, out, in_, tile):
        key = (("cv", self.cvn), 0)
        self.cvn += 1
        self.sem[key] = self.nc.alloc_semaphore("cv%d" % self.cvn)
        self.nsem += 1
        ins = self.nc.gpsimd.dma_start(out=out, in_=in_)
        ins.then_inc(self.sem[key], 16)
        self.cnt[key] = 1
        tile.b.lw = (key, 1)
        self.ninst += 1

    def barrier(self):
        for e in self.eng:
            for key, c in list(self.cnt.items()):
                if c > 0 and key[0] != e and not (isinstance(key[0], tuple) and key[0][0] == "cv"):
                    self._wait(e, (key, c))


def _rope_tables():
    t = np.arange(SEQ)
    row, col = (t // GRID_W).astype(np.float64), (t % GRID_W).astype(np.float64)

    def tab(dim):
        seg = dim // 2
        half = seg // 2
        C = np.ones((dim, TT))
        S = np.zeros((dim, TT))
        perm = np.zeros(dim, np.int64)
        inv = 10000.0 ** (-np.arange(half, dtype=np.float32) / half)
        for d in range(dim):
            s, e = d // seg, d % seg
            i = e % half
            first = e < half
            pos = row if s == 0 else col
            ang = (pos.astype(np.float32) * np.float32(inv[i])).astype(np.float32)
            C[d, :SEQ] = np.cos(ang)
            S[d, :SEQ] = -np.sin(ang) if first else np.sin(ang)
            perm[d] = d + half if first else d - half
        return C.astype(np.float32), S.astype(np.float32), perm

    C128, S128, p128 = tab(128)
    C64, S64, p64 = tab(64)
    rope128 = np.stack([C128, S128])
    rope64 = np.stack([np.concatenate([C64, C64], 0), np.concatenate([S64, S64], 0)])
    Pm128 = np.zeros((128, 128), np.float32)
    Pm128[p128, np.arange(128)] = 1.0
    Pm64 = np.zeros((128, 128), np.float32)
    for hh in range(2):
        Pm64[hh * 64 + p64, hh * 64 + np.arange(64)] = 1.0
    return rope128, rope64, np.stack([Pm128, Pm64])


def _amask():
    p = np.arange(128)
    a, kcol = p // 64, p % 64
    m = np.arange(14)
    j = 6 - m
    dr = j[None, :] + a[:, None]
    qcol = np.arange(64)
    c0 = np.clip(qcol - 8, 0, 48)
    col_in = (kcol[:, None] >= c0[None, :]) & (kcol[:, None] < c0[None, :] + 16)
    v_edge = np.broadcast_to(col_in[:, None, :], (128, 14, 64))
    v_gen = v_edge & ((dr >= -4) & (dr <= 3))[:, :, None]
    out = []
    for v in (v_gen, v_edge):
        out.append((v / SC128).astype(np.float32).reshape(128, 14 * 64))
        out.append(np.where(v, 0.0, NEG).astype(np.float32).reshape(128, 14 * 64))
    return np.stack(out)


def _gtab(rpb):
    p = np.arange(128)
    a, kcol = p // 64, p % 64
    j = 6 - np.arange(14)
    dr = j[None, :] + a[:, None]
    qcol = np.arange(64)
    dc = np.clip(kcol[:, None] - qcol[None, :] + 15, 0, 30)
    g = rpb[:, :, (dr + 7)[:, :, None], dc[:, None, :]]
    return np.ascontiguousarray(g.reshape(L, 4, 128, 14 * 64))


def _chunk_w(w, kc):
    Lw, K, N = w.shape
    return np.ascontiguousarray(w.reshape(Lw, kc, 128, N // 128, 128).transpose(0, 3, 2, 1, 4).reshape(Lw, N // 128, 128, kc * 128))


_CONST_CACHE = {}


def _consts():
    if not _CONST_CACHE:
        rope128, rope64, perm = _rope_tables()
        k = np.arange(128)
        tri = np.stack([np.where(k[:, None] >= k[None, :], 0.0, NEG), np.where(k[:, None] <= k[None, :], 0.0, NEG)], 1)
        _CONST_CACHE.update(rope128=rope128, rope64=rope64, perm=perm, amask=_amask(),
                            identf=np.eye(128, dtype=np.float32),
                            identb=np.eye(128, dtype=np.float32).astype(ml_dtypes.bfloat16),
                            trimask=np.ascontiguousarray(tri.reshape(128, 256).astype(np.float32)))
    return _CONST_CACHE


def prep_shared(inp):
    sh = {}
    w_in = inp["w_in"]
    sizes = [512, 512, 512, 512, 256, 256, 384, 128, 64, 512, 256, 256]
    offs = np.cumsum([0] + sizes)
    seg = lambda i: w_in[:, :, offs[i]:offs[i + 1]]
    w_in_r = np.concatenate([seg(0), seg(1), seg(2), seg(3), seg(4), seg(5), seg(6), seg(7), seg(9), seg(10), seg(11),
                             seg(8), np.zeros((L, D, 64), np.float32)], axis=2)
    sh["w_in"] = _chunk_w(w_in_r, KC)
    sh["w_out"] = _chunk_w(inp["w_out"], KC)
    sh["w_gate"] = _chunk_w(inp["ffn_w_gate"], KC)
    sh["w_up"] = _chunk_w(inp["ffn_w_up"], KC)
    sh["w_down"] = _chunk_w(inp["ffn_w_down"], FC)
    uq = inp["mla_w_uq"].reshape(L, 384, 4, 192)
    uq = np.concatenate([uq[..., :128].reshape(L, 384, 512), uq[..., 128:].reshape(L, 384, 256)], -1)
    sh["w_uq"] = np.ascontiguousarray(uq.reshape(L, 3, 128, 768).transpose(0, 2, 1, 3).reshape(L, 128, 3 * 768))
    ukv = inp["mla_w_ukv"].reshape(L, 128, 4, 256)
    sh["w_ukv"] = np.ascontiguousarray(np.concatenate([ukv[..., :128].reshape(L, 128, 512), ukv[..., 128:].reshape(L, 128, 512)], -1))
    sh["w_ada"] = np.ascontiguousarray(inp["w_ada"].reshape(L * D, 6 * D))
    vec = np.zeros((128, NVEC), np.float32)

    def put(name, l, arr):
        o, w = VEC[(name, l)]
        vec[:, o:o + w] = arr

    for l in range(L):
        put("b_ada", l, inp["b_ada"][l].reshape(96, 128).T)
        for n in ("ln1_g", "ln1_b", "ln2_g", "ln2_b"):
            put(n, l, inp[n][l].reshape(16, 128).T)
        put("conv_w", l, inp["ffn_conv_w"][l].reshape(3, FC, 128).transpose(2, 1, 0).reshape(128, FC * 3))
        put("conv_b", l, inp["ffn_conv_b"][l].reshape(FC, 128).T)
        put("gqa_qn", l, inp["gqa_q_norm"][l].reshape(128, 1))
        put("gqa_kn", l, inp["gqa_k_norm"][l].reshape(128, 1))
        put("mla_qn", l, inp["mla_q_norm"][l].reshape(3, 128).T)
        put("mla_kvn", l, inp["mla_kv_norm"][l].reshape(128, 1))
        put("sink", l, np.broadcast_to(inp["swa_sink"][l][None, :], (128, 4)))
    sh["vecs"] = vec
    sh["gtab"] = _gtab(inp["na_rpb"])
    sh.update(_consts())
    return sh


def build(dbg=(), stop_after=None):
    nc = bass.Bass("TRN2", target_bir_lowering=False)
    P = Prog(nc)
    es = ExitStack()

    def din(name, shape, dt=F32):
        return nc.dram_tensor(name, list(shape), dt, kind="ExternalInput").ap()

    def dscr(name, shape, dt):
        kind = "ExternalOutput" if name in dbg else "Internal"
        return nc.dram_tensor(name, list(shape), dt, kind=kind).ap()

    xin = din("xin", [TT, D])
    cvec = din("cvec", [128, 32])
    w_ada = din("w_ada", [L * D, 6 * D])
    vecs_d = din("vecs", [128, NVEC])
    w_in_d = din("w_in", [L, 33, 128, 2048])
    w_out_d = din("w_out", [L, 16, 128, 2048])
    w_gate_d = din("w_gate", [L, FC, 128, 2048])
    w_up_d = din("w_up", [L, FC, 128, 2048])
    w_down_d = din("w_down", [L, 16, 128, DFF])
    w_uq_d = din("w_uq", [L, 128, 3 * 768])
    w_ukv_d = din("w_ukv", [L, 128, 1024])
    gtab_d = din("gtab", [L, 4, 128, 14 * 64])
    rope128_d = din("rope128", [2, 128, TT])
    rope64_d = din("rope64", [2, 128, TT])
    perm_d = din("perm", [2, 128, 128])
    amask_d = din("amask", [4, 128, 14 * 64])
    identf_d = din("identf", [128, 128])
    identb_d = din("identb", [128, 128], BF16)
    trimask_d = din("trimask", [128, 256])
    out_d = nc.dram_tensor("out", [SEQ, D], F32, kind="ExternalOutput").ap()

    wb = {}
    wsrc = {"w_in": w_in_d, "w_out": w_out_d, "w_gate": w_gate_d, "w_up": w_up_d, "w_down": w_down_d,
            "w_uq": w_uq_d, "w_ukv": w_ukv_d}
    for n, src in wsrc.items():
        wb[n] = nc.dram_tensor(n + "_bf", list(src.shape), BF16, kind="Internal").ap()
    wbt = {(n, l): Tile(None) for n in wsrc for l in range(L)}

    xT = dscr("xT", [KC, 128, TT], F32)
    hT = dscr("hT", [KC, 128, TT], BF16)
    h2T = dscr("h2T", [KC, 128, H2W], BF16)
    fmT = dscr("fmT", [NFM, 128, TT], BF16)
    vtok = dscr("vtok", [NVT, TT, 128], BF16)
    mixT = dscr("mixT", [KC, 128, TT], BF16)

    uid = [0]

    def sb(name, shape, dt, stack=None, nsub=0):
        uid[0] += 1
        t = (stack or es).enter_context(nc.sbuf_tensor("sb%d_%s" % (uid[0], name), list(shape), dt))
        return Tile(t, nsub)

    def ring(name, n, shape, dt, stack=None, nsub=0):
        return Ring([sb("%s%d" % (name, i), shape, dt, stack, nsub) for i in range(n)])

    PS = [Tile(es.enter_context(nc.psum_tensor("ps%d" % i, [128, 512], F32))) for i in range(8)]

    def flat2k(ap):
        n = 1
        for s in ap.shape:
            n *= s
        names = " ".join("a%d" % i for i in range(len(ap.shape)))
        return ap.rearrange("%s -> (%s)" % (names, names)).rearrange("(r c) -> r c", c=2048)

    def convert_weights(l):
        for n in ("w_in", "w_uq", "w_ukv", "w_out", "w_gate", "w_up", "w_down"):
            P.cvt(flat2k(wb[n][l]), flat2k(wsrc[n][l]), wbt[(n, l)])

    convert_weights(0)

    identf = sb("identf", [128, 128], F32)
    identb = sb("identb", [128, 128], BF16)
    perm = sb("perm", [128, 2, 128], F32)
    trimask = sb("trimask", [128, 256], F32)
    vecs = sb("vecs", [128, NVEC], F32)
    onesb = sb("onesb", [128, 128], BF16)
    ones128 = sb("ones128", [128, 128], BF16)
    ones384 = sb("ones384", [128, 128], BF16)
    modv = sb("modv", [128, L, 96, 2], F32)
    esink = sb("esink", [128, L * 4], F32)
    P.dma("sp", identf[:], identf_d, writes=[identf])
    P.dma("sp", identb[:], identb_d, writes=[identb])
    P.dma("sp", perm[:], perm_d.rearrange("a p c -> p a c"), writes=[perm])
    P.dma("sp", trimask[:], trimask_d, writes=[trimask])
    P.dma("sp", vecs[:], vecs_d, writes=[vecs])
    P.op("dve", lambda e: e.memset(onesb[:], 1.0 / D), writes=[onesb])
    P.op("dve", lambda e: e.memset(ones128[:], 1.0 / 128), writes=[ones128])
    P.op("dve", lambda e: e.memset(ones384[:], 1.0 / 384), writes=[ones384])

    zcol = sb("zcol", [128, KC, 1], BF16)
    P.op("dve", lambda e: e.memset(zcol[:], 0.0), writes=[zcol])
    for c in (0, SEQ + 1, SEQ + 2, H2W - 1):
        P.dma("sp", h2T[:, :, c:c + 1].rearrange("kc p t -> p kc t"), zcol[:], reads=[zcol], allow_slow_non_contiguous=True)

    def vcol(name, l, j=0, w=1):
        o, _ = VEC[(name, l)]
        return vecs[:, o + j:o + j + w]

    def mod(l, s, kc, which):
        return modv[:, l, s * 16 + kc, which:which + 1]

    def ln_stats(zt, N, eps, pools, psM, psV):
        sqp, zbp, stp = pools
        for kc in range(KC):
            sq = sqp.next()
            P.op("act", lambda e: e.activation(out=sq[:, :N], in_=zt[:, kc, :N], func=AF.Square), reads=[zt.sub[kc]], writes=[sq])
            zb = zbp.next()
            P.op("pool" if kc % 2 == 0 else "dve", lambda e: e.tensor_copy(out=zb[:, :N], in_=zt[:, kc, :N]), reads=[zt.sub[kc]], writes=[zb])
            P.op("pe", lambda e: e.matmul(psM[:, :N], lhsT=onesb[:], rhs=zb[:, :N], start=(kc == 0), stop=(kc == KC - 1)),
                 reads=[onesb, zb], writes=[psM])
            P.op("pe", lambda e: e.matmul(psV[:, :N], lhsT=onesb[:], rhs=sq[:, :N], start=(kc == 0), stop=(kc == KC - 1)),
                 reads=[onesb, sq], writes=[psV])
        m2 = stp.next()
        P.op("act", lambda e: e.activation(out=m2[:, :N], in_=psM[:, :N], func=AF.Square), reads=[psM], writes=[m2])
        P.op("dve", lambda e: e.tensor_tensor(out=m2[:, :N], in0=psV[:, :N], in1=m2[:, :N], op=ALU.subtract), reads=[psV, m2], writes=[m2])
        P.op("dve", lambda e: e.tensor_scalar(out=m2[:, :N], in0=m2[:, :N], scalar1=float(eps), scalar2=None, op0=ALU.add), reads=[m2], writes=[m2])
        P.op("act", lambda e: e.activation(out=m2[:, :N], in_=m2[:, :N], func=AF.Sqrt), reads=[m2], writes=[m2])
        P.op("dve", lambda e: e.reciprocal(out=psV[:, :N], in_=m2[:, :N]), reads=[m2], writes=[psV])
        return psM, psV

    def ln_apply(zt, kc, N, Mt, Rt, scale_ap, bias_ap, out_ap, out_tile, tp, extra_reads=()):
        t = tp.next()
        P.op("dve", lambda e: e.tensor_tensor(out=t[:, :N], in0=zt[:, kc, :N], in1=Mt[:, :N], op=ALU.subtract), reads=[zt.sub[kc], Mt], writes=[t])
        P.op("dve", lambda e: e.tensor_tensor(out=t[:, :N], in0=t[:, :N], in1=Rt[:, :N], op=ALU.mult), reads=[t, Rt], writes=[t])
        P.op("act", lambda e: e.activation(out=out_ap, in_=t[:, :N], func=AF.Identity, scale=scale_ap, bias=bias_ap),
             reads=[t, vecs, modv] + list(extra_reads), writes=[out_tile])

    groups = [(g * 512, 512, 0) for g in range(8)] + [(SEQ, CTX, 1)]

    def h2col(t0, which):
        return (t0 if which == 0 else H2_CTX0)

    with ExitStack() as st:
        cv = sb("cv", [128, 32], F32, st)
        scv = sb("scv", [128, 32], F32, st)
        wblk = ring("wblk", 2, [128, KC, 512], F32, st)
        P.dma("sp", cv[:], cvec, writes=[cv])
        P.op("act", lambda e: e.activation(out=scv[:], in_=cv[:], func=AF.Silu), reads=[cv], writes=[scv])
        for l in range(L):
            psA = PS[l]
            psv = psA[:, 0:192].rearrange("p (j w) -> p j w", w=2)
            wl = w_ada[l * D:(l + 1) * D, :].rearrange("(kc p) n -> p kc n", p=128)
            for blk in range(24):
                wt = wblk.next()
                P.dma("sp", wt[:], wl[:, :, blk * 512:(blk + 1) * 512], writes=[wt])
                for j in range(4):
                    for kc in range(KC):
                        P.op("pe", lambda e: e.matmul(psv[:, blk * 4 + j, :], lhsT=wt[:, kc, j * 128:(j + 1) * 128],
                                                      rhs=scv[:, 2 * kc:2 * kc + 2], start=(kc == 0), stop=(kc == KC - 1)),
                             reads=[wt, scv], writes=[psA])
            o, w = VEC[("b_ada", l)]
            for which in range(2):
                P.op("dve", lambda e: e.tensor_tensor(out=modv[:, l, :, which], in0=psv[:, :, which], in1=vecs[:, o:o + 96], op=ALU.add),
                     reads=[psA, vecs], writes=[modv])
            for s in (1, 4):
                P.op("dve", lambda e: e.tensor_scalar(out=modv[:, l, s * 16:(s + 1) * 16, :], in0=modv[:, l, s * 16:(s + 1) * 16, :],
                                                      scalar1=1.0, scalar2=None, op0=ALU.add), reads=[modv], writes=[modv])
            for s in (2, 5):
                P.op("dve", lambda e: e.tensor_scalar(out=modv[:, l, s * 16:(s + 1) * 16, :], in0=modv[:, l, s * 16:(s + 1) * 16, :],
                                                      scalar1=1.0 / ALPHA, scalar2=None, op0=ALU.mult), reads=[modv], writes=[modv])
            o, w = VEC[("sink", l)]
            P.op("act", lambda e: e.activation(out=esink[:, l * 4:(l + 1) * 4], in_=vecs[:, o:o + 4], func=AF.Exp), reads=[vecs], writes=[esink])
        P.barrier()
    if "modv" in dbg:
        modv_o = nc.dram_tensor("modv_o", [128, L * 96 * 2], F32, kind="ExternalOutput").ap()
        P.dma("sp", modv_o, modv[:].rearrange("p l j w -> p (l j w)"), reads=[modv])

    def finish():
        P.barrier()
        es.close()
        return nc, P

    if stop_after == "ada":
        return finish()

    with ExitStack() as st:
        xr = ring("xin", 2, [128, 4, D], F32, st)
        ztr = ring("zt0", 2, [128, KC, 512], F32, st, nsub=KC)
        hbr = ring("hb0", 2, [128, KC, 512], BF16, st, nsub=KC)
        pools = (ring("sq0", 3, [128, 512], BF16, st), ring("zb0", 3, [128, 512], BF16, st), ring("st0", 6, [128, 512], F32, st))
        tp = ring("t0", 3, [128, 512], F32, st)
        psr = Ring([PS[0], PS[1], PS[2], PS[3]])
        stat_sets = Ring([(PS[4], PS[5]), (PS[6], PS[7])])
        for (t0, N, which) in groups:
            nt = N // 128
            xt = xr.next()
            P.dma("sp", xt[:, :nt, :], xin[t0:t0 + N, :].rearrange("(a p) d -> p a d", p=128), writes=[xt])
            zt = ztr.next()
            for kc in range(KC):
                ps = psr.next()
                for a in range(nt):
                    P.op("pe", lambda e: e.transpose(out=ps[:, a * 128:(a + 1) * 128], in_=xt[:, a, kc * 128:(kc + 1) * 128], identity=identf[:]),
                         reads=[xt, identf], writes=[ps])
                eng = "act" if kc % 2 == 0 else "dve"
                if eng == "act":
                    P.op("act", lambda e: e.activation(out=zt[:, kc, :N], in_=ps[:, :N], func=AF.Copy), reads=[ps], writes=[zt.sub[kc]])
                else:
                    P.op("dve", lambda e: e.tensor_copy(out=zt[:, kc, :N], in_=ps[:, :N]), reads=[ps], writes=[zt.sub[kc]])
            P.dma("pool", xT[:, :, t0:t0 + N].rearrange("kc p t -> p kc t"), zt[:, :, :N], reads=zt.sub)
            bM, bV = stat_sets.next()
            Mt, Rt = ln_stats(zt, N, EPS, pools, bM, bV)
            hb = hbr.next()
            for kc in range(KC):
                ln_apply(zt, kc, N, Mt, Rt, mod(0, 1, kc, which), mod(0, 0, kc, which), hb[:, kc, :N], hb.sub[kc], tp)
            P.dma("pool", hT[:, :, t0:t0 + N].rearrange("kc p t -> p kc t"), hb[:, :, :N], reads=hb.sub)
        P.barrier()
    if stop_after == "p0":
        return finish()


    def phase1(l):
        with ExitStack() as st:
            hr = ring("h1_", 2, [128, KC, 512], BF16, st)
            wr = ring("w1_", 6, [128, KC, 128], BF16, st)
            wv = sb("wv1", [128, 8, KC, 128], BF16, st)
            wuq = sb("wuq", [128, 3, 768], BF16, st)
            wukv = sb("wukv", [128, 1024], BF16, st)
            rc128 = ring("rc128_", 2, [128, 2, 512], F32, st)
            rc64 = ring("rc64_", 2, [128, 2, 512], F32, st)
            xsr = ring("xs1_", 4, [128, 512], F32, st)
            t1r = ring("t1_", 6, [128, 512], F32, st)
            obr = ring("ob1_", 6, [128, 512], BF16, st)
            sqr = ring("sq1_", 3, [128, 512], BF16, st)
            rsr = ring("rs1_", 3, [128, 512], F32, st)
            vobr = ring("vob1_", 3, [128, 512], BF16, st)
            cqs = sb("cqs", [128, 3, 512], F32, st)
            cqn = sb("cqn", [128, 3, 512], BF16, st)
            ckvn = sb("ckvn", [128, 512], BF16, st)
            mainr = Ring([PS[0], PS[1], PS[2]])
            ppr = Ring([PS[3], PS[4]])
            psR = PS[5]
            vr = Ring([PS[6], PS[7]])
            wt_in = wbt[("w_in", l)]
            wsrc_l = w_in_bf = wb["w_in"]

            def wchunks(c0, n):
                return wb["w_in"][l, c0:c0 + n].rearrange("c p (kc n) -> p c kc n", kc=KC)

            P.dma("sp", wv[:, 0:4], wchunks(8, 4), reads=[wt_in], writes=[wv])
            P.dma("sp", wv[:, 4:6], wchunks(18, 2), reads=[wt_in], writes=[wv])
            P.dma("sp", wv[:, 6:8], wchunks(30, 2), reads=[wt_in], writes=[wv])
            P.dma("sp", wuq[:], wb["w_uq"][l].rearrange("p (kc n) -> p kc n", kc=3), reads=[wbt[("w_uq", l)]], writes=[wuq])
            P.dma("sp", wukv[:], wb["w_ukv"][l], reads=[wbt[("w_ukv", l)]], writes=[wukv])
            cnt = [0]

            for (t0, N, which) in groups:
                skipq = (l == L - 1 and which == 1)
                nt = N // 128
                ht = hr.next()
                P.dma("sp", ht[:, :, :N], hT[:, :, t0:t0 + N].rearrange("kc p t -> p kc t"), writes=[ht])
                r128 = rc128.next()
                P.dma("sp", r128[:, :, :N], rope128_d[:, :, t0:t0 + N].rearrange("a p t -> p a t"), writes=[r128])
                r64 = rc64.next()
                P.dma("sp", r64[:, :, :N], rope64_d[:, :, t0:t0 + N].rearrange("a p t -> p a t"), writes=[r64])

                def proj(cc, M=128):
                    w = wr.next()
                    P.dma("sp", w[:], wb["w_in"][l, cc].rearrange("p (kc n) -> p kc n", kc=KC), reads=[wt_in], writes=[w])
                    ps = mainr.next()
                    for kc in range(KC):
                        P.op("pe", lambda e: e.matmul(ps[0:M, :N], lhsT=w[:, kc, 0:M], rhs=ht[:, kc, :N], start=(kc == 0), stop=(kc == KC - 1)),
                             reads=[w, ht], writes=[ps])
                    return ps

                def copy_out(dst_ap, dst_tile, src_ap, src_tile):
                    cnt[0] += 1
                    if cnt[0] % 2 == 0:
                        P.op("act", lambda e: e.activation(out=dst_ap, in_=src_ap, func=AF.Copy), reads=[src_tile], writes=[dst_tile])
                    else:
                        P.op("dve", lambda e: e.tensor_copy(out=dst_ap, in_=src_ap), reads=[src_tile], writes=[dst_tile])

                def evac_bf(ps, M=128):
                    ob = obr.next()
                    copy_out(ob[:M, :N], ob, ps[:M, :N], ps)
                    return ob

                def store_fm(idx, ob, M=128, prow=0):
                    P.dma("pool", fmT[idx, prow:prow + M, t0:t0 + N], ob[:M, :N], reads=[ob])

                def rope(xs, M, tab, pmi):
                    pp = ppr.next()
                    P.op("pe", lambda e: e.matmul(pp[:M, :N], lhsT=perm[:M, pmi, :M], rhs=xs[:M, :N], start=True, stop=True),
                         reads=[perm, xs], writes=[pp])
                    t1 = t1r.next()
                    P.op("pool", lambda e: e.tensor_tensor(out=t1[:M, :N], in0=xs[:M, :N], in1=tab[:M, 0, :N], op=ALU.mult), reads=[xs, tab], writes=[t1])
                    t2 = t1r.next()
                    P.op("dve", lambda e: e.tensor_tensor(out=t2[:M, :N], in0=pp[:M, :N], in1=tab[:M, 1, :N], op=ALU.mult), reads=[pp, tab], writes=[t2])
                    ob = obr.next()
                    P.op("dve", lambda e: e.tensor_tensor(out=ob[:M, :N], in0=t1[:M, :N], in1=t2[:M, :N], op=ALU.add), reads=[t1, t2], writes=[ob])
                    return ob

                def rstd_from(psr_tile):
                    m = rsr.next()
                    P.op("dve", lambda e: e.tensor_scalar(out=m[:, :N], in0=psr_tile[:, :N], scalar1=EPS, scalar2=None, op0=ALU.add), reads=[psr_tile], writes=[m])
                    P.op("act", lambda e: e.activation(out=m[:, :N], in_=m[:, :N], func=AF.Sqrt), reads=[m], writes=[m])
                    P.op("dve", lambda e: e.reciprocal(out=m[:, :N], in_=m[:, :N]), reads=[m], writes=[m])
                    return m

                for cc in range(0, 8):
                    if skipq and cc < 4:
                        continue
                    ps = proj(cc)
                    store_fm(QA + cc, evac_bf(ps))
                for i, cc in enumerate(range(12, 18)):
                    if skipq and i < 4:
                        continue
                    ps = proj(cc)
                    xs = xsr.next()
                    copy_out(xs[:, :N], xs, ps[:, :N], ps)
                    store_fm(QB + i, rope(xs, 128, r128, 0))
                for i, cc in enumerate(range(24, 30)):
                    if skipq and i < 4:
                        continue
                    ps = proj(cc)
                    sq = sqr.next()
                    P.op("act", lambda e: e.activation(out=sq[:, :N], in_=ps[:, :N], func=AF.Square), reads=[ps], writes=[sq])
                    P.op("pe", lambda e: e.matmul(psR[:, :N], lhsT=ones128[:], rhs=sq[:, :N], start=True, stop=True), reads=[ones128, sq], writes=[psR])
                    m = rstd_from(psR)
                    xs = xsr.next()
                    g = vcol("gqa_qn" if i < 4 else "gqa_kn", l)
                    P.op("dve", lambda e: e.scalar_tensor_tensor(out=xs[:, :N], in0=ps[:, :N], scalar=g, in1=m[:, :N], op0=ALU.mult, op1=ALU.mult),
                         reads=[ps, m, vecs], writes=[xs])
                    store_fm(QD + i, rope(xs, 128, r128, 0))
                if not skipq:
                    for i in range(3):
                        ps = proj(20 + i)
                        P.op("act", lambda e: e.activation(out=cqs[:, i, :N], in_=ps[:, :N], func=AF.Copy), reads=[ps], writes=[cqs])
                        sq = sqr.next()
                        P.op("act", lambda e: e.activation(out=sq[:, :N], in_=ps[:, :N], func=AF.Square), reads=[ps], writes=[sq])
                        P.op("pe", lambda e: e.matmul(psR[:, :N], lhsT=ones384[:], rhs=sq[:, :N], start=(i == 0), stop=(i == 2)), reads=[ones384, sq], writes=[psR])
                    m = rstd_from(psR)
                    for i in range(3):
                        P.op("dve", lambda e: e.scalar_tensor_tensor(out=cqn[:, i, :N], in0=cqs[:, i, :N], scalar=vcol("mla_qn", l, i), in1=m[:, :N],
                                                                    op0=ALU.mult, op1=ALU.mult), reads=[cqs, m, vecs], writes=[cqn])
                    for h in range(4):
                        ps = mainr.next()
                        for i in range(3):
                            P.op("pe", lambda e: e.matmul(ps[:, :N], lhsT=wuq[:, i, h * 128:(h + 1) * 128], rhs=cqn[:, i, :N], start=(i == 0), stop=(i == 2)),
                                 reads=[wuq, cqn], writes=[ps])
                        store_fm(QNC + h, evac_bf(ps))
                    for j in range(2):
                        ps = mainr.next()
                        for i in range(3):
                            P.op("pe", lambda e: e.matmul(ps[:, :N], lhsT=wuq[:, i, 512 + j * 128:512 + (j + 1) * 128], rhs=cqn[:, i, :N], start=(i == 0), stop=(i == 2)),
                                 reads=[wuq, cqn], writes=[ps])
                        xs = xsr.next()
                        copy_out(xs[:, :N], xs, ps[:, :N], ps)
                        store_fm(QPEC + j, rope(xs, 128, r64, 1))
                ps = proj(23)
                xs = xsr.next()
                P.op("act", lambda e: e.activation(out=xs[:, :N], in_=ps[:, :N], func=AF.Copy), reads=[ps], writes=[xs])
                sq = sqr.next()
                P.op("act", lambda e: e.activation(out=sq[:, :N], in_=ps[:, :N], func=AF.Square), reads=[ps], writes=[sq])
                P.op("pe", lambda e: e.matmul(psR[:, :N], lhsT=ones128[:], rhs=sq[:, :N], start=True, stop=True), reads=[ones128, sq], writes=[psR])
                m = rstd_from(psR)
                P.op("dve", lambda e: e.scalar_tensor_tensor(out=ckvn[:, :N], in0=xs[:, :N], scalar=vcol("mla_kvn", l), in1=m[:, :N], op0=ALU.mult, op1=ALU.mult),
                     reads=[xs, m, vecs], writes=[ckvn])
                for h in range(4):
                    ps = mainr.next()
                    P.op("pe", lambda e: e.matmul(ps[:, :N], lhsT=wukv[:, h * 128:(h + 1) * 128], rhs=ckvn[:, :N], start=True, stop=True), reads=[wukv, ckvn], writes=[ps])
                    store_fm(KNC + h, evac_bf(ps))
                for a in range(nt):
                    psv = vr.next()
                    P.op("pe", lambda e: e.matmul(psv[:, :], lhsT=ckvn[:, a * 128:(a + 1) * 128], rhs=wukv[:, 512:1024], start=True, stop=True), reads=[wukv, ckvn], writes=[psv])
                    vob = vobr.next()
                    copy_out(vob[:, :], vob, psv[:, :], psv)
                    P.dma("pool", vtok[VC:VC + 4, t0 + a * 128:t0 + (a + 1) * 128, :].rearrange("h t d -> t h d"),
                          vob[:, :].rearrange("t (h d) -> t h d", h=4), reads=[vob])
                ps = proj(32, M=64)
                xs = xsr.next()
                copy_out(xs[:64, :N], xs, ps[:64, :N], ps)
                ob = rope(xs, 64, r64, 1)
                store_fm(KPEC, ob, M=64, prow=0)
                store_fm(KPEC, ob, M=64, prow=64)
                for a in range(nt):
                    for blk in range(2):
                        psv = vr.next()
                        for kc in range(KC):
                            P.op("pe", lambda e: e.matmul(psv[:, :].rearrange("t (c n) -> t c n", c=4), lhsT=ht[:, kc, a * 128:(a + 1) * 128],
                                                          rhs=wv[:, blk * 4:(blk + 1) * 4, kc, :], start=(kc == 0), stop=(kc == KC - 1)),
                                 reads=[ht, wv], writes=[psv])
                        vob = vobr.next()
                        copy_out(vob[:, :], vob, psv[:, :], psv)
                        tsl = slice(t0 + a * 128, t0 + (a + 1) * 128)
                        if blk == 0:
                            P.dma("pool", vtok[VA:VA + 4, tsl, :].rearrange("h t d -> t h d"), vob[:, :].rearrange("t (h d) -> t h d", h=4), reads=[vob])
                        else:
                            P.dma("pool", vtok[VB:VB + 2, tsl, :].rearrange("h t d -> t h d"), vob[:, 0:256].rearrange("t (h d) -> t h d", h=2), reads=[vob])
                            P.dma("pool", vtok[VD:VD + 2, tsl, :].rearrange("h t d -> t h d"), vob[:, 256:512].rearrange("t (h d) -> t h d", h=2), reads=[vob])
            P.barrier()


    def phase2(l):
        need_ctx = l < L - 1
        if l + 1 < L:
            convert_weights(l + 1)
        qgroups = groups if need_ctx else groups[:8]
        with ExitStack() as st:
            ktr = ring("kt2_", 2, [128, TT], BF16, st)
            vtr = ring("vt2_", 2, [128, NKT, 130], BF16, st)
            kpe = sb("kpe2", [128, TT], BF16, st)
            qr = ring("q2_", 3, [128, 512], BF16, st)
            qpr = ring("qp2_", 3, [128, 512], BF16, st)
            ptr = ring("pt2_", 4, [128, 512], BF16, st)
            onr = ring("on2_", 4, [128, 128], BF16, st)
            mor = ring("mo2_", 3, [128, 512], BF16, st)
            rir = ring("ri2_", 8, [128, 1], F32, st)
            amask = sb("amask", [128, 4, 896], F32, st)
            gtr = ring("gt2_", 2, [128, 896], F32, st)
            ggr = ring("gg2_", 2, [128, 896], F32, st)
            ger = ring("ge2_", 2, [128, 896], F32, st)
            psT = PS[6]
            psT_bf = psT[:, :].bitcast(BF16)
            for vt in vtr.tiles:
                P.op("dve", lambda e: e.memset(vt[:, :, 128:129], 1.0), writes=[vt])
                P.op("dve", lambda e: e.memset(vt[:, :, 129:130], 0.0), writes=[vt])
            P.dma("sp", amask[:], amask_d.rearrange("a p c -> p a c"), writes=[amask])
            P.dma("sp", kpe[:], fmT[KPEC], writes=[kpe])

            def load_kv(kidx, vidx):
                kt = ktr.next()
                P.dma("sp", kt[:], fmT[kidx], writes=[kt])
                vt = vtr.next()
                P.dma("sp", vt[:, :, 0:128], vtok[vidx].rearrange("(kt p) d -> p kt d", p=128), writes=[vt])
                return kt, vt

            def load_q(idx, t0, N, r=None):
                q = (r or qr).next()
                P.dma("sp", q[:, :N], fmT[idx, :, t0:t0 + N], writes=[q])
                return q

            def fin_tile(O, col, sink_ap=None):
                ri = rir.next()
                if sink_ap is not None:
                    P.op("dve", lambda e: e.tensor_scalar(out=ri[:], in0=O[:, 128:129], scalar1=sink_ap, scalar2=None, op0=ALU.add), reads=[O, esink], writes=[ri])
                    P.op("dve", lambda e: e.reciprocal(out=ri[:], in_=ri[:]), reads=[ri], writes=[ri])
                else:
                    P.op("dve", lambda e: e.reciprocal(out=ri[:], in_=O[:, 128:129]), reads=[O], writes=[ri])
                on = onr.next()
                P.op("act", lambda e: e.activation(out=on[:], in_=O[:, 0:128], func=AF.Copy, scale=ri[:]), reads=[O, ri], writes=[on])
                P.op("pe", lambda e: e.transpose(out=psT_bf[:, col:col + 128], in_=on[:], identity=identb[:]), reads=[on, identb], writes=[psT])

            def fin_group(head, t0, N):
                mo = mor.next()
                P.op("dve", lambda e: e.tensor_copy(out=mo[:, :N], in_=psT_bf[:, :N]), reads=[psT], writes=[mo])
                P.dma("pool", mixT[head, :, t0:t0 + N], mo[:, :N], reads=[mo])

            sring = Ring([PS[0], PS[1], PS[7]])
            Od = [PS[2], PS[3], PS[4], PS[5]]

            def dense(ktiles, N, smm, scale, vt):
                nq = N // 128
                n = len(ktiles)
                sbank = [None] * n

                def issue_s(i):
                    sbank[i] = sring.next()
                    smm(ktiles[i], sbank[i])

                issue_s(0)
                if n > 1:
                    issue_s(1)
                for i, ktile in enumerate(ktiles):
                    s_ = sbank[i]
                    pt = ptr.next()
                    P.op("act", lambda e: e.activation(out=pt[:, :N], in_=s_[:, :N], func=AF.Exp, scale=scale), reads=[s_], writes=[pt])
                    if i + 2 < n:
                        issue_s(i + 2)
                    for a in range(nq):
                        P.op("pe", lambda e: e.matmul(Od[a][:, 0:130], lhsT=pt[:, a * 128:(a + 1) * 128], rhs=vt[:, ktile, :], start=(i == 0), stop=(i == n - 1)),
                             reads=[pt, vt], writes=[Od[a]])

            def run_dense(kind):
                heads = range(4)
                kt = vt = None
                for h in heads:
                    if kind == "D":
                        if h % 2 == 0:
                            kt, vt = load_kv(KD + h // 2, VD + h // 2)
                        head = 12 + h
                    else:
                        kt, vt = load_kv(KNC + h, VC + h)
                        head = 8 + h
                    for (t0, N, which) in qgroups:
                        ktiles = list(range(NKT)) if which == 0 else [32, 33]
                        if kind == "D":
                            q = load_q(QD + h, t0, N)

                            def smm(ktile, s_):
                                P.op("pe", lambda e: e.matmul(s_[:, :N], lhsT=kt[:, ktile * 128:(ktile + 1) * 128], rhs=q[:, :N], start=True, stop=True),
                                     reads=[kt, q], writes=[s_])
                            dense(ktiles, N, smm, SC128, vt)
                        else:
                            q = load_q(QNC + h, t0, N)
                            qp = load_q(QPEC + h // 2, t0, N, qpr)
                            lo = (h % 2) * 64

                            def smm(ktile, s_):
                                P.op("pe", lambda e: e.matmul(s_[:, :N], lhsT=kt[:, ktile * 128:(ktile + 1) * 128], rhs=q[:, :N], start=True, stop=False),
                                     reads=[kt, q], writes=[s_])
                                P.op("pe", lambda e: e.matmul(s_[:, :N], lhsT=kpe[lo:lo + 64, ktile * 128:(ktile + 1) * 128], rhs=qp[lo:lo + 64, :N], start=False, stop=True),
                                     reads=[kpe, qp], writes=[s_])
                            dense(ktiles, N, smm, SC192, vt)
                        for a in range(N // 128):
                            fin_tile(Od[a], a * 128)
                        fin_group(head, t0, N)

            ssets = Ring([(PS[0], PS[1]), (PS[2], PS[3])])
            Ow = Ring([PS[4], PS[5], PS[7]])

            wjobs = []

            def window_tile(kt, vt, qf, a, blocks, bias_fn, sink_ap, col, post=None):
                nb = len(blocks)
                nA = min(nb, 4)
                nB = nb - nA

                def stage1():
                    q = qf()
                    bA, bB = ssets.next()
                    for i, ktile in enumerate(blocks):
                        bank, c = (bA, i * 128) if i < 4 else (bB, (i - 4) * 128)
                        P.op("pe", lambda e: e.matmul(bank[:, c:c + 128], lhsT=kt[:, ktile * 128:(ktile + 1) * 128], rhs=q[:, a * 128:(a + 1) * 128], start=True, stop=True),
                             reads=[kt, q], writes=[bank])
                    bias_fn(bA, bB)
                    ptA = ptr.next()
                    P.op("act", lambda e: e.activation(out=ptA[:, :nA * 128], in_=bA[:, :nA * 128], func=AF.Exp, scale=SC128), reads=[bA], writes=[ptA])
                    ptB = None
                    if nB:
                        ptB = ptr.next()
                        P.op("act", lambda e: e.activation(out=ptB[:, :nB * 128], in_=bB[:, :nB * 128], func=AF.Exp, scale=SC128), reads=[bB], writes=[ptB])
                    return ptA, ptB

                def stage2(st_):
                    ptA, ptB = st_
                    O = Ow.next()
                    for i, ktile in enumerate(blocks):
                        pt, c = (ptA, i * 128) if i < 4 else (ptB, (i - 4) * 128)
                        P.op("pe", lambda e: e.matmul(O[:, 0:130], lhsT=pt[:, c:c + 128], rhs=vt[:, ktile, :], start=(i == 0), stop=(i == nb - 1)),
                             reads=[pt, vt], writes=[O])
                    fin_tile(O, col, sink_ap)
                    if post is not None:
                        post()

                wjobs.append((stage1, stage2))

            def run_wjobs():
                prev = None
                for (s1, s2) in wjobs:
                    st_ = s1()
                    if prev is not None:
                        prev[0](prev[1])
                    prev = (s2, st_)
                if prev is not None:
                    prev[0](prev[1])
                del wjobs[:]

            def addbias(bank, c0, c1, tab, tab_ap):
                P.op("dve", lambda e: e.tensor_tensor(out=bank[:, c0:c1], in0=bank[:, c0:c1], in1=tab_ap, op=ALU.add), reads=[bank, tab], writes=[bank])

            def run_A():
                for h in range(4):
                    kt, vt = load_kv(KA + h, VA + h)
                    gt = gtr.next()
                    P.dma("sp", gt[:], gtab_d[l, h], writes=[gt])
                    gg, ge = ggr.next(), ger.next()
                    for (g_, mi) in ((gg, 0), (ge, 2)):
                        P.op("dve", lambda e: e.tensor_tensor(out=g_[:], in0=gt[:], in1=amask[:, mi, :], op=ALU.mult), reads=[gt, amask], writes=[g_])
                        P.op("dve", lambda e: e.tensor_tensor(out=g_[:], in0=g_[:], in1=amask[:, mi + 1, :], op=ALU.add), reads=[g_, amask], writes=[g_])
                    for (t0, N, which) in qgroups:
                        qh_ = {}
                        q = (lambda qh_=qh_, h=h, t0=t0, N=N: qh_["q"] if "q" in qh_ else qh_.setdefault("q", load_q(QA + h, t0, N)))
                        for a in range(N // 128):
                            qt = t0 // 128 + a
                            post = (lambda h=h, t0=t0, N=N: fin_group(h, t0, N)) if a == N // 128 - 1 else None
                            if which == 1:
                                window_tile(kt, vt, q, a, [32, 33], lambda bA, bB: None, None, a * 128, post)
                                continue
                            if 2 <= qt <= 29:
                                d0, nw, tab = 2, 5, gg
                            elif qt == 0:
                                d0, nw, tab = 3, 4, ge
                            elif qt == 1:
                                d0, nw, tab = 2, 4, ge
                            elif qt == 30:
                                d0, nw, tab = 1, 4, ge
                            else:
                                d0, nw, tab = 0, 4, ge
                            m0 = 6 - 2 * d0
                            blocks = [qt + d0 - i for i in range(nw)] + [32, 33]

                            def bias_fn(bA, bB, m0=m0, nw=nw, tab=tab):
                                addbias(bA, 0, 512, tab, tab[:, m0 * 64:m0 * 64 + 512])
                                if nw == 5:
                                    addbias(bB, 0, 128, tab, tab[:, (m0 + 8) * 64:(m0 + 10) * 64])
                            window_tile(kt, vt, q, a, blocks, bias_fn, None, a * 128, post)
                    run_wjobs()

            def run_B():
                kt = vt = None
                for h in range(4):
                    if h % 2 == 0:
                        kt, vt = load_kv(KB + h // 2, VB + h // 2)
                    sink_ap = esink[:, l * 4 + h:l * 4 + h + 1]
                    for (t0, N, which) in qgroups:
                        qh_ = {}
                        q = (lambda qh_=qh_, h=h, t0=t0, N=N: qh_["q"] if "q" in qh_ else qh_.setdefault("q", load_q(QB + h, t0, N)))
                        for a in range(N // 128):
                            qt = t0 // 128 + a
                            post = (lambda h=h, t0=t0, N=N: fin_group(4 + h, t0, N)) if a == N // 128 - 1 else None
                            if which == 1:
                                window_tile(kt, vt, q, a, [32, 33], lambda bA, bB: None, sink_ap, a * 128, post)
                                continue
                            masked = ([qt - 1] if qt > 0 else []) + ([qt + 1] if qt < 31 else [])
                            blocks = masked + [qt, 32, 33]

                            def bias_fn(bA, bB, qt=qt):
                                if 0 < qt < 31:
                                    addbias(bA, 0, 256, trimask, trimask[:, 0:256])
                                elif qt == 0:
                                    addbias(bA, 0, 128, trimask, trimask[:, 128:256])
                                else:
                                    addbias(bA, 0, 128, trimask, trimask[:, 0:128])
                            window_tile(kt, vt, q, a, blocks, bias_fn, sink_ap, a * 128, post)
                    run_wjobs()

            run_A()
            run_B()
            run_dense("C")
            run_dense("D")
            P.barrier()


    def phase3(l):
        grp = groups if l < L - 1 else groups[:8]
        with ExitStack() as st:
            mr = ring("m3_", 2, [128, KC, 512], BF16, st)
            ztr = ring("z3_", 2, [128, KC, 512], F32, st, nsub=KC)
            wr = ring("w3_", 4, [128, KC, 128], BF16, st)
            hbr = ring("hb3_", 2, [128, KC, 512], BF16, st, nsub=KC)
            pools = (ring("sq3_", 3, [128, 512], BF16, st), ring("zb3_", 3, [128, 512], BF16, st), ring("st3_", 6, [128, 512], F32, st))
            tp = ring("t3_", 3, [128, 512], F32, st)
            mainr = Ring([PS[0], PS[1], PS[2], PS[3]])
            wt_ = wbt[("w_out", l)]
            for (t0, N, which) in grp:
                mt = mr.next()
                P.dma("sp", mt[:, :, :N], mixT[:, :, t0:t0 + N].rearrange("h p t -> p h t"), writes=[mt])
                zt = ztr.next()
                P.dma("sp", zt[:, :, :N], xT[:, :, t0:t0 + N].rearrange("kc p t -> p kc t"), writes=zt.sub)
                for cc in range(KC):
                    w = wr.next()
                    P.dma("sp", w[:], wb["w_out"][l, cc].rearrange("p (kc n) -> p kc n", kc=KC), reads=[wt_], writes=[w])
                    ps = mainr.next()
                    for kc in range(KC):
                        P.op("pe", lambda e: e.matmul(ps[:, :N], lhsT=w[:, kc, :], rhs=mt[:, kc, :N], start=(kc == 0), stop=(kc == KC - 1)),
                             reads=[w, mt], writes=[ps])
                    P.op("dve", lambda e: e.scalar_tensor_tensor(out=zt[:, cc, :N], in0=ps[:, :N], scalar=mod(l, 2, cc, which), in1=zt[:, cc, :N],
                                                                op0=ALU.mult, op1=ALU.add), reads=[ps, zt.sub[cc], modv], writes=[zt.sub[cc]])
                Mt, Rt = ln_stats(zt, N, EPS / ALPHA ** 2, pools, PS[4], PS[5])
                for kc in range(KC):
                    ln_apply(zt, kc, N, Mt, Rt, vcol("ln1_g", l, kc), vcol("ln1_b", l, kc), zt[:, kc, :N], zt.sub[kc], tp)
                P.dma("pool", xT[:, :, t0:t0 + N].rearrange("kc p t -> p kc t"), zt[:, :, :N], reads=zt.sub)
                Mt, Rt = ln_stats(zt, N, EPS, pools, PS[6], PS[7])
                hb = hbr.next()
                for kc in range(KC):
                    ln_apply(zt, kc, N, Mt, Rt, mod(l, 4, kc, which), mod(l, 3, kc, which), hb[:, kc, :N], hb.sub[kc], tp)
                c0 = h2col(t0, which) + 1
                P.dma("pool", h2T[:, :, c0:c0 + N].rearrange("kc p t -> p kc t"), hb[:, :, :N], reads=hb.sub)
            P.barrier()

    def phase4(l):
        last = (l == L - 1)
        grp = [(510 * n, 510, 0) for n in range(8)] + [(4080, 16, 0)]
        if not last:
            grp.append((SEQ, CTX, 1))
        with ExitStack() as st:
            h2 = sb("h4", [128, KC, 512], BF16, st)
            zt = sb("z4", [128, KC, 512], F32, st, nsub=KC)
            act = sb("act4", [128, FC, 512], BF16, st)
            hb = Tile(act.t, nsub=KC)
            for s_ in hb.sub:
                s_.b = act.b
            wgr = ring("wg4_", 3, [128, KC, 128], BF16, st)
            wur = ring("wu4_", 3, [128, KC, 128], BF16, st)
            wdr = ring("wd4_", 2, [128, FC, 128], BF16, st)
            gbr = ring("gb4_", 3, [128, 512], F32, st)
            a1r = ring("a14_", 3, [128, 512], F32, st)
            sr = ring("s4_", 2, [128, 512], F32, st)
            pools = (ring("sq4_", 2, [128, 512], BF16, st), ring("zb4_", 2, [128, 512], BF16, st), ring("st4_", 3, [128, 512], F32, st))
            tp = ring("t4_", 3, [128, 512], F32, st)
            ost = sb("ost4", [128, D], F32, st) if last else None
            gr = Ring([PS[0], PS[1]])
            ur = Ring([PS[2], PS[3]])
            dr = Ring([PS[0], PS[1], PS[2], PS[3]])
            o_w, _ = VEC[("conv_w", l)]
            o_b, _ = VEC[("conv_b", l)]
            for (t0, N, which) in grp:
                c0 = h2col(t0, which)
                P.dma("sp", h2[:, :, :N + 2], h2T[:, :, c0:c0 + N + 2].rearrange("kc p t -> p kc t"), writes=[h2])
                P.dma("sp", zt[:, :, :N], xT[:, :, t0:t0 + N].rearrange("kc p t -> p kc t"), writes=zt.sub)
                for j in range(FC):
                    wg = wgr.next()
                    P.dma("sp", wg[:], wb["w_gate"][l, j].rearrange("p (kc n) -> p kc n", kc=KC), reads=[wbt[("w_gate", l)]], writes=[wg])
                    wu = wur.next()
                    P.dma("sp", wu[:], wb["w_up"][l, j].rearrange("p (kc n) -> p kc n", kc=KC), reads=[wbt[("w_up", l)]], writes=[wu])
                    psG, psU = gr.next(), ur.next()
                    for kc in range(KC):
                        P.op("pe", lambda e: e.matmul(psG[:, :N + 2], lhsT=wg[:, kc, :], rhs=h2[:, kc, 0:N + 2], start=(kc == 0), stop=(kc == KC - 1)),
                             reads=[wg, h2], writes=[psG])
                    for kc in range(KC):
                        P.op("pe", lambda e: e.matmul(psU[:, :N], lhsT=wu[:, kc, :], rhs=h2[:, kc, 1:N + 1], start=(kc == 0), stop=(kc == KC - 1)),
                             reads=[wu, h2], writes=[psU])
                    gb = gbr.next()
                    P.op("act", lambda e: e.activation(out=gb[:, :N + 2], in_=psG[:, :N + 2], func=AF.Copy), reads=[psG], writes=[gb])
                    a1 = a1r.next()
                    w0 = vecs[:, o_w + j * 3 + 0:o_w + j * 3 + 1]
                    w1 = vecs[:, o_w + j * 3 + 1:o_w + j * 3 + 2]
                    w2 = vecs[:, o_w + j * 3 + 2:o_w + j * 3 + 3]
                    cb = vecs[:, o_b + j:o_b + j + 1]
                    P.op("act", lambda e: e.activation(out=a1[:, :N], in_=gb[:, 1:N + 1], func=AF.Identity, scale=w1, bias=cb), reads=[gb, vecs], writes=[a1])
                    P.op("dve", lambda e: e.scalar_tensor_tensor(out=a1[:, :N], in0=gb[:, 0:N], scalar=w0, in1=a1[:, :N], op0=ALU.mult, op1=ALU.add),
                         reads=[gb, a1, vecs], writes=[a1])
                    P.op("dve", lambda e: e.scalar_tensor_tensor(out=a1[:, :N], in0=gb[:, 2:N + 2], scalar=w2, in1=a1[:, :N], op0=ALU.mult, op1=ALU.add),
                         reads=[gb, a1, vecs], writes=[a1])
                    sg = sr.next()
                    P.op("act", lambda e: e.activation(out=sg[:, :N], in_=a1[:, :N], func=AF.Silu), reads=[a1], writes=[sg])
                    P.op("dve", lambda e: e.tensor_tensor(out=act[:, j, :N], in0=sg[:, :N], in1=psU[:, :N], op=ALU.mult), reads=[sg, psU], writes=[act])
                for cc in range(KC):
                    wd = wdr.next()
                    P.dma("sp", wd[:], wb["w_down"][l, cc].rearrange("p (j n) -> p j n", j=FC), reads=[wbt[("w_down", l)]], writes=[wd])
                    ps = dr.next()
                    for j in range(FC):
                        P.op("pe", lambda e: e.matmul(ps[:, :N], lhsT=wd[:, j, :], rhs=act[:, j, :N], start=(j == 0), stop=(j == FC - 1)),
                             reads=[wd, act], writes=[ps])
                    P.op("dve", lambda e: e.scalar_tensor_tensor(out=zt[:, cc, :N], in0=ps[:, :N], scalar=mod(l, 5, cc, which), in1=zt[:, cc, :N],
                                                                op0=ALU.mult, op1=ALU.add), reads=[ps, zt.sub[cc], modv], writes=[zt.sub[cc]])
                Mt, Rt = ln_stats(zt, N, EPS / ALPHA ** 2, pools, PS[5], PS[6])
                for kc in range(KC):
                    ln_apply(zt, kc, N, Mt, Rt, vcol("ln2_g", l, kc), vcol("ln2_b", l, kc), zt[:, kc, :N], zt.sub[kc], tp)
                if not last:
                    P.dma("pool", xT[:, :, t0:t0 + N].rearrange("kc p t -> p kc t"), zt[:, :, :N], reads=zt.sub)
                    Mt, Rt = ln_stats(zt, N, EPS, pools, PS[4], PS[7])
                    for kc in range(KC):
                        ln_apply(zt, kc, N, Mt, Rt, mod(l + 1, 1, kc, which), mod(l + 1, 0, kc, which), hb[:, kc, :N], hb.sub[kc], tp)
                    P.dma("pool", hT[:, :, t0:t0 + N].rearrange("kc p t -> p kc t"), hb[:, 0:KC, :N], reads=[act])
                else:
                    for c_ in range(0, N, 128):
                        w_ = min(128, N - c_)
                        for k4 in range(4):
                            ps = dr.next()
                            for i in range(4):
                                kc = k4 * 4 + i
                                P.op("pe", lambda e: e.transpose(out=ps[0:w_, i * 128:(i + 1) * 128], in_=zt[:, kc, c_:c_ + w_], identity=identf[:]),
                                     reads=[zt.sub[kc], identf], writes=[ps])
                            if k4 % 2 == 0:
                                P.op("act", lambda e: e.activation(out=ost[0:w_, k4 * 512:(k4 + 1) * 512], in_=ps[0:w_, :], func=AF.Copy), reads=[ps], writes=[ost])
                            else:
                                P.op("dve", lambda e: e.tensor_copy(out=ost[0:w_, k4 * 512:(k4 + 1) * 512], in_=ps[0:w_, :]), reads=[ps], writes=[ost])
                        P.dma("pool", out_d[t0 + c_:t0 + c_ + w_, :], ost[0:w_, :], reads=[ost])
            P.barrier()

    for l in range(L):
        for nm, ph in (("p1", phase1), ("p2", phase2), ("p3", phase3), ("p4", phase4)):
            ph(l)
            if stop_after == "%s_%d" % (nm, l):
                return finish()
    return finish()


_NC_CACHE = {}


def kernel(**inp):
    inp = {k: np.asarray(v) for k, v in inp.items()}
    sh = prep_shared(inp)
    if "nc" not in _NC_CACHE:
        _NC_CACHE["nc"] = build()[0]
    nc = _NC_CACHE["nc"]
    B = inp["x"].shape[0]
    in_maps = []
    for b in range(B):
        m = dict(sh)
        m["xin"] = np.ascontiguousarray(np.concatenate([inp["x"][b], inp["ctx"][b]], 0))
        cv = np.stack([inp["c"][b].reshape(16, 128).T, inp["c_ctx"].reshape(16, 128).T], -1).reshape(128, 32)
        m["cvec"] = np.ascontiguousarray(cv)
        in_maps.append(m)
    res = run_bass_kernel_spmd(nc, in_maps, core_ids=list(range(B)))
    return np.stack([np.asarray(r["out"]) for r in res.results], 0).astype(np.float32)
```

```python
import numpy as np
import ml_dtypes
from contextlib import ExitStack
import concourse.bass as bass
import concourse.mybir as mybir
from concourse.bass_utils import run_bass_kernel_spmd

F32 = mybir.dt.float32
BF16 = mybir.dt.bfloat16
AF = mybir.ActivationFunctionType
ALU = mybir.AluOpType

D = 2048
KC = 16
SEQ = 4096
CTX = 256
TT = SEQ + CTX
NKT = TT // 128
L = 2
DFF = 5632
FC = 44
GRID_W = 64
ALPHA = (2 * L) ** 0.25
EPS = 1e-6
NEG = -1e30
SC128 = 128 ** -0.5
SC192 = 192 ** -0.5
NCORES = 4

QA, KA, QB, KB, QD, KD, QNC, QPEC, KNC, KPEC = 0, 4, 8, 12, 14, 18, 20, 24, 26, 30
NFM = 31
VA, VB, VD, VC = 0, 4, 6, 8
NVT = 12
H2W = SEQ + CTX + 4
H2_CTX0 = SEQ + 2

VEC = {}
_o = 0
for _l in range(L):
    for _n, _w in (("b_ada", 96), ("ln1_g", 16), ("ln1_b", 16), ("ln2_g", 16), ("ln2_b", 16),
                   ("conv_w", FC * 3), ("conv_b", FC), ("gqa_qn", 1), ("gqa_kn", 1),
                   ("mla_qn", 3), ("mla_kvn", 1), ("sink", 4)):
        VEC[(_n, _l)] = (_o, _w)
        _o += _w
NVEC = _o


class Buf:
    __slots__ = ("lw", "rd")

    def __init__(self):
        self.lw = None
        self.rd = {}


class Tile:
    def __init__(self, t, nsub=0):
        self.t = t
        self.b = Buf()
        self.sub = [Tile(t) for _ in range(nsub)]

    def __getitem__(self, idx):
        return self.t[idx]


class Ring:
    def __init__(self, tiles):
        self.tiles = tiles
        self.i = 0

    def next(self):
        t = self.tiles[self.i % len(self.tiles)]
        self.i += 1
        return t


class Prog:
    NDMA = 8
    LIM = 28000

    def __init__(self, nc):
        self.nc = nc
        self.eng = {"pe": nc.tensor, "act": nc.scalar, "dve": nc.vector,
                    "pool": nc.gpsimd, "sp": nc.sync}
        self.sem = {}
        self.cnt = {}
        self.cur = {}
        self.nsem = 0
        for k in ("pe", "act", "dve", "pool"):
            self._fresh(k)
        self.dqn = {"sp": 0, "pool": 0, "act": 0}
        for q in self.dqn:
            for i in range(self.NDMA):
                self._fresh(("dma", q, i))
        self.seen = {k: {} for k in self.eng}
        self.ninst = 0
        self.nwait = 0
        self.cvn = 0

    def _fresh(self, base):
        ep = self.cur.get(base, (None, -1))[1] + 1
        key = (base, ep)
        self.cur[base] = key
        self.sem[key] = self.nc.alloc_semaphore("s%d" % self.nsem)
        self.nsem += 1
        self.cnt[key] = 0
        return key

    def _wait(self, e, ev):
        if ev is None:
            return
        key, c = ev
        if key[0] == e and e == "pe":
            return
        if self.seen[e].get(key, 0) >= c:
            return
        self.seen[e][key] = c
        val = c * 16 if isinstance(key[0], tuple) else c
        self.eng[e].wait_ge(self.sem[key], val)
        self.nwait += 1

    def _deps(self, e, reads, writes):
        for t in reads:
            self._wait(e, t.b.lw)
        for t in writes:
            self._wait(e, t.b.lw)
            for k, c in t.b.rd.items():
                self._wait(e, (k, c))

    def _commit(self, ev, reads, writes):
        k, c = ev
        for t in reads:
            t.b.rd[k] = c
        for t in writes:
            t.b.lw = ev
            t.b.rd = {}

    def op(self, e, fn, reads=(), writes=()):
        key = self.cur[e]
        if self.cnt[key] >= self.LIM:
            key = self._fresh(e)
        self._deps(e, reads, writes)
        ins = fn(self.eng[e])
        self.cnt[key] += 1
        ins.then_inc(self.sem[key], 1)
        self._commit((key, self.cnt[key]), reads, writes)
        self.ninst += 1
        return ins

    def dma(self, q, out, in_, reads=(), writes=(), **kw):
        i = self.dqn[q] % self.NDMA
        self.dqn[q] += 1
        base = ("dma", q, i)
        key = self.cur[base]
        if self.cnt[key] > 0:
            self._wait(q, (key, self.cnt[key]))
        if self.cnt[key] * 16 >= self.LIM:
            key = self._fresh(base)
        self._deps(q, reads, writes)
        ins = self.eng[q].dma_start(out=out, in_=in_, **kw)
        self.cnt[key] += 1
        ins.then_inc(self.sem[key], 16)
        self._commit((key, self.cnt[key]), reads, writes)
        self.ninst += 1
        return ins

    def cvt(self, out, in_, tile):
        key = (("cv", self.cvn), 0)
        self.cvn += 1
        self.sem[key] = self.nc.alloc_semaphore("cv%d" % self.cvn)
        self.nsem += 1
        ins = self.nc.gpsimd.dma_start(out=out, in_=in_)
        ins.then_inc(self.sem[key], 16)
        self.cnt[key] = 1
        tile.b.lw = (key, 1)
        self.ninst += 1

    def barrier(self):
        for e in self.eng:
            for key, c in list(self.cnt.items()):
                if c > 0 and key[0] != e and not (isinstance(key[0], tuple) and key[0][0] == "cv"):
                    self._wait(e, (key, c))


def _rope_tables():
    t = np.arange(SEQ)
    row, col = (t // GRID_W).astype(np.float64), (t % GRID_W).astype(np.float64)

    def tab(dim):
        seg = dim // 2
        half = seg // 2
        C = np.ones((dim, TT))
        S = np.zeros((dim, TT))
        perm = np.zeros(dim, np.int64)
        inv = 10000.0 ** (-np.arange(half, dtype=np.float32) / half)
        for d in range(dim):
            s, e = d // seg, d % seg
            i = e % half
            first = e < half
            pos = row if s == 0 else col
            ang = (pos.astype(np.float32) * np.float32(inv[i])).astype(np.float32)
            C[d, :SEQ] = np.cos(ang)
            S[d, :SEQ] = -np.sin(ang) if first else np.sin(ang)
            perm[d] = d + half if first else d - half
        return C.astype(np.float32), S.astype(np.float32), perm

    C128, S128, p128 = tab(128)
    C64, S64, p64 = tab(64)
    rope128 = np.stack([C128, S128])
    rope64 = np.stack([np.concatenate([C64, C64], 0), np.concatenate([S64, S64], 0)])
    Pm128 = np.zeros((128, 128), np.float32)
    Pm128[p128, np.arange(128)] = 1.0
    Pm64 = np.zeros((128, 128), np.float32)
    for hh in range(2):
        Pm64[hh * 64 + p64, hh * 64 + np.arange(64)] = 1.0
    return rope128, rope64, np.stack([Pm128, Pm64])


def _amask():
    p = np.arange(128)
    a, kcol = p // 64, p % 64
    m = np.arange(14)
    j = 6 - m
    dr = j[None, :] + a[:, None]
    qcol = np.arange(64)
    c0 = np.clip(qcol - 8, 0, 48)
    col_in = (kcol[:, None] >= c0[None, :]) & (kcol[:, None] < c0[None, :] + 16)
    v_edge = np.broadcast_to(col_in[:, None, :], (128, 14, 64))
    v_gen = v_edge & ((dr >= -4) & (dr <= 3))[:, :, None]
    out = []
    for v in (v_gen, v_edge):
        out.append((v / SC128).astype(np.float32).reshape(128, 14 * 64))
        out.append(np.where(v, 0.0, NEG).astype(np.float32).reshape(128, 14 * 64))
    return np.stack(out)


def _gtab(rpb):
    p = np.arange(128)
    a, kcol = p // 64, p % 64
    j = 6 - np.arange(14)
    dr = j[None, :] + a[:, None]
    qcol = np.arange(64)
    dc = np.clip(kcol[:, None] - qcol[None, :] + 15, 0, 30)
    g = rpb[:, :, (dr + 7)[:, :, None], dc[:, None, :]]
    return np.ascontiguousarray(g.reshape(L, 4, 128, 14 * 64))


def _chunk_w(w, kc):
    Lw, K, N = w.shape
    return np.ascontiguousarray(w.reshape(Lw, kc, 128, N // 128, 128).transpose(0, 3, 2, 1, 4).reshape(Lw, N // 128, 128, kc * 128))


_CONST_CACHE = {}


def _consts():
    if not _CONST_CACHE:
        rope128, rope64, perm = _rope_tables()
        k = np.arange(128)
        tri = np.stack([np.where(k[:, None] >= k[None, :], 0.0, NEG), np.where(k[:, None] <= k[None, :], 0.0, NEG)], 1)
        _CONST_CACHE.update(rope128=rope128, rope64=rope64, perm=perm, amask=_amask(),
                            identf=np.eye(128, dtype=np.float32),
                            identb=np.eye(128, dtype=np.float32).astype(ml_dtypes.bfloat16),
                            trimask=np.ascontiguousarray(tri.reshape(128, 256).astype(np.float32)))
    return _CONST_CACHE


def prep_shared(inp):
    sh = {}
    w_in = inp["w_in"]
    sizes = [512, 512, 512, 512, 256, 256, 384, 128, 64, 512, 256, 256]
    offs = np.cumsum([0] + sizes)
    seg = lambda i: w_in[:, :, offs[i]:offs[i + 1]]
    w_in_r = np.concatenate([seg(0), seg(1), seg(2), seg(3), seg(4), seg(5), seg(6), seg(7), seg(9), seg(10), seg(11),
                             seg(8), np.zeros((L, D, 64), np.float32)], axis=2)
    sh["w_in"] = _chunk_w(w_in_r, KC)
    sh["w_out"] = _chunk_w(inp["w_out"], KC)
    sh["w_gate"] = _chunk_w(inp["ffn_w_gate"], KC)
    sh["w_up"] = _chunk_w(inp["ffn_w_up"], KC)
    sh["w_down"] = _chunk_w(inp["ffn_w_down"], FC)
    uq = inp["mla_w_uq"].reshape(L, 384, 4, 192)
    uq = np.concatenate([uq[..., :128].reshape(L, 384, 512), uq[..., 128:].reshape(L, 384, 256)], -1)
    sh["w_uq"] = np.ascontiguousarray(uq.reshape(L, 3, 128, 768).transpose(0, 2, 1, 3).reshape(L, 128, 3 * 768))
    ukv = inp["mla_w_ukv"].reshape(L, 128, 4, 256)
    sh["w_ukv"] = np.ascontiguousarray(np.concatenate([ukv[..., :128].reshape(L, 128, 512), ukv[..., 128:].reshape(L, 128, 512)], -1))
    sh["w_ada"] = np.ascontiguousarray(inp["w_ada"].reshape(L * D, 6 * D))
    vec = np.zeros((128, NVEC), np.float32)

    def put(name, l, arr):
        o, w = VEC[(name, l)]
        vec[:, o:o + w] = arr

    for l in range(L):
        put("b_ada", l, inp["b_ada"][l].reshape(96, 128).T)
        for n in ("ln1_g", "ln1_b", "ln2_g", "ln2_b"):
            put(n, l, inp[n][l].reshape(16, 128).T)
        put("conv_w", l, inp["ffn_conv_w"][l].reshape(3, FC, 128).transpose(2, 1, 0).reshape(128, FC * 3))
        put("conv_b", l, inp["ffn_conv_b"][l].reshape(FC, 128).T)
        put("gqa_qn", l, inp["gqa_q_norm"][l].reshape(128, 1))
        put("gqa_kn", l, inp["gqa_k_norm"][l].reshape(128, 1))
        put("mla_qn", l, inp["mla_q_norm"][l].reshape(3, 128).T)
        put("mla_kvn", l, inp["mla_kv_norm"][l].reshape(128, 1))
        put("sink", l, np.broadcast_to(inp["swa_sink"][l][None, :], (128, 4)))
    sh["vecs"] = vec
    sh["gtab"] = _gtab(inp["na_rpb"])
    sh.update(_consts())
    return sh


def build(dbg=(), stop_after=None):
    nc = bass.Bass("TRN2", target_bir_lowering=False)
    P = Prog(nc)
    es = ExitStack()

    def din(name, shape, dt=F32):
        return nc.dram_tensor(name, list(shape), dt, kind="ExternalInput").ap()

    def dscr(name, shape, dt):
        kind = "ExternalOutput" if name in dbg else "Internal"
        return nc.dram_tensor(name, list(shape), dt, kind=kind).ap()

    xin = din("xin", [TT, D])
    cvec = din("cvec", [128, 32])
    w_ada = din("w_ada", [L * D, 6 * D])
    vecs_d = din("vecs", [128, NVEC])
    w_in_d = din("w_in", [L, 33, 128, 2048])
    w_out_d = din("w_out", [L, 16, 128, 2048])
    w_gate_d = din("w_gate", [L, FC, 128, 2048])
    w_up_d = din("w_up", [L, FC, 128, 2048])
    w_down_d = din("w_down", [L, 16, 128, DFF])
    w_uq_d = din("w_uq", [L, 128, 3 * 768])
    w_ukv_d = din("w_ukv", [L, 128, 1024])
    gtab_d = din("gtab", [L, 4, 128, 14 * 64])
    rope128_d = din("rope128", [2, 128, TT])
    rope64_d = din("rope64", [2, 128, TT])
    perm_d = din("perm", [2, 128, 128])
    amask_d = din("amask", [4, 128, 14 * 64])
    identf_d = din("identf", [128, 128])
    identb_d = din("identb", [128, 128], BF16)
    trimask_d = din("trimask", [128, 256])
    out_d = nc.dram_tensor("out", [SEQ, D], F32, kind="ExternalOutput").ap()

    wb = {}
    wsrc = {"w_in": w_in_d, "w_out": w_out_d, "w_gate": w_gate_d, "w_up": w_up_d, "w_down": w_down_d,
            "w_uq": w_uq_d, "w_ukv": w_ukv_d}
    for n, src in wsrc.items():
        wb[n] = nc.dram_tensor(n + "_bf", list(src.shape), BF16, kind="Internal").ap()
    wbt = {(n, l): Tile(None) for n in wsrc for l in range(L)}

    xT = dscr("xT", [KC, 128, TT], F32)
    hT = dscr("hT", [KC, 128, TT], BF16)
    h2T = dscr("h2T", [KC, 128, H2W], BF16)
    fmT = dscr("fmT", [NFM, 128, TT], BF16)
    vtok = dscr("vtok", [NVT, TT, 128], BF16)
    mixT = dscr("mixT", [KC, 128, TT], BF16)

    uid = [0]

    def sb(name, shape, dt, stack=None, nsub=0):
        uid[0] += 1
        t = (stack or es).enter_context(nc.sbuf_tensor("sb%d_%s" % (uid[0], name), list(shape), dt))
        return Tile(t, nsub)

    def ring(name, n, shape, dt, stack=None, nsub=0):
        return Ring([sb("%s%d" % (name, i), shape, dt, stack, nsub) for i in range(n)])

    PS = [Tile(es.enter_context(nc.psum_tensor("ps%d" % i, [128, 512], F32))) for i in range(8)]

    def flat2k(ap):
        n = 1
        for s in ap.shape:
            n *= s
        names = " ".join("a%d" % i for i in range(len(ap.shape)))
        return ap.rearrange("%s -> (%s)" % (names, names)).rearrange("(r c) -> r c", c=2048)

    def convert_weights(l):
        for n in ("w_in", "w_uq", "w_ukv", "w_out", "w_gate", "w_up", "w_down"):
            P.cvt(flat2k(wb[n][l]), flat2k(wsrc[n][l]), wbt[(n, l)])

    convert_weights(0)

    identf = sb("identf", [128, 128], F32)
    identb = sb("identb", [128, 128], BF16)
    perm = sb("perm", [128, 2, 128], F32)
    trimask = sb("trimask", [128, 256], F32)
    vecs = sb("vecs", [128, NVEC], F32)
    onesb = sb("onesb", [128, 128], BF16)
    ones128 = sb("ones128", [128, 128], BF16)
    ones384 = sb("ones384", [128, 128], BF16)
    modv = sb("modv", [128, L, 96, 2], F32)
    esink = sb("esink", [128, L * 4], F32)
    P.dma("sp", identf[:], identf_d, writes=[identf])
    P.dma("sp", identb[:], identb_d, writes=[identb])
    P.dma("sp", perm[:], perm_d.rearrange("a p c -> p a c"), writes=[perm])
    P.dma("sp", trimask[:], trimask_d, writes=[trimask])
    P.dma("sp", vecs[:], vecs_d, writes=[vecs])
    P.op("dve", lambda e: e.memset(onesb[:], 1.0 / D), writes=[onesb])
    P.op("dve", lambda e: e.memset(ones128[:], 1.0 / 128), writes=[ones128])
    P.op("dve", lambda e: e.memset(ones384[:], 1.0 / 384), writes=[ones384])

    zcol = sb("zcol", [128, KC, 1], BF16)
    P.op("dve", lambda e: e.memset(zcol[:], 0.0), writes=[zcol])
    for c in (0, SEQ + 1, SEQ + 2, H2W - 1):
        P.dma("sp", h2T[:, :, c:c + 1].rearrange("kc p t -> p kc t"), zcol[:], reads=[zcol], allow_slow_non_contiguous=True)

    def vcol(name, l, j=0, w=1):
        o, _ = VEC[(name, l)]
        return vecs[:, o + j:o + j + w]

    def mod(l, s, kc, which):
        return modv[:, l, s * 16 + kc, which:which + 1]

    def ln_stats(zt, N, eps, pools, psM, psV):
        sqp, zbp, stp = pools
        for kc in range(KC):
            sq = sqp.next()
            P.op("act", lambda e: e.activation(out=sq[:, :N], in_=zt[:, kc, :N], func=AF.Square), reads=[zt.sub[kc]], writes=[sq])
            zb = zbp.next()
            P.op("pool" if kc % 2 == 0 else "dve", lambda e: e.tensor_copy(out=zb[:, :N], in_=zt[:, kc, :N]), reads=[zt.sub[kc]], writes=[zb])
            P.op("pe", lambda e: e.matmul(psM[:, :N], lhsT=onesb[:], rhs=zb[:, :N], start=(kc == 0), stop=(kc == KC - 1)),
                 reads=[onesb, zb], writes=[psM])
            P.op("pe", lambda e: e.matmul(psV[:, :N], lhsT=onesb[:], rhs=sq[:, :N], start=(kc == 0), stop=(kc == KC - 1)),
                 reads=[onesb, sq], writes=[psV])
        m2 = stp.next()
        P.op("act", lambda e: e.activation(out=m2[:, :N], in_=psM[:, :N], func=AF.Square), reads=[psM], writes=[m2])
        P.op("dve", lambda e: e.tensor_tensor(out=m2[:, :N], in0=psV[:, :N], in1=m2[:, :N], op=ALU.subtract), reads=[psV, m2], writes=[m2])
        P.op("dve", lambda e: e.tensor_scalar(out=m2[:, :N], in0=m2[:, :N], scalar1=float(eps), scalar2=None, op0=ALU.add), reads=[m2], writes=[m2])
        P.op("act", lambda e: e.activation(out=m2[:, :N], in_=m2[:, :N], func=AF.Sqrt), reads=[m2], writes=[m2])
        P.op("dve", lambda e: e.reciprocal(out=psV[:, :N], in_=m2[:, :N]), reads=[m2], writes=[psV])
        return psM, psV

    def ln_apply(zt, kc, N, Mt, Rt, scale_ap, bias_ap, out_ap, out_tile, tp, extra_reads=()):
        t = tp.next()
        P.op("dve", lambda e: e.tensor_tensor(out=t[:, :N], in0=zt[:, kc, :N], in1=Mt[:, :N], op=ALU.subtract), reads=[zt.sub[kc], Mt], writes=[t])
        P.op("dve", lambda e: e.tensor_tensor(out=t[:, :N], in0=t[:, :N], in1=Rt[:, :N], op=ALU.mult), reads=[t, Rt], writes=[t])
        P.op("act", lambda e: e.activation(out=out_ap, in_=t[:, :N], func=AF.Identity, scale=scale_ap, bias=bias_ap),
             reads=[t, vecs, modv] + list(extra_reads), writes=[out_tile])

    groups = [(g * 512, 512, 0) for g in range(8)] + [(SEQ, CTX, 1)]

    def h2col(t0, which):
        return (t0 if which == 0 else H2_CTX0)

    with ExitStack() as st:
        cv = sb("cv", [128, 32], F32, st)
        scv = sb("scv", [128, 32], F32, st)
        wblk = ring("wblk", 2, [128, KC, 512], F32, st)
        P.dma("sp", cv[:], cvec, writes=[cv])
        P.op("act", lambda e: e.activation(out=scv[:], in_=cv[:], func=AF.Silu), reads=[cv], writes=[scv])
        mrr = ring("mrow", 2, [2, 512], F32, st)
        rowr = Ring([PS[2], PS[3]])
        for l in range(L):
            psA = PS[l]
            psv = psA[:, 0:192].rearrange("p (j w) -> p j w", w=2)
            wl = w_ada[l * D:(l + 1) * D, :].rearrange("(kc p) n -> p kc n", p=128)
            for blk in range(24):
                wt = wblk.next()
                P.dma("sp", wt[:], wl[:, :, blk * 512:(blk + 1) * 512], writes=[wt])
                prow = rowr.next()
                for kc in range(KC):
                    P.op("pe", lambda e: e.matmul(prow[0:2, :], lhsT=scv[:, 2 * kc:2 * kc + 2], rhs=wt[:, kc, :], start=(kc == 0), stop=(kc == KC - 1)),
                         reads=[wt, scv], writes=[prow])
                mr = mrr.next()
                P.op("act", lambda e: e.activation(out=mr[0:2, :], in_=prow[0:2, :], func=AF.Copy), reads=[prow], writes=[mr])
                for j in range(4):
                    P.op("pe", lambda e: e.transpose(out=psv[:, blk * 4 + j, :], in_=mr[0:2, j * 128:(j + 1) * 128], identity=identf[0:2, 0:2]),
                         reads=[mr, identf], writes=[psA])
            o, w = VEC[("b_ada", l)]
            for which in range(2):
                P.op("dve", lambda e: e.tensor_tensor(out=modv[:, l, :, which], in0=psv[:, :, which], in1=vecs[:, o:o + 96], op=ALU.add),
                     reads=[psA, vecs], writes=[modv])
            for s in (1, 4):
                P.op("dve", lambda e: e.tensor_scalar(out=modv[:, l, s * 16:(s + 1) * 16, :], in0=modv[:, l, s * 16:(s + 1) * 16, :],
                                                      scalar1=1.0, scalar2=None, op0=ALU.add), reads=[modv], writes=[modv])
            for s in (2, 5):
                P.op("dve", lambda e: e.tensor_scalar(out=modv[:, l, s * 16:(s + 1) * 16, :], in0=modv[:, l, s * 16:(s + 1) * 16, :],
                                                      scalar1=1.0 / ALPHA, scalar2=None, op0=ALU.mult), reads=[modv], writes=[modv])
            o, w = VEC[("sink", l)]
            P.op("act", lambda e: e.activation(out=esink[:, l * 4:(l + 1) * 4], in_=vecs[:, o:o + 4], func=AF.Exp), reads=[vecs], writes=[esink])
        P.barrier()
    if "modv" in dbg:
        modv_o = nc.dram_tensor("modv_o", [128, L * 96 * 2], F32, kind="ExternalOutput").ap()
        P.dma("sp", modv_o, modv[:].rearrange("p l j w -> p (l j w)"), reads=[modv])

    def finish():
        P.barrier()
        es.close()
        return nc, P

    if stop_after == "ada":
        return finish()

    with ExitStack() as st:
        xr = ring("xin", 2, [128, 4, D], F32, st)
        ztr = ring("zt0", 2, [128, KC, 512], F32, st, nsub=KC)
        hbr = ring("hb0", 2, [128, KC, 512], BF16, st, nsub=KC)
        pools = (ring("sq0", 3, [128, 512], BF16, st), ring("zb0", 3, [128, 512], BF16, st), ring("st0", 6, [128, 512], F32, st))
        tp = ring("t0", 3, [128, 512], F32, st)
        psr = Ring([PS[0], PS[1], PS[2], PS[3]])
        stat_sets = Ring([(PS[4], PS[5]), (PS[6], PS[7])])
        for (t0, N, which) in groups:
            nt = N // 128
            xt = xr.next()
            P.dma("sp", xt[:, :nt, :], xin[t0:t0 + N, :].rearrange("(a p) d -> p a d", p=128), writes=[xt])
            zt = ztr.next()
            for kc in range(KC):
                ps = psr.next()
                for a in range(nt):
                    P.op("pe", lambda e: e.transpose(out=ps[:, a * 128:(a + 1) * 128], in_=xt[:, a, kc * 128:(kc + 1) * 128], identity=identf[:]),
                         reads=[xt, identf], writes=[ps])
                eng = "act" if kc % 2 == 0 else "dve"
                if eng == "act":
                    P.op("act", lambda e: e.activation(out=zt[:, kc, :N], in_=ps[:, :N], func=AF.Copy), reads=[ps], writes=[zt.sub[kc]])
                else:
                    P.op("dve", lambda e: e.tensor_copy(out=zt[:, kc, :N], in_=ps[:, :N]), reads=[ps], writes=[zt.sub[kc]])
            P.dma("pool", xT[:, :, t0:t0 + N].rearrange("kc p t -> p kc t"), zt[:, :, :N], reads=zt.sub)
            bM, bV = stat_sets.next()
            Mt, Rt = ln_stats(zt, N, EPS, pools, bM, bV)
            hb = hbr.next()
            for kc in range(KC):
                ln_apply(zt, kc, N, Mt, Rt, mod(0, 1, kc, which), mod(0, 0, kc, which), hb[:, kc, :N], hb.sub[kc], tp)
            P.dma("pool", hT[:, :, t0:t0 + N].rearrange("kc p t -> p kc t"), hb[:, :, :N], reads=hb.sub)
        P.barrier()
    if stop_after == "p0":
        return finish()


    def phase1(l):
        with ExitStack() as st:
            hr = ring("h1_", 2, [128, KC, 512], BF16, st)
            wr = ring("w1_", 6, [128, KC, 128], BF16, st)
            wv = sb("wv1", [128, 8, KC, 128], BF16, st)
            wuq = sb("wuq", [128, 3, 768], BF16, st)
            wukv = sb("wukv", [128, 1024], BF16, st)
            rc128 = ring("rc128_", 2, [128, 2, 512], F32, st)
            rc64 = ring("rc64_", 2, [128, 2, 512], F32, st)
            xsr = ring("xs1_", 4, [128, 512], F32, st)
            t1r = ring("t1_", 6, [128, 512], F32, st)
            obr = ring("ob1_", 6, [128, 512], BF16, st)
            sqr = ring("sq1_", 3, [128, 512], BF16, st)
            rsr = ring("rs1_", 4, [128, 512], F32, st)
            vobr = ring("vob1_", 3, [128, 512], BF16, st)
            cqs = sb("cqs", [128, 3, 512], F32, st)
            cqn = sb("cqn", [128, 3, 512], BF16, st)
            ckvn = sb("ckvn", [128, 512], BF16, st)
            mainr = Ring([PS[0], PS[1], PS[2]])
            ppr = Ring([PS[3], PS[4]])
            psR = PS[5]
            vr = Ring([PS[6], PS[7]])
            wt_in = wbt[("w_in", l)]
            wsrc_l = w_in_bf = wb["w_in"]

            def wchunks(c0, n):
                return wb["w_in"][l, c0:c0 + n].rearrange("c p (kc n) -> p c kc n", kc=KC)

            P.dma("sp", wv[:, 0:4], wchunks(8, 4), reads=[wt_in], writes=[wv])
            P.dma("sp", wv[:, 4:6], wchunks(18, 2), reads=[wt_in], writes=[wv])
            P.dma("sp", wv[:, 6:8], wchunks(30, 2), reads=[wt_in], writes=[wv])
            P.dma("sp", wuq[:], wb["w_uq"][l].rearrange("p (kc n) -> p kc n", kc=3), reads=[wbt[("w_uq", l)]], writes=[wuq])
            P.dma("sp", wukv[:], wb["w_ukv"][l], reads=[wbt[("w_ukv", l)]], writes=[wukv])
            cnt = [0]

            for (t0, N, which) in groups:
                skipq = (l == L - 1 and which == 1)
                nt = N // 128
                ht = hr.next()
                P.dma("sp", ht[:, :, :N], hT[:, :, t0:t0 + N].rearrange("kc p t -> p kc t"), writes=[ht])
                r128 = rc128.next()
                P.dma("sp", r128[:, :, :N], rope128_d[:, :, t0:t0 + N].rearrange("a p t -> p a t"), writes=[r128])
                r64 = rc64.next()
                P.dma("sp", r64[:, :, :N], rope64_d[:, :, t0:t0 + N].rearrange("a p t -> p a t"), writes=[r64])

                def proj(cc, M=128):
                    w = wr.next()
                    P.dma("sp", w[:], wb["w_in"][l, cc].rearrange("p (kc n) -> p kc n", kc=KC), reads=[wt_in], writes=[w])
                    ps = mainr.next()
                    for kc in range(KC):
                        P.op("pe", lambda e: e.matmul(ps[0:M, :N], lhsT=w[:, kc, 0:M], rhs=ht[:, kc, :N], start=(kc == 0), stop=(kc == KC - 1)),
                             reads=[w, ht], writes=[ps])
                    return ps

                def copy_out(dst_ap, dst_tile, src_ap, src_tile):
                    cnt[0] += 1
                    if cnt[0] % 2 == 0:
                        P.op("act", lambda e: e.activation(out=dst_ap, in_=src_ap, func=AF.Copy), reads=[src_tile], writes=[dst_tile])
                    else:
                        P.op("dve", lambda e: e.tensor_copy(out=dst_ap, in_=src_ap), reads=[src_tile], writes=[dst_tile])

                def evac_bf(ps, M=128):
                    ob = obr.next()
                    copy_out(ob[:M, :N], ob, ps[:M, :N], ps)
                    return ob

                def store_fm(idx, ob, M=128, prow=0):
                    P.dma("pool", fmT[idx, prow:prow + M, t0:t0 + N], ob[:M, :N], reads=[ob])

                def rope(xs, M, tab, pmi):
                    pp = ppr.next()
                    P.op("pe", lambda e: e.matmul(pp[:M, :N], lhsT=perm[:M, pmi, :M], rhs=xs[:M, :N], start=True, stop=True),
                         reads=[perm, xs], writes=[pp])
                    t1 = t1r.next()
                    P.op("pool", lambda e: e.tensor_tensor(out=t1[:M, :N], in0=xs[:M, :N], in1=tab[:M, 0, :N], op=ALU.mult), reads=[xs, tab], writes=[t1])
                    t2 = t1r.next()
                    P.op("dve", lambda e: e.tensor_tensor(out=t2[:M, :N], in0=pp[:M, :N], in1=tab[:M, 1, :N], op=ALU.mult), reads=[pp, tab], writes=[t2])
                    ob = obr.next()
                    P.op("dve", lambda e: e.tensor_tensor(out=ob[:M, :N], in0=t1[:M, :N], in1=t2[:M, :N], op=ALU.add), reads=[t1, t2], writes=[ob])
                    return ob

                def rstd_from(psr_tile):
                    m = rsr.next()
                    P.op("dve", lambda e: e.tensor_scalar(out=m[:, :N], in0=psr_tile[:, :N], scalar1=EPS, scalar2=None, op0=ALU.add), reads=[psr_tile], writes=[m])
                    P.op("act", lambda e: e.activation(out=m[:, :N], in_=m[:, :N], func=AF.Sqrt), reads=[m], writes=[m])
                    P.op("dve", lambda e: e.reciprocal(out=m[:, :N], in_=m[:, :N]), reads=[m], writes=[m])
                    return m

                jobs = []
                mcq = []

                def post_plain(idx):
                    return lambda ps: store_fm(idx, evac_bf(ps))

                def post_rope(idx):
                    def f(ps):
                        xs = xsr.next()
                        copy_out(xs[:, :N], xs, ps[:, :N], ps)
                        store_fm(idx, rope(xs, 128, r128, 0))
                    return f

                def post_norm_rope(idx, gname):
                    def f(ps):
                        sq = sqr.next()
                        P.op("act", lambda e: e.activation(out=sq[:, :N], in_=ps[:, :N], func=AF.Square), reads=[ps], writes=[sq])
                        P.op("pe", lambda e: e.matmul(psR[:, :N], lhsT=ones128[:], rhs=sq[:, :N], start=True, stop=True), reads=[ones128, sq], writes=[psR])
                        m = rstd_from(psR)
                        xs = xsr.next()
                        P.op("dve", lambda e: e.scalar_tensor_tensor(out=xs[:, :N], in0=ps[:, :N], scalar=vcol(gname, l), in1=m[:, :N], op0=ALU.mult, op1=ALU.mult),
                             reads=[ps, m, vecs], writes=[xs])
                        store_fm(idx, rope(xs, 128, r128, 0))
                    return f

                def post_cq(i):
                    def f(ps):
                        P.op("act", lambda e: e.activation(out=cqs[:, i, :N], in_=ps[:, :N], func=AF.Copy), reads=[ps], writes=[cqs])
                        sq = sqr.next()
                        P.op("act", lambda e: e.activation(out=sq[:, :N], in_=ps[:, :N], func=AF.Square), reads=[ps], writes=[sq])
                        P.op("pe", lambda e: e.matmul(psR[:, :N], lhsT=ones384[:], rhs=sq[:, :N], start=(i == 0), stop=(i == 2)), reads=[ones384, sq], writes=[psR])
                        if i == 2:
                            mcq.append(rstd_from(psR))
                    return f

                def mla_q_tail(_):
                    m = mcq.pop()
                    for i in range(3):
                        P.op("dve", lambda e: e.scalar_tensor_tensor(out=cqn[:, i, :N], in0=cqs[:, i, :N], scalar=vcol("mla_qn", l, i), in1=m[:, :N],
                                                                    op0=ALU.mult, op1=ALU.mult), reads=[cqs, m, vecs], writes=[cqn])
                    for h in range(4):
                        ps = vr.next()
                        for i in range(3):
                            P.op("pe", lambda e: e.matmul(ps[:, :N], lhsT=wuq[:, i, h * 128:(h + 1) * 128], rhs=cqn[:, i, :N], start=(i == 0), stop=(i == 2)),
                                 reads=[wuq, cqn], writes=[ps])
                        store_fm(QNC + h, evac_bf(ps))
                    for j in range(2):
                        ps = vr.next()
                        for i in range(3):
                            P.op("pe", lambda e: e.matmul(ps[:, :N], lhsT=wuq[:, i, 512 + j * 128:512 + (j + 1) * 128], rhs=cqn[:, i, :N], start=(i == 0), stop=(i == 2)),
                                 reads=[wuq, cqn], writes=[ps])
                        xs = xsr.next()
                        copy_out(xs[:, :N], xs, ps[:, :N], ps)
                        store_fm(QPEC + j, rope(xs, 128, r64, 1))

                def post_ckv(ps):
                    xs = xsr.next()
                    P.op("act", lambda e: e.activation(out=xs[:, :N], in_=ps[:, :N], func=AF.Copy), reads=[ps], writes=[xs])
                    sq = sqr.next()
                    P.op("act", lambda e: e.activation(out=sq[:, :N], in_=ps[:, :N], func=AF.Square), reads=[ps], writes=[sq])
                    P.op("pe", lambda e: e.matmul(psR[:, :N], lhsT=ones128[:], rhs=sq[:, :N], start=True, stop=True), reads=[ones128, sq], writes=[psR])
                    m = rstd_from(psR)
                    P.op("dve", lambda e: e.scalar_tensor_tensor(out=ckvn[:, :N], in0=xs[:, :N], scalar=vcol("mla_kvn", l), in1=m[:, :N], op0=ALU.mult, op1=ALU.mult),
                         reads=[xs, m, vecs], writes=[ckvn])

                def mla_kv_tail(_):
                    for h in range(4):
                        ps = vr.next()
                        P.op("pe", lambda e: e.matmul(ps[:, :N], lhsT=wukv[:, h * 128:(h + 1) * 128], rhs=ckvn[:, :N], start=True, stop=True), reads=[wukv, ckvn], writes=[ps])
                        store_fm(KNC + h, evac_bf(ps))
                    for a in range(nt):
                        psv = vr.next()
                        P.op("pe", lambda e: e.matmul(psv[:, :], lhsT=ckvn[:, a * 128:(a + 1) * 128], rhs=wukv[:, 512:1024], start=True, stop=True), reads=[wukv, ckvn], writes=[psv])
                        vob = vobr.next()
                        copy_out(vob[:, :], vob, psv[:, :], psv)
                        P.dma("pool", vtok[VC:VC + 4, t0 + a * 128:t0 + (a + 1) * 128, :].rearrange("h t d -> t h d"),
                              vob[:, :].rearrange("t (h d) -> t h d", h=4), reads=[vob])

                def post_kpe(ps):
                    xs = xsr.next()
                    copy_out(xs[:64, :N], xs, ps[:64, :N], ps)
                    ob = rope(xs, 64, r64, 1)
                    store_fm(KPEC, ob, M=64, prow=0)
                    store_fm(KPEC, ob, M=64, prow=64)

                def addjob(cc, post, M=128):
                    jobs.append(((lambda cc=cc, M=M: proj(cc, M)), post))

                if not skipq:
                    for i in range(3):
                        addjob(20 + i, post_cq(i))
                addjob(23, post_ckv)
                if not skipq:
                    jobs.append((None, mla_q_tail))
                for cc in range(0, 8):
                    if not (skipq and cc < 4):
                        addjob(cc, post_plain(QA + cc))
                    if cc == 1:
                        jobs.append((None, mla_kv_tail))
                for i, cc in enumerate(range(12, 18)):
                    if not (skipq and i < 4):
                        addjob(cc, post_rope(QB + i))
                for i, cc in enumerate(range(24, 30)):
                    if not (skipq and i < 4):
                        addjob(cc, post_norm_rope(QD + i, "gqa_qn" if i < 4 else "gqa_kn"))
                addjob(32, post_kpe, M=64)
                prev = None
                for (pj, post) in jobs:
                    ps = pj() if pj is not None else None
                    if prev is not None:
                        prev[0](prev[1])
                    prev = (post, ps)
                prev[0](prev[1])
                for a in range(nt):
                    for blk in range(2):
                        psv = vr.next()
                        for kc in range(KC):
                            P.op("pe", lambda e: e.matmul(psv[:, :].rearrange("t (c n) -> t c n", c=4), lhsT=ht[:, kc, a * 128:(a + 1) * 128],
                                                          rhs=wv[:, blk * 4:(blk + 1) * 4, kc, :], start=(kc == 0), stop=(kc == KC - 1)),
                                 reads=[ht, wv], writes=[psv])
                        vob = vobr.next()
                        copy_out(vob[:, :], vob, psv[:, :], psv)
                        tsl = slice(t0 + a * 128, t0 + (a + 1) * 128)
                        if blk == 0:
                            P.dma("pool", vtok[VA:VA + 4, tsl, :].rearrange("h t d -> t h d"), vob[:, :].rearrange("t (h d) -> t h d", h=4), reads=[vob])
                        else:
                            P.dma("pool", vtok[VB:VB + 2, tsl, :].rearrange("h t d -> t h d"), vob[:, 0:256].rearrange("t (h d) -> t h d", h=2), reads=[vob])
                            P.dma("pool", vtok[VD:VD + 2, tsl, :].rearrange("h t d -> t h d"), vob[:, 256:512].rearrange("t (h d) -> t h d", h=2), reads=[vob])
            P.barrier()


    def phase2(l):
        need_ctx = l < L - 1
        if l + 1 < L:
            convert_weights(l + 1)
        qgroups = groups if need_ctx else groups[:8]
        with ExitStack() as st:
            ktr = ring("kt2_", 2, [128, TT], BF16, st)
            vtr = ring("vt2_", 2, [128, NKT, 130], BF16, st)
            kpe = sb("kpe2", [128, TT], BF16, st)
            qr = ring("q2_", 3, [128, 512], BF16, st)
            qpr = ring("qp2_", 3, [128, 512], BF16, st)
            ptr = ring("pt2_", 4, [128, 512], BF16, st)
            onr = ring("on2_", 4, [128, 128], BF16, st)
            mor = ring("mo2_", 3, [128, 512], BF16, st)
            rir = ring("ri2_", 8, [128, 1], F32, st)
            amask = sb("amask", [128, 4, 896], F32, st)
            gtr = ring("gt2_", 2, [128, 896], F32, st)
            ggr = ring("gg2_", 2, [128, 896], F32, st)
            ger = ring("ge2_", 2, [128, 896], F32, st)
            psT = PS[6]
            psT_bf = psT[:, :].bitcast(BF16)
            for vt in vtr.tiles:
                P.op("dve", lambda e: e.memset(vt[:, :, 128:129], 1.0), writes=[vt])
                P.op("dve", lambda e: e.memset(vt[:, :, 129:130], 0.0), writes=[vt])
            P.dma("sp", amask[:], amask_d.rearrange("a p c -> p a c"), writes=[amask])
            P.dma("sp", kpe[:], fmT[KPEC], writes=[kpe])

            def load_kv(kidx, vidx):
                kt = ktr.next()
                P.dma("sp", kt[:], fmT[kidx], writes=[kt])
                vt = vtr.next()
                P.dma("sp", vt[:, :, 0:128], vtok[vidx].rearrange("(kt p) d -> p kt d", p=128), writes=[vt])
                return kt, vt

            def load_q(idx, t0, N, r=None):
                q = (r or qr).next()
                P.dma("sp", q[:, :N], fmT[idx, :, t0:t0 + N], writes=[q])
                return q

            def fin_tile(O, col, sink_ap=None):
                ri = rir.next()
                if sink_ap is not None:
                    P.op("dve", lambda e: e.tensor_scalar(out=ri[:], in0=O[:, 128:129], scalar1=sink_ap, scalar2=None, op0=ALU.add), reads=[O, esink], writes=[ri])
                    P.op("dve", lambda e: e.reciprocal(out=ri[:], in_=ri[:]), reads=[ri], writes=[ri])
                else:
                    P.op("dve", lambda e: e.reciprocal(out=ri[:], in_=O[:, 128:129]), reads=[O], writes=[ri])
                on = onr.next()
                P.op("act", lambda e: e.activation(out=on[:], in_=O[:, 0:128], func=AF.Copy, scale=ri[:]), reads=[O, ri], writes=[on])
                P.op("pe", lambda e: e.transpose(out=psT_bf[:, col:col + 128], in_=on[:], identity=identb[:]), reads=[on, identb], writes=[psT])

            def fin_group(head, t0, N):
                mo = mor.next()
                P.op("dve", lambda e: e.tensor_copy(out=mo[:, :N], in_=psT_bf[:, :N]), reads=[psT], writes=[mo])
                P.dma("pool", mixT[head, :, t0:t0 + N], mo[:, :N], reads=[mo])

            sring = Ring([PS[0], PS[1], PS[7]])
            Od = [PS[2], PS[3], PS[4], PS[5]]

            def dense(ktiles, N, smm, scale, vt):
                nq = N // 128
                n = len(ktiles)
                sbank = [None] * n

                def issue_s(i):
                    sbank[i] = sring.next()
                    smm(ktiles[i], sbank[i])

                issue_s(0)
                if n > 1:
                    issue_s(1)
                for i, ktile in enumerate(ktiles):
                    s_ = sbank[i]
                    pt = ptr.next()
                    P.op("act", lambda e: e.activation(out=pt[:, :N], in_=s_[:, :N], func=AF.Exp, scale=scale), reads=[s_], writes=[pt])
                    if i + 2 < n:
                        issue_s(i + 2)
                    for a in range(nq):
                        P.op("pe", lambda e: e.matmul(Od[a][:, 0:130], lhsT=pt[:, a * 128:(a + 1) * 128], rhs=vt[:, ktile, :], start=(i == 0), stop=(i == n - 1)),
                             reads=[pt, vt], writes=[Od[a]])

            def run_dense(kind):
                heads = range(4)
                kt = vt = None
                for h in heads:
                    if kind == "D":
                        if h % 2 == 0:
                            kt, vt = load_kv(KD + h // 2, VD + h // 2)
                        head = 12 + h
                    else:
                        kt, vt = load_kv(KNC + h, VC + h)
                        head = 8 + h
                    for (t0, N, which) in qgroups:
                        ktiles = list(range(NKT)) if which == 0 else [32, 33]
                        if kind == "D":
                            q = load_q(QD + h, t0, N)

                            def smm(ktile, s_):
                                P.op("pe", lambda e: e.matmul(s_[:, :N], lhsT=kt[:, ktile * 128:(ktile + 1) * 128], rhs=q[:, :N], start=True, stop=True),
                                     reads=[kt, q], writes=[s_])
                            dense(ktiles, N, smm, SC128, vt)
                        else:
                            q = load_q(QNC + h, t0, N)
                            qp = load_q(QPEC + h // 2, t0, N, qpr)
                            lo = (h % 2) * 64

                            def smm(ktile, s_):
                                P.op("pe", lambda e: e.matmul(s_[:, :N], lhsT=kt[:, ktile * 128:(ktile + 1) * 128], rhs=q[:, :N], start=True, stop=False),
                                     reads=[kt, q], writes=[s_])
                                P.op("pe", lambda e: e.matmul(s_[:, :N], lhsT=kpe[lo:lo + 64, ktile * 128:(ktile + 1) * 128], rhs=qp[lo:lo + 64, :N], start=False, stop=True),
                                     reads=[kpe, qp], writes=[s_])
                            dense(ktiles, N, smm, SC192, vt)
                        for a in range(N // 128):
                            fin_tile(Od[a], a * 128)
                        fin_group(head, t0, N)

            ssets = Ring([(PS[0], PS[1]), (PS[2], PS[3])])
            Ow = Ring([PS[4], PS[5], PS[7]])

            wjobs = []

            def window_tile(kt, vt, qf, a, blocks, bias_fn, sink_ap, col, post=None):
                nb = len(blocks)
                nA = min(nb, 4)
                nB = nb - nA

                def stage1():
                    q = qf()
                    bA, bB = ssets.next()
                    for i, ktile in enumerate(blocks):
                        bank, c = (bA, i * 128) if i < 4 else (bB, (i - 4) * 128)
                        P.op("pe", lambda e: e.matmul(bank[:, c:c + 128], lhsT=kt[:, ktile * 128:(ktile + 1) * 128], rhs=q[:, a * 128:(a + 1) * 128], start=True, stop=True),
                             reads=[kt, q], writes=[bank])
                    bias_fn(bA, bB)
                    ptA = ptr.next()
                    P.op("act", lambda e: e.activation(out=ptA[:, :nA * 128], in_=bA[:, :nA * 128], func=AF.Exp, scale=SC128), reads=[bA], writes=[ptA])
                    ptB = None
                    if nB:
                        ptB = ptr.next()
                        P.op("act", lambda e: e.activation(out=ptB[:, :nB * 128], in_=bB[:, :nB * 128], func=AF.Exp, scale=SC128), reads=[bB], writes=[ptB])
                    return ptA, ptB

                def stage2(st_):
                    ptA, ptB = st_
                    O = Ow.next()
                    for i, ktile in enumerate(blocks):
                        pt, c = (ptA, i * 128) if i < 4 else (ptB, (i - 4) * 128)
                        P.op("pe", lambda e: e.matmul(O[:, 0:130], lhsT=pt[:, c:c + 128], rhs=vt[:, ktile, :], start=(i == 0), stop=(i == nb - 1)),
                             reads=[pt, vt], writes=[O])
                    fin_tile(O, col, sink_ap)
                    if post is not None:
                        post()

                wjobs.append((stage1, stage2))

            def run_wjobs():
                prev = None
                for (s1, s2) in wjobs:
                    st_ = s1()
                    if prev is not None:
                        prev[0](prev[1])
                    prev = (s2, st_)
                if prev is not None:
                    prev[0](prev[1])
                del wjobs[:]

            def addbias(bank, c0, c1, tab, tab_ap):
                P.op("dve", lambda e: e.tensor_tensor(out=bank[:, c0:c1], in0=bank[:, c0:c1], in1=tab_ap, op=ALU.add), reads=[bank, tab], writes=[bank])

            def run_A():
                for h in range(4):
                    kt, vt = load_kv(KA + h, VA + h)
                    gt = gtr.next()
                    P.dma("sp", gt[:], gtab_d[l, h], writes=[gt])
                    gg, ge = ggr.next(), ger.next()
                    for (g_, mi) in ((gg, 0), (ge, 2)):
                        P.op("dve", lambda e: e.tensor_tensor(out=g_[:], in0=gt[:], in1=amask[:, mi, :], op=ALU.mult), reads=[gt, amask], writes=[g_])
                        P.op("dve", lambda e: e.tensor_tensor(out=g_[:], in0=g_[:], in1=amask[:, mi + 1, :], op=ALU.add), reads=[g_, amask], writes=[g_])
                    for (t0, N, which) in qgroups:
                        qh_ = {}
                        q = (lambda qh_=qh_, h=h, t0=t0, N=N: qh_["q"] if "q" in qh_ else qh_.setdefault("q", load_q(QA + h, t0, N)))
                        for a in range(N // 128):
                            qt = t0 // 128 + a
                            post = (lambda h=h, t0=t0, N=N: fin_group(h, t0, N)) if a == N // 128 - 1 else None
                            if which == 1:
                                window_tile(kt, vt, q, a, [32, 33], lambda bA, bB: None, None, a * 128, post)
                                continue
                            if 2 <= qt <= 29:
                                d0, nw, tab = 2, 5, gg
                            elif qt == 0:
                                d0, nw, tab = 3, 4, ge
                            elif qt == 1:
                                d0, nw, tab = 2, 4, ge
                            elif qt == 30:
                                d0, nw, tab = 1, 4, ge
                            else:
                                d0, nw, tab = 0, 4, ge
                            m0 = 6 - 2 * d0
                            blocks = [qt + d0 - i for i in range(nw)] + [32, 33]

                            def bias_fn(bA, bB, m0=m0, nw=nw, tab=tab):
                                addbias(bA, 0, 512, tab, tab[:, m0 * 64:m0 * 64 + 512])
                                if nw == 5:
                                    addbias(bB, 0, 128, tab, tab[:, (m0 + 8) * 64:(m0 + 10) * 64])
                            window_tile(kt, vt, q, a, blocks, bias_fn, None, a * 128, post)
                    run_wjobs()

            def run_B():
                kt = vt = None
                for h in range(4):
                    if h % 2 == 0:
                        kt, vt = load_kv(KB + h // 2, VB + h // 2)
                    sink_ap = esink[:, l * 4 + h:l * 4 + h + 1]
                    for (t0, N, which) in qgroups:
                        qh_ = {}
                        q = (lambda qh_=qh_, h=h, t0=t0, N=N: qh_["q"] if "q" in qh_ else qh_.setdefault("q", load_q(QB + h, t0, N)))
                        for a in range(N // 128):
                            qt = t0 // 128 + a
                            post = (lambda h=h, t0=t0, N=N: fin_group(4 + h, t0, N)) if a == N // 128 - 1 else None
                            if which == 1:
                                window_tile(kt, vt, q, a, [32, 33], lambda bA, bB: None, sink_ap, a * 128, post)
                                continue
                            masked = ([qt - 1] if qt > 0 else []) + ([qt + 1] if qt < 31 else [])
                            blocks = masked + [qt, 32, 33]

                            def bias_fn(bA, bB, qt=qt):
                                if 0 < qt < 31:
                                    addbias(bA, 0, 256, trimask, trimask[:, 0:256])
                                elif qt == 0:
                                    addbias(bA, 0, 128, trimask, trimask[:, 128:256])
                                else:
                                    addbias(bA, 0, 128, trimask, trimask[:, 0:128])
                            window_tile(kt, vt, q, a, blocks, bias_fn, sink_ap, a * 128, post)
                    run_wjobs()

            run_A()
            run_B()
            run_dense("C")
            run_dense("D")
            P.barrier()


    def phase3(l):
        grp = groups if l < L - 1 else groups[:8]
        with ExitStack() as st:
            mr = ring("m3_", 2, [128, KC, 512], BF16, st)
            ztr = ring("z3_", 2, [128, KC, 512], F32, st, nsub=KC)
            wr = ring("w3_", 4, [128, KC, 128], BF16, st)
            hbr = ring("hb3_", 2, [128, KC, 512], BF16, st, nsub=KC)
            pools = (ring("sq3_", 3, [128, 512], BF16, st), ring("zb3_", 3, [128, 512], BF16, st), ring("st3_", 6, [128, 512], F32, st))
            tp = ring("t3_", 3, [128, 512], F32, st)
            mainr = Ring([PS[0], PS[1], PS[2], PS[3]])
            wt_ = wbt[("w_out", l)]
            for (t0, N, which) in grp:
                mt = mr.next()
                P.dma("sp", mt[:, :, :N], mixT[:, :, t0:t0 + N].rearrange("h p t -> p h t"), writes=[mt])
                zt = ztr.next()
                P.dma("sp", zt[:, :, :N], xT[:, :, t0:t0 + N].rearrange("kc p t -> p kc t"), writes=zt.sub)
                for cc in range(KC):
                    w = wr.next()
                    P.dma("sp", w[:], wb["w_out"][l, cc].rearrange("p (kc n) -> p kc n", kc=KC), reads=[wt_], writes=[w])
                    ps = mainr.next()
                    for kc in range(KC):
                        P.op("pe", lambda e: e.matmul(ps[:, :N], lhsT=w[:, kc, :], rhs=mt[:, kc, :N], start=(kc == 0), stop=(kc == KC - 1)),
                             reads=[w, mt], writes=[ps])
                    P.op("dve", lambda e: e.scalar_tensor_tensor(out=zt[:, cc, :N], in0=ps[:, :N], scalar=mod(l, 2, cc, which), in1=zt[:, cc, :N],
                                                                op0=ALU.mult, op1=ALU.add), reads=[ps, zt.sub[cc], modv], writes=[zt.sub[cc]])
                Mt, Rt = ln_stats(zt, N, EPS / ALPHA ** 2, pools, PS[4], PS[5])
                for kc in range(KC):
                    ln_apply(zt, kc, N, Mt, Rt, vcol("ln1_g", l, kc), vcol("ln1_b", l, kc), zt[:, kc, :N], zt.sub[kc], tp)
                P.dma("pool", xT[:, :, t0:t0 + N].rearrange("kc p t -> p kc t"), zt[:, :, :N], reads=zt.sub)
                Mt, Rt = ln_stats(zt, N, EPS, pools, PS[6], PS[7])
                hb = hbr.next()
                for kc in range(KC):
                    ln_apply(zt, kc, N, Mt, Rt, mod(l, 4, kc, which), mod(l, 3, kc, which), hb[:, kc, :N], hb.sub[kc], tp)
                c0 = h2col(t0, which) + 1
                P.dma("pool", h2T[:, :, c0:c0 + N].rearrange("kc p t -> p kc t"), hb[:, :, :N], reads=hb.sub)
            P.barrier()

    def phase4(l):
        last = (l == L - 1)
        grp = [(456 * n, 456, 0) for n in range(8)] + [(3648, 448, 0)]
        if not last:
            grp.append((SEQ, CTX, 1))
        with ExitStack() as st:
            h2 = sb("h4", [128, KC, 512], BF16, st)
            zt = sb("z4", [128, KC, 512], F32, st, nsub=KC)
            act = sb("act4", [128, FC, 512], BF16, st)
            hb = sb("hb4", [128, KC, 512], BF16, st, nsub=KC) if not last else None
            wgr = ring("wg4_", 3, [128, KC, 128], BF16, st)
            wur = ring("wu4_", 3, [128, KC, 128], BF16, st)
            wdr = ring("wd4_", 2, [128, FC, 128], BF16, st)
            gbr = ring("gb4_", 3, [128, 512], F32, st)
            a1r = ring("a14_", 3, [128, 512], F32, st)
            sr = ring("s4_", 2, [128, 512], F32, st)
            pools = (ring("sq4_", 2, [128, 512], BF16, st), ring("zb4_", 2, [128, 512], BF16, st), ring("st4_", 3, [128, 512], F32, st))
            tp = ring("t4_", 3, [128, 512], F32, st)
            ost = sb("ost4", [128, D], F32, st) if last else None
            gr = Ring([PS[0], PS[1]])
            ur = Ring([PS[2], PS[3]])
            dr = Ring([PS[0], PS[1], PS[2], PS[3]])
            o_w, _ = VEC[("conv_w", l)]
            o_b, _ = VEC[("conv_b", l)]

            def make_tail(t0, N, which):
                def tail():
                    if not last:
                        P.dma("pool", xT[:, :, t0:t0 + N].rearrange("kc p t -> p kc t"), zt[:, :, :N], reads=zt.sub)
                        Mt, Rt = ln_stats(zt, N, EPS, pools, PS[4], PS[7])
                        for kc in range(KC):
                            ln_apply(zt, kc, N, Mt, Rt, mod(l + 1, 1, kc, which), mod(l + 1, 0, kc, which), hb[:, kc, :N], hb.sub[kc], tp)
                        P.dma("pool", hT[:, :, t0:t0 + N].rearrange("kc p t -> p kc t"), hb[:, :, :N], reads=hb.sub)
                    else:
                        for c_ in range(0, N, 128):
                            w_ = min(128, N - c_)
                            for k4 in range(4):
                                ps = dr.next()
                                for i in range(4):
                                    kc = k4 * 4 + i
                                    P.op("pe", lambda e: e.transpose(out=ps[0:w_, i * 128:(i + 1) * 128], in_=zt[:, kc, c_:c_ + w_], identity=identf[:]),
                                         reads=[zt.sub[kc], identf], writes=[ps])
                                if k4 % 2 == 0:
                                    P.op("act", lambda e: e.activation(out=ost[0:w_, k4 * 512:(k4 + 1) * 512], in_=ps[0:w_, :], func=AF.Copy), reads=[ps], writes=[ost])
                                else:
                                    P.op("dve", lambda e: e.tensor_copy(out=ost[0:w_, k4 * 512:(k4 + 1) * 512], in_=ps[0:w_, :]), reads=[ps], writes=[ost])
                            P.dma("pool", out_d[t0 + c_:t0 + c_ + w_, :], ost[0:w_, :], reads=[ost])
                return tail

            pending = None
            for (t0, N, which) in grp:
                c0 = h2col(t0, which)
                P.dma("sp", h2[:, :, :N + 2], h2T[:, :, c0:c0 + N + 2].rearrange("kc p t -> p kc t"), writes=[h2])
                if pending is None:
                    P.dma("sp", zt[:, :, :N], xT[:, :, t0:t0 + N].rearrange("kc p t -> p kc t"), writes=zt.sub)
                for j in range(FC):
                    if j == 3 and pending is not None:
                        pending()
                        pending = None
                        P.dma("sp", zt[:, :, :N], xT[:, :, t0:t0 + N].rearrange("kc p t -> p kc t"), writes=zt.sub)
                    wg = wgr.next()
                    P.dma("sp", wg[:], wb["w_gate"][l, j].rearrange("p (kc n) -> p kc n", kc=KC), reads=[wbt[("w_gate", l)]], writes=[wg])
                    wu = wur.next()
                    P.dma("sp", wu[:], wb["w_up"][l, j].rearrange("p (kc n) -> p kc n", kc=KC), reads=[wbt[("w_up", l)]], writes=[wu])
                    psG, psU = gr.next(), ur.next()
                    for kc in range(KC):
                        P.op("pe", lambda e: e.matmul(psG[:, :N + 2], lhsT=wg[:, kc, :], rhs=h2[:, kc, 0:N + 2], start=(kc == 0), stop=(kc == KC - 1)),
                             reads=[wg, h2], writes=[psG])
                    for kc in range(KC):
                        P.op("pe", lambda e: e.matmul(psU[:, :N], lhsT=wu[:, kc, :], rhs=h2[:, kc, 1:N + 1], start=(kc == 0), stop=(kc == KC - 1)),
                             reads=[wu, h2], writes=[psU])
                    gb = gbr.next()
                    P.op("act", lambda e: e.activation(out=gb[:, :N + 2], in_=psG[:, :N + 2], func=AF.Copy), reads=[psG], writes=[gb])
                    a1 = a1r.next()
                    w0 = vecs[:, o_w + j * 3 + 0:o_w + j * 3 + 1]
                    w1 = vecs[:, o_w + j * 3 + 1:o_w + j * 3 + 2]
                    w2 = vecs[:, o_w + j * 3 + 2:o_w + j * 3 + 3]
                    cb = vecs[:, o_b + j:o_b + j + 1]
                    P.op("act", lambda e: e.activation(out=a1[:, :N], in_=gb[:, 1:N + 1], func=AF.Identity, scale=w1, bias=cb), reads=[gb, vecs], writes=[a1])
                    P.op("dve", lambda e: e.scalar_tensor_tensor(out=a1[:, :N], in0=gb[:, 0:N], scalar=w0, in1=a1[:, :N], op0=ALU.mult, op1=ALU.add),
                         reads=[gb, a1, vecs], writes=[a1])
                    P.op("dve", lambda e: e.scalar_tensor_tensor(out=a1[:, :N], in0=gb[:, 2:N + 2], scalar=w2, in1=a1[:, :N], op0=ALU.mult, op1=ALU.add),
                         reads=[gb, a1, vecs], writes=[a1])
                    sg = sr.next()
                    P.op("act", lambda e: e.activation(out=sg[:, :N], in_=a1[:, :N], func=AF.Silu), reads=[a1], writes=[sg])
                    P.op("dve", lambda e: e.tensor_tensor(out=act[:, j, :N], in0=sg[:, :N], in1=psU[:, :N], op=ALU.mult), reads=[sg, psU], writes=[act])
                for cc in range(KC):
                    wd = wdr.next()
                    P.dma("sp", wd[:], wb["w_down"][l, cc].rearrange("p (j n) -> p j n", j=FC), reads=[wbt[("w_down", l)]], writes=[wd])
                    ps = dr.next()
                    for j in range(FC):
                        P.op("pe", lambda e: e.matmul(ps[:, :N], lhsT=wd[:, j, :], rhs=act[:, j, :N], start=(j == 0), stop=(j == FC - 1)),
                             reads=[wd, act], writes=[ps])
                    P.op("dve", lambda e: e.scalar_tensor_tensor(out=zt[:, cc, :N], in0=ps[:, :N], scalar=mod(l, 5, cc, which), in1=zt[:, cc, :N],
                                                                op0=ALU.mult, op1=ALU.add), reads=[ps, zt.sub[cc], modv], writes=[zt.sub[cc]])
                Mt, Rt = ln_stats(zt, N, EPS / ALPHA ** 2, pools, PS[5], PS[6])
                for kc in range(KC):
                    ln_apply(zt, kc, N, Mt, Rt, vcol("ln2_g", l, kc), vcol("ln2_b", l, kc), zt[:, kc, :N], zt.sub[kc], tp)
                pending = make_tail(t0, N, which)
            pending()
            P.barrier()

    for l in range(L):
        for nm, ph in (("p1", phase1), ("p2", phase2), ("p3", phase3), ("p4", phase4)):
            ph(l)
            if stop_after == "%s_%d" % (nm, l):
                return finish()
    return finish()


_NC_CACHE = {}


def kernel(**inp):
    inp = {k: np.asarray(v) for k, v in inp.items()}
    sh = prep_shared(inp)
    if "nc" not in _NC_CACHE:
        _NC_CACHE["nc"] = build()[0]
    nc = _NC_CACHE["nc"]
    B = inp["x"].shape[0]
    in_maps = []
    for b in range(B):
        m = dict(sh)
        m["xin"] = np.ascontiguousarray(np.concatenate([inp["x"][b], inp["ctx"][b]], 0))
        cv = np.stack([inp["c"][b].reshape(16, 128).T, inp["c_ctx"].reshape(16, 128).T], -1).reshape(128, 32)
        m["cvec"] = np.ascontiguousarray(cv)
        in_maps.append(m)
    res = run_bass_kernel_spmd(nc, in_maps, core_ids=list(range(B)))
    return np.stack([np.asarray(r["out"]) for r in res.results], 0).astype(np.float32)
```

```python
import numpy as np
import ml_dtypes
from contextlib import ExitStack
import concourse.bass as bass
import concourse.mybir as mybir
from concourse.bass_utils import run_bass_kernel_spmd

F32 = mybir.dt.float32
BF16 = mybir.dt.bfloat16
AF = mybir.ActivationFunctionType
ALU = mybir.AluOpType

D = 2048
KC = 16
SEQ = 4096
CTX = 256
TT = SEQ + CTX
NKT = TT // 128
L = 2
DFF = 5632
FC = 44
GRID_W = 64
ALPHA = (2 * L) ** 0.25
EPS = 1e-6
NEG = -1e30
SC128 = 128 ** -0.5
SC192 = 192 ** -0.5
NCORES = 4

QA, KA, QB, KB, QD, KD, QNC, QPEC, KNC, KPEC = 0, 4, 8, 12, 14, 18, 20, 24, 26, 30
NFM = 31
VA, VB, VD, VC = 0, 4, 6, 8
NVT = 12
H2W = SEQ + CTX + 4
H2_CTX0 = SEQ + 2

VEC = {}
_o = 0
for _l in range(L):
    for _n, _w in (("b_ada", 96), ("ln1_g", 16), ("ln1_b", 16), ("ln2_g", 16), ("ln2_b", 16),
                   ("conv_w", FC * 3), ("conv_b", FC), ("gqa_qn", 1), ("gqa_kn", 1),
                   ("mla_qn", 3), ("mla_kvn", 1), ("sink", 4)):
        VEC[(_n, _l)] = (_o, _w)
        _o += _w
NVEC = _o


class Buf:
    __slots__ = ("lw", "rd")

    def __init__(self):
        self.lw = None
        self.rd = {}


class Tile:
    def __init__(self, t, nsub=0):
        self.t = t
        self.b = Buf()
        self.sub = [Tile(t) for _ in range(nsub)]

    def __getitem__(self, idx):
        return self.t[idx]


class Ring:
    def __init__(self, tiles):
        self.tiles = tiles
        self.i = 0

    def next(self):
        t = self.tiles[self.i % len(self.tiles)]
        self.i += 1
        return t


class Prog:
    NDMA = 8
    LIM = 28000

    def __init__(self, nc):
        self.nc = nc
        self.eng = {"pe": nc.tensor, "act": nc.scalar, "dve": nc.vector,
                    "pool": nc.gpsimd, "sp": nc.sync}
        self.sem = {}
        self.cnt = {}
        self.cur = {}
        self.nsem = 0
        for k in ("pe", "act", "dve", "pool"):
            self._fresh(k)
        self.dqn = {"sp": 0, "pool": 0, "act": 0}
        for q in self.dqn:
            for i in range(self.NDMA):
                self._fresh(("dma", q, i))
        self.seen = {k: {} for k in self.eng}
        self.ninst = 0
        self.nwait = 0
        self.cvn = 0

    def _fresh(self, base):
        ep = self.cur.get(base, (None, -1))[1] + 1
        key = (base, ep)
        self.cur[base] = key
        self.sem[key] = self.nc.alloc_semaphore("s%d" % self.nsem)
        self.nsem += 1
        self.cnt[key] = 0
        return key

    def _wait(self, e, ev):
        if ev is None:
            return
        key, c = ev
        if key[0] == e and e == "pe":
            return
        if self.seen[e].get(key, 0) >= c:
            return
        self.seen[e][key] = c
        val = c * 16 if isinstance(key[0], tuple) else c
        self.eng[e].wait_ge(self.sem[key], val)
        self.nwait += 1

    def _deps(self, e, reads, writes):
        for t in reads:
            self._wait(e, t.b.lw)
        for t in writes:
            self._wait(e, t.b.lw)
            for k, c in t.b.rd.items():
                self._wait(e, (k, c))

    def _commit(self, ev, reads, writes):
        k, c = ev
        for t in reads:
            t.b.rd[k] = c
        for t in writes:
            t.b.lw = ev
            t.b.rd = {}

    def op(self, e, fn, reads=(), writes=()):
        key = self.cur[e]
        if self.cnt[key] >= self.LIM:
            key = self._fresh(e)
        self._deps(e, reads, writes)
        ins = fn(self.eng[e])
        self.cnt[key] += 1
        ins.then_inc(self.sem[key], 1)
        self._commit((key, self.cnt[key]), reads, writes)
        self.ninst += 1
        return ins

    def dma(self, q, out, in_, reads=(), writes=(), **kw):
        i = self.dqn[q] % self.NDMA
        self.dqn[q] += 1
        base = ("dma", q, i)
        key = self.cur[base]
        if self.cnt[key] > 0:
            self._wait(q, (key, self.cnt[key]))
        if self.cnt[key] * 16 >= self.LIM:
            key = self._fresh(base)
        self._deps(q, reads, writes)
        ins = self.eng[q].dma_start(out=out, in_=in_, **kw)
        self.cnt[key] += 1
        ins.then_inc(self.sem[key], 16)
        self._commit((key, self.cnt[key]), reads, writes)
        self.ninst += 1
        return ins

    def cvt(self, out, in_, tile):
        key = (("cv", self.cvn), 0)
        self.cvn += 1
        self.sem[key] = self.nc.alloc_semaphore("cv%d" % self.cvn)
        self.nsem += 1
        ins = self.nc.gpsimd.dma_start(out=out, in_=in_)
        ins.then_inc(self.sem[key], 16)
        self.cnt[key] = 1
        tile.b.lw = (key, 1)
        self.ninst += 1

    def barrier(self):
        for e in self.eng:
            for key, c in list(self.cnt.items()):
                if c > 0 and key[0] != e and not (isinstance(key[0], tuple) and key[0][0] == "cv"):
                    self._wait(e, (key, c))


def _rope_tables():
    t = np.arange(SEQ)
    row, col = (t // GRID_W).astype(np.float64), (t % GRID_W).astype(np.float64)

    def tab(dim):
        seg = dim // 2
        half = seg // 2
        C = np.ones((dim, TT))
        S = np.zeros((dim, TT))
        perm = np.zeros(dim, np.int64)
        inv = 10000.0 ** (-np.arange(half, dtype=np.float32) / half)
        for d in range(dim):
            s, e = d // seg, d % seg
            i = e % half
            first = e < half
            pos = row if s == 0 else col
            ang = (pos.astype(np.float32) * np.float32(inv[i])).astype(np.float32)
            C[d, :SEQ] = np.cos(ang)
            S[d, :SEQ] = -np.sin(ang) if first else np.sin(ang)
            perm[d] = d + half if first else d - half
        return C.astype(np.float32), S.astype(np.float32), perm

    C128, S128, p128 = tab(128)
    C64, S64, p64 = tab(64)
    rope128 = np.stack([C128, S128])
    rope64 = np.stack([np.concatenate([C64, C64], 0), np.concatenate([S64, S64], 0)])
    Pm128 = np.zeros((128, 128), np.float32)
    Pm128[p128, np.arange(128)] = 1.0
    Pm64 = np.zeros((128, 128), np.float32)
    for hh in range(2):
        Pm64[hh * 64 + p64, hh * 64 + np.arange(64)] = 1.0
    return rope128, rope64, np.stack([Pm128, Pm64])


def _amask():
    p = np.arange(128)
    a, kcol = p // 64, p % 64
    m = np.arange(14)
    j = 6 - m
    dr = j[None, :] + a[:, None]
    qcol = np.arange(64)
    c0 = np.clip(qcol - 8, 0, 48)
    col_in = (kcol[:, None] >= c0[None, :]) & (kcol[:, None] < c0[None, :] + 16)
    v_edge = np.broadcast_to(col_in[:, None, :], (128, 14, 64))
    v_gen = v_edge & ((dr >= -4) & (dr <= 3))[:, :, None]
    out = []
    for v in (v_gen, v_edge):
        out.append((v / SC128).astype(np.float32).reshape(128, 14 * 64))
        out.append(np.where(v, 0.0, NEG).astype(np.float32).reshape(128, 14 * 64))
    return np.stack(out)


def _gtab(rpb):
    p = np.arange(128)
    a, kcol = p // 64, p % 64
    j = 6 - np.arange(14)
    dr = j[None, :] + a[:, None]
    qcol = np.arange(64)
    dc = np.clip(kcol[:, None] - qcol[None, :] + 15, 0, 30)
    g = rpb[:, :, (dr + 7)[:, :, None], dc[:, None, :]]
    return np.ascontiguousarray(g.reshape(L, 4, 128, 14 * 64))


def _chunk_w(w, kc):
    Lw, K, N = w.shape
    return np.ascontiguousarray(w.reshape(Lw, kc, 128, N // 128, 128).transpose(0, 3, 2, 1, 4).reshape(Lw, N // 128, 128, kc * 128))


_CONST_CACHE = {}


def _consts():
    if not _CONST_CACHE:
        rope128, rope64, perm = _rope_tables()
        k = np.arange(128)
        tri = np.stack([np.where(k[:, None] >= k[None, :], 0.0, NEG), np.where(k[:, None] <= k[None, :], 0.0, NEG)], 1)
        _CONST_CACHE.update(rope128=rope128, rope64=rope64, perm=perm, amask=_amask(),
                            identf=np.eye(128, dtype=np.float32),
                            identb=np.eye(128, dtype=np.float32).astype(ml_dtypes.bfloat16),
                            trimask=np.ascontiguousarray(tri.reshape(128, 256).astype(np.float32)))
    return _CONST_CACHE


def prep_shared(inp):
    sh = {}
    w_in = inp["w_in"]
    sizes = [512, 512, 512, 512, 256, 256, 384, 128, 64, 512, 256, 256]
    offs = np.cumsum([0] + sizes)
    seg = lambda i: w_in[:, :, offs[i]:offs[i + 1]]
    w_in_r = np.concatenate([seg(0), seg(1), seg(2), seg(3), seg(4), seg(5), seg(6), seg(7), seg(9), seg(10), seg(11),
                             seg(8), np.zeros((L, D, 64), np.float32)], axis=2)
    sh["w_in"] = _chunk_w(w_in_r, KC)
    sh["w_out"] = _chunk_w(inp["w_out"], KC)
    sh["w_gate"] = _chunk_w(inp["ffn_w_gate"], KC)
    sh["w_up"] = _chunk_w(inp["ffn_w_up"], KC)
    sh["w_down"] = _chunk_w(inp["ffn_w_down"], FC)
    uq = inp["mla_w_uq"].reshape(L, 384, 4, 192)
    uq = np.concatenate([uq[..., :128].reshape(L, 384, 512), uq[..., 128:].reshape(L, 384, 256)], -1)
    sh["w_uq"] = np.ascontiguousarray(uq.reshape(L, 3, 128, 768).transpose(0, 2, 1, 3).reshape(L, 128, 3 * 768))
    ukv = inp["mla_w_ukv"].reshape(L, 128, 4, 256)
    sh["w_ukv"] = np.ascontiguousarray(np.concatenate([ukv[..., :128].reshape(L, 128, 512), ukv[..., 128:].reshape(L, 128, 512)], -1))
    sh["w_ada"] = np.ascontiguousarray(inp["w_ada"].reshape(L * D, 6 * D))
    vec = np.zeros((128, NVEC), np.float32)

    def put(name, l, arr):
        o, w = VEC[(name, l)]
        vec[:, o:o + w] = arr

    for l in range(L):
        put("b_ada", l, inp["b_ada"][l].reshape(96, 128).T)
        for n in ("ln1_g", "ln1_b", "ln2_g", "ln2_b"):
            put(n, l, inp[n][l].reshape(16, 128).T)
        put("conv_w", l, inp["ffn_conv_w"][l].reshape(3, FC, 128).transpose(2, 1, 0).reshape(128, FC * 3))
        put("conv_b", l, inp["ffn_conv_b"][l].reshape(FC, 128).T)
        put("gqa_qn", l, inp["gqa_q_norm"][l].reshape(128, 1))
        put("gqa_kn", l, inp["gqa_k_norm"][l].reshape(128, 1))
        put("mla_qn", l, inp["mla_q_norm"][l].reshape(3, 128).T)
        put("mla_kvn", l, inp["mla_kv_norm"][l].reshape(128, 1))
        put("sink", l, np.broadcast_to(inp["swa_sink"][l][None, :], (128, 4)))
    sh["vecs"] = vec
    sh["gtab"] = _gtab(inp["na_rpb"])
    sh.update(_consts())
    return sh


def build(dbg=(), stop_after=None):
    nc = bass.Bass("TRN2", target_bir_lowering=False)
    P = Prog(nc)
    es = ExitStack()

    def din(name, shape, dt=F32):
        return nc.dram_tensor(name, list(shape), dt, kind="ExternalInput").ap()

    def dscr(name, shape, dt):
        kind = "ExternalOutput" if name in dbg else "Internal"
        return nc.dram_tensor(name, list(shape), dt, kind=kind).ap()

    xin = din("xin", [TT, D])
    cvec = din("cvec", [128, 32])
    w_ada = din("w_ada", [L * D, 6 * D])
    vecs_d = din("vecs", [128, NVEC])
    w_in_d = din("w_in", [L, 33, 128, 2048])
    w_out_d = din("w_out", [L, 16, 128, 2048])
    w_gate_d = din("w_gate", [L, FC, 128, 2048])
    w_up_d = din("w_up", [L, FC, 128, 2048])
    w_down_d = din("w_down", [L, 16, 128, DFF])
    w_uq_d = din("w_uq", [L, 128, 3 * 768])
    w_ukv_d = din("w_ukv", [L, 128, 1024])
    gtab_d = din("gtab", [L, 4, 128, 14 * 64])
    rope128_d = din("rope128", [2, 128, TT])
    rope64_d = din("rope64", [2, 128, TT])
    perm_d = din("perm", [2, 128, 128])
    amask_d = din("amask", [4, 128, 14 * 64])
    identf_d = din("identf", [128, 128])
    identb_d = din("identb", [128, 128], BF16)
    trimask_d = din("trimask", [128, 256])
    out_d = nc.dram_tensor("out", [SEQ, D], F32, kind="ExternalOutput").ap()

    wb = {}
    wsrc = {"w_in": w_in_d, "w_out": w_out_d, "w_gate": w_gate_d, "w_up": w_up_d, "w_down": w_down_d,
            "w_uq": w_uq_d, "w_ukv": w_ukv_d}
    for n, src in wsrc.items():
        wb[n] = nc.dram_tensor(n + "_bf", list(src.shape), BF16, kind="Internal").ap()
    NPIECE = {"w_in": 1, "w_out": 1, "w_uq": 1, "w_ukv": 1, "w_gate": 4, "w_up": 4, "w_down": 4}
    wbt = {(n, l, p): Tile(None) for n in wsrc for l in range(L) for p in range(NPIECE[n])}

    def wdep(n, l, idx=0):
        per = wsrc[n].shape[1] // NPIECE[n] if NPIECE[n] > 1 else 1
        return wbt[(n, l, idx // per if NPIECE[n] > 1 else 0)]

    xT = dscr("xT", [KC, 128, TT], F32)
    hT = dscr("hT", [KC, 128, TT], BF16)
    h2T = dscr("h2T", [KC, 128, H2W], BF16)
    fmT = dscr("fmT", [NFM, 128, TT], BF16)
    vtok = dscr("vtok", [NVT, TT, 128], BF16)
    mixT = dscr("mixT", [KC, 128, TT], BF16)

    uid = [0]

    def sb(name, shape, dt, stack=None, nsub=0):
        uid[0] += 1
        t = (stack or es).enter_context(nc.sbuf_tensor("sb%d_%s" % (uid[0], name), list(shape), dt))
        return Tile(t, nsub)

    def ring(name, n, shape, dt, stack=None, nsub=0):
        return Ring([sb("%s%d" % (name, i), shape, dt, stack, nsub) for i in range(n)])

    PS = [Tile(es.enter_context(nc.psum_tensor("ps%d" % i, [128, 512], F32))) for i in range(8)]

    def flat2k(ap):
        n = 1
        for s in ap.shape:
            n *= s
        names = " ".join("a%d" % i for i in range(len(ap.shape)))
        return ap.rearrange("%s -> (%s)" % (names, names)).rearrange("(r c) -> r c", c=2048)

    def convert_small(l):
        for n in ("w_in", "w_uq", "w_ukv", "w_out"):
            P.cvt(flat2k(wb[n][l]), flat2k(wsrc[n][l]), wbt[(n, l, 0)])

    def ffn_pieces(l):
        out = []
        for p in range(4):
            for n in ("w_gate", "w_up", "w_down"):
                per = wsrc[n].shape[1] // 4

                def f(n=n, p=p, per=per):
                    P.cvt(flat2k(wb[n][l, p * per:(p + 1) * per]), flat2k(wsrc[n][l, p * per:(p + 1) * per]), wbt[(n, l, p)])
                out.append(f)
        return out

    convert_small(0)
    cvq = ffn_pieces(0)

    def cv_step(n=1):
        for _ in range(n):
            if cvq:
                cvq.pop(0)()

    identf = sb("identf", [128, 128], F32)
    identb = sb("identb", [128, 128], BF16)
    perm = sb("perm", [128, 2, 128], F32)
    trimask = sb("trimask", [128, 256], F32)
    vecs = sb("vecs", [128, NVEC], F32)
    onesb = sb("onesb", [128, 128], BF16)
    ones128 = sb("ones128", [128, 128], BF16)
    ones384 = sb("ones384", [128, 128], BF16)
    modv = sb("modv", [128, L, 96, 2], F32)
    esink = sb("esink", [128, L * 4], F32)
    P.dma("sp", identf[:], identf_d, writes=[identf])
    P.dma("sp", identb[:], identb_d, writes=[identb])
    P.dma("sp", perm[:], perm_d.rearrange("a p c -> p a c"), writes=[perm])
    P.dma("sp", trimask[:], trimask_d, writes=[trimask])
    P.dma("sp", vecs[:], vecs_d, writes=[vecs])
    P.op("dve", lambda e: e.memset(onesb[:], 1.0 / D), writes=[onesb])
    P.op("dve", lambda e: e.memset(ones128[:], 1.0 / 128), writes=[ones128])
    P.op("dve", lambda e: e.memset(ones384[:], 1.0 / 384), writes=[ones384])

    zcol = sb("zcol", [128, KC, 1], BF16)
    P.op("dve", lambda e: e.memset(zcol[:], 0.0), writes=[zcol])
    for c in (0, SEQ + 1, SEQ + 2, H2W - 1):
        P.dma("sp", h2T[:, :, c:c + 1].rearrange("kc p t -> p kc t"), zcol[:], reads=[zcol], allow_slow_non_contiguous=True)

    def vcol(name, l, j=0, w=1):
        o, _ = VEC[(name, l)]
        return vecs[:, o + j:o + j + w]

    def mod(l, s, kc, which):
        return modv[:, l, s * 16 + kc, which:which + 1]

    def ln_stats(zt, N, eps, pools, psM, psV):
        sqp, zbp, stp = pools
        for kc in range(KC):
            sq = sqp.next()
            P.op("act", lambda e: e.activation(out=sq[:, :N], in_=zt[:, kc, :N], func=AF.Square), reads=[zt.sub[kc]], writes=[sq])
            zb = zbp.next()
            P.op("pool" if kc % 2 == 0 else "dve", lambda e: e.tensor_copy(out=zb[:, :N], in_=zt[:, kc, :N]), reads=[zt.sub[kc]], writes=[zb])
            P.op("pe", lambda e: e.matmul(psM[:, :N], lhsT=onesb[:], rhs=zb[:, :N], start=(kc == 0), stop=(kc == KC - 1)),
                 reads=[onesb, zb], writes=[psM])
            P.op("pe", lambda e: e.matmul(psV[:, :N], lhsT=onesb[:], rhs=sq[:, :N], start=(kc == 0), stop=(kc == KC - 1)),
                 reads=[onesb, sq], writes=[psV])
        m2 = stp.next()
        P.op("act", lambda e: e.activation(out=m2[:, :N], in_=psM[:, :N], func=AF.Square), reads=[psM], writes=[m2])
        P.op("dve", lambda e: e.tensor_tensor(out=m2[:, :N], in0=psV[:, :N], in1=m2[:, :N], op=ALU.subtract), reads=[psV, m2], writes=[m2])
        P.op("dve", lambda e: e.tensor_scalar(out=m2[:, :N], in0=m2[:, :N], scalar1=float(eps), scalar2=None, op0=ALU.add), reads=[m2], writes=[m2])
        P.op("act", lambda e: e.activation(out=m2[:, :N], in_=m2[:, :N], func=AF.Sqrt), reads=[m2], writes=[m2])
        P.op("dve", lambda e: e.reciprocal(out=psV[:, :N], in_=m2[:, :N]), reads=[m2], writes=[psV])
        return psM, psV

    def ln_apply(zt, kc, N, Mt, Rt, scale_ap, bias_ap, out_ap, out_tile, tp, extra_reads=()):
        t = tp.next()
        P.op("dve", lambda e: e.tensor_tensor(out=t[:, :N], in0=zt[:, kc, :N], in1=Mt[:, :N], op=ALU.subtract), reads=[zt.sub[kc], Mt], writes=[t])
        P.op("dve", lambda e: e.tensor_tensor(out=t[:, :N], in0=t[:, :N], in1=Rt[:, :N], op=ALU.mult), reads=[t, Rt], writes=[t])
        P.op("act", lambda e: e.activation(out=out_ap, in_=t[:, :N], func=AF.Identity, scale=scale_ap, bias=bias_ap),
             reads=[t, vecs, modv] + list(extra_reads), writes=[out_tile])

    groups = [(g * 512, 512, 0) for g in range(8)] + [(SEQ, CTX, 1)]

    def h2col(t0, which):
        return (t0 if which == 0 else H2_CTX0)

    with ExitStack() as st:
        cv = sb("cv", [128, 32], F32, st)
        scv = sb("scv", [128, 32], F32, st)
        wblk = ring("wblk", 2, [128, KC, 512], F32, st)
        P.dma("sp", cv[:], cvec, writes=[cv])
        P.op("act", lambda e: e.activation(out=scv[:], in_=cv[:], func=AF.Silu), reads=[cv], writes=[scv])
        mrr = ring("mrow", 2, [2, 512], F32, st)
        rowr = Ring([PS[2], PS[3]])
        for l in range(L):
            psA = PS[l]
            psv = psA[:, 0:192].rearrange("p (j w) -> p j w", w=2)
            wl = w_ada[l * D:(l + 1) * D, :].rearrange("(kc p) n -> p kc n", p=128)
            for blk in range(24):
                wt = wblk.next()
                P.dma("sp", wt[:], wl[:, :, blk * 512:(blk + 1) * 512], writes=[wt])
                prow = rowr.next()
                for kc in range(KC):
                    P.op("pe", lambda e: e.matmul(prow[0:2, :], lhsT=scv[:, 2 * kc:2 * kc + 2], rhs=wt[:, kc, :], start=(kc == 0), stop=(kc == KC - 1)),
                         reads=[wt, scv], writes=[prow])
                mr = mrr.next()
                P.op("act", lambda e: e.activation(out=mr[0:2, :], in_=prow[0:2, :], func=AF.Copy), reads=[prow], writes=[mr])
                for j in range(4):
                    P.op("pe", lambda e: e.transpose(out=psv[:, blk * 4 + j, :], in_=mr[0:2, j * 128:(j + 1) * 128], identity=identf[0:2, 0:2]),
                         reads=[mr, identf], writes=[psA])
            o, w = VEC[("b_ada", l)]
            for which in range(2):
                P.op("dve", lambda e: e.tensor_tensor(out=modv[:, l, :, which], in0=psv[:, :, which], in1=vecs[:, o:o + 96], op=ALU.add),
                     reads=[psA, vecs], writes=[modv])
            for s in (1, 4):
                P.op("dve", lambda e: e.tensor_scalar(out=modv[:, l, s * 16:(s + 1) * 16, :], in0=modv[:, l, s * 16:(s + 1) * 16, :],
                                                      scalar1=1.0, scalar2=None, op0=ALU.add), reads=[modv], writes=[modv])
            for s in (2, 5):
                P.op("dve", lambda e: e.tensor_scalar(out=modv[:, l, s * 16:(s + 1) * 16, :], in0=modv[:, l, s * 16:(s + 1) * 16, :],
                                                      scalar1=1.0 / ALPHA, scalar2=None, op0=ALU.mult), reads=[modv], writes=[modv])
            o, w = VEC[("sink", l)]
            P.op("act", lambda e: e.activation(out=esink[:, l * 4:(l + 1) * 4], in_=vecs[:, o:o + 4], func=AF.Exp), reads=[vecs], writes=[esink])
        P.barrier()
    if "modv" in dbg:
        modv_o = nc.dram_tensor("modv_o", [128, L * 96 * 2], F32, kind="ExternalOutput").ap()
        P.dma("sp", modv_o, modv[:].rearrange("p l j w -> p (l j w)"), reads=[modv])

    def finish():
        P.barrier()
        es.close()
        return nc, P

    if stop_after == "ada":
        return finish()

    with ExitStack() as st:
        xr = ring("xin", 2, [128, 4, D], F32, st)
        ztr = ring("zt0", 2, [128, KC, 512], F32, st, nsub=KC)
        hbr = ring("hb0", 2, [128, KC, 512], BF16, st, nsub=KC)
        pools = (ring("sq0", 3, [128, 512], BF16, st), ring("zb0", 3, [128, 512], BF16, st), ring("st0", 6, [128, 512], F32, st))
        tp = ring("t0", 3, [128, 512], F32, st)
        psr = Ring([PS[0], PS[1], PS[2], PS[3]])
        stat_sets = Ring([(PS[4], PS[5]), (PS[6], PS[7])])
        for (t0, N, which) in groups:
            cv_step()
            nt = N // 128
            xt = xr.next()
            P.dma("sp", xt[:, :nt, :], xin[t0:t0 + N, :].rearrange("(a p) d -> p a d", p=128), writes=[xt])
            zt = ztr.next()
            for kc in range(KC):
                ps = psr.next()
                for a in range(nt):
                    P.op("pe", lambda e: e.transpose(out=ps[:, a * 128:(a + 1) * 128], in_=xt[:, a, kc * 128:(kc + 1) * 128], identity=identf[:]),
                         reads=[xt, identf], writes=[ps])
                eng = "act" if kc % 2 == 0 else "dve"
                if eng == "act":
                    P.op("act", lambda e: e.activation(out=zt[:, kc, :N], in_=ps[:, :N], func=AF.Copy), reads=[ps], writes=[zt.sub[kc]])
                else:
                    P.op("dve", lambda e: e.tensor_copy(out=zt[:, kc, :N], in_=ps[:, :N]), reads=[ps], writes=[zt.sub[kc]])
            P.dma("pool", xT[:, :, t0:t0 + N].rearrange("kc p t -> p kc t"), zt[:, :, :N], reads=zt.sub)
            bM, bV = stat_sets.next()
            Mt, Rt = ln_stats(zt, N, EPS, pools, bM, bV)
            hb = hbr.next()
            for kc in range(KC):
                ln_apply(zt, kc, N, Mt, Rt, mod(0, 1, kc, which), mod(0, 0, kc, which), hb[:, kc, :N], hb.sub[kc], tp)
            P.dma("pool", hT[:, :, t0:t0 + N].rearrange("kc p t -> p kc t"), hb[:, :, :N], reads=hb.sub)
        P.barrier()
    if stop_after == "p0":
        return finish()


    def phase1(l):
        with ExitStack() as st:
            hr = ring("h1_", 2, [128, KC, 512], BF16, st)
            wr = ring("w1_", 6, [128, KC, 128], BF16, st)
            wv = sb("wv1", [128, 8, KC, 128], BF16, st)
            wuq = sb("wuq", [128, 3, 768], BF16, st)
            wukv = sb("wukv", [128, 1024], BF16, st)
            rc128 = ring("rc128_", 2, [128, 2, 512], F32, st)
            rc64 = ring("rc64_", 2, [128, 2, 512], F32, st)
            xsr = ring("xs1_", 4, [128, 512], F32, st)
            t1r = ring("t1_", 6, [128, 512], F32, st)
            obr = ring("ob1_", 6, [128, 512], BF16, st)
            sqr = ring("sq1_", 3, [128, 512], BF16, st)
            rsr = ring("rs1_", 4, [128, 512], F32, st)
            vobr = ring("vob1_", 3, [128, 512], BF16, st)
            cqs = sb("cqs", [128, 3, 512], F32, st)
            cqn = sb("cqn", [128, 3, 512], BF16, st)
            ckvn = sb("ckvn", [128, 512], BF16, st)
            mainr = Ring([PS[0], PS[1], PS[2], PS[3]])
            ppr = Ring([PS[4]])
            psR = PS[5]
            vr = Ring([PS[6], PS[7]])
            wt_in = wbt[("w_in", l, 0)]
            wsrc_l = w_in_bf = wb["w_in"]

            def wchunks(c0, n):
                return wb["w_in"][l, c0:c0 + n].rearrange("c p (kc n) -> p c kc n", kc=KC)

            P.dma("sp", wv[:, 0:4], wchunks(8, 4), reads=[wt_in], writes=[wv])
            P.dma("sp", wv[:, 4:6], wchunks(18, 2), reads=[wt_in], writes=[wv])
            P.dma("sp", wv[:, 6:8], wchunks(30, 2), reads=[wt_in], writes=[wv])
            P.dma("sp", wuq[:], wb["w_uq"][l].rearrange("p (kc n) -> p kc n", kc=3), reads=[wbt[("w_uq", l, 0)]], writes=[wuq])
            P.dma("sp", wukv[:], wb["w_ukv"][l], reads=[wbt[("w_ukv", l, 0)]], writes=[wukv])
            cnt = [0]

            for (t0, N, which) in groups:
                cv_step()
                skipq = (l == L - 1 and which == 1)
                nt = N // 128
                ht = hr.next()
                P.dma("sp", ht[:, :, :N], hT[:, :, t0:t0 + N].rearrange("kc p t -> p kc t"), writes=[ht])
                r128 = rc128.next()
                P.dma("sp", r128[:, :, :N], rope128_d[:, :, t0:t0 + N].rearrange("a p t -> p a t"), writes=[r128])
                r64 = rc64.next()
                P.dma("sp", r64[:, :, :N], rope64_d[:, :, t0:t0 + N].rearrange("a p t -> p a t"), writes=[r64])

                def proj(cc, M=128):
                    w = wr.next()
                    P.dma("sp", w[:], wb["w_in"][l, cc].rearrange("p (kc n) -> p kc n", kc=KC), reads=[wt_in], writes=[w])
                    ps = mainr.next()
                    for kc in range(KC):
                        P.op("pe", lambda e: e.matmul(ps[0:M, :N], lhsT=w[:, kc, 0:M], rhs=ht[:, kc, :N], start=(kc == 0), stop=(kc == KC - 1)),
                             reads=[w, ht], writes=[ps])
                    return ps

                def copy_out(dst_ap, dst_tile, src_ap, src_tile):
                    cnt[0] += 1
                    if cnt[0] % 2 == 0:
                        P.op("act", lambda e: e.activation(out=dst_ap, in_=src_ap, func=AF.Copy), reads=[src_tile], writes=[dst_tile])
                    else:
                        P.op("dve", lambda e: e.tensor_copy(out=dst_ap, in_=src_ap), reads=[src_tile], writes=[dst_tile])

                def evac_bf(ps, M=128):
                    ob = obr.next()
                    copy_out(ob[:M, :N], ob, ps[:M, :N], ps)
                    return ob

                def store_fm(idx, ob, M=128, prow=0):
                    P.dma("pool", fmT[idx, prow:prow + M, t0:t0 + N], ob[:M, :N], reads=[ob])

                def rope(xs, M, tab, pmi):
                    pp = ppr.next()
                    P.op("pe", lambda e: e.matmul(pp[:M, :N], lhsT=perm[:M, pmi, :M], rhs=xs[:M, :N], start=True, stop=True),
                         reads=[perm, xs], writes=[pp])
                    t1 = t1r.next()
                    P.op("pool", lambda e: e.tensor_tensor(out=t1[:M, :N], in0=xs[:M, :N], in1=tab[:M, 0, :N], op=ALU.mult), reads=[xs, tab], writes=[t1])
                    t2 = t1r.next()
                    P.op("dve", lambda e: e.tensor_tensor(out=t2[:M, :N], in0=pp[:M, :N], in1=tab[:M, 1, :N], op=ALU.mult), reads=[pp, tab], writes=[t2])
                    ob = obr.next()
                    P.op("dve", lambda e: e.tensor_tensor(out=ob[:M, :N], in0=t1[:M, :N], in1=t2[:M, :N], op=ALU.add), reads=[t1, t2], writes=[ob])
                    return ob

                def rstd_from(psr_tile):
                    m = rsr.next()
                    P.op("dve", lambda e: e.tensor_scalar(out=m[:, :N], in0=psr_tile[:, :N], scalar1=EPS, scalar2=None, op0=ALU.add), reads=[psr_tile], writes=[m])
                    P.op("act", lambda e: e.activation(out=m[:, :N], in_=m[:, :N], func=AF.Sqrt), reads=[m], writes=[m])
                    P.op("dve", lambda e: e.reciprocal(out=m[:, :N], in_=m[:, :N]), reads=[m], writes=[m])
                    return m

                jobs = []
                mcq = []

                def post_plain(idx):
                    return lambda ps: store_fm(idx, evac_bf(ps))

                def post_rope(idx):
                    def f(ps):
                        xs = xsr.next()
                        copy_out(xs[:, :N], xs, ps[:, :N], ps)
                        store_fm(idx, rope(xs, 128, r128, 0))
                    return f

                def post_norm_rope(idx, gname):
                    def f(ps):
                        sq = sqr.next()
                        P.op("act", lambda e: e.activation(out=sq[:, :N], in_=ps[:, :N], func=AF.Square), reads=[ps], writes=[sq])
                        P.op("pe", lambda e: e.matmul(psR[:, :N], lhsT=ones128[:], rhs=sq[:, :N], start=True, stop=True), reads=[ones128, sq], writes=[psR])
                        m = rstd_from(psR)
                        xs = xsr.next()
                        P.op("dve", lambda e: e.scalar_tensor_tensor(out=xs[:, :N], in0=ps[:, :N], scalar=vcol(gname, l), in1=m[:, :N], op0=ALU.mult, op1=ALU.mult),
                             reads=[ps, m, vecs], writes=[xs])
                        store_fm(idx, rope(xs, 128, r128, 0))
                    return f

                def post_cq(i):
                    def f(ps):
                        P.op("act", lambda e: e.activation(out=cqs[:, i, :N], in_=ps[:, :N], func=AF.Copy), reads=[ps], writes=[cqs])
                        sq = sqr.next()
                        P.op("act", lambda e: e.activation(out=sq[:, :N], in_=ps[:, :N], func=AF.Square), reads=[ps], writes=[sq])
                        P.op("pe", lambda e: e.matmul(psR[:, :N], lhsT=ones384[:], rhs=sq[:, :N], start=(i == 0), stop=(i == 2)), reads=[ones384, sq], writes=[psR])
                        if i == 2:
                            mcq.append(rstd_from(psR))
                    return f

                def mla_q_tail(_):
                    m = mcq.pop()
                    for i in range(3):
                        P.op("dve", lambda e: e.scalar_tensor_tensor(out=cqn[:, i, :N], in0=cqs[:, i, :N], scalar=vcol("mla_qn", l, i), in1=m[:, :N],
                                                                    op0=ALU.mult, op1=ALU.mult), reads=[cqs, m, vecs], writes=[cqn])
                    for h in range(4):
                        ps = vr.next()
                        for i in range(3):
                            P.op("pe", lambda e: e.matmul(ps[:, :N], lhsT=wuq[:, i, h * 128:(h + 1) * 128], rhs=cqn[:, i, :N], start=(i == 0), stop=(i == 2)),
                                 reads=[wuq, cqn], writes=[ps])
                        store_fm(QNC + h, evac_bf(ps))
                    for j in range(2):
                        ps = vr.next()
                        for i in range(3):
                            P.op("pe", lambda e: e.matmul(ps[:, :N], lhsT=wuq[:, i, 512 + j * 128:512 + (j + 1) * 128], rhs=cqn[:, i, :N], start=(i == 0), stop=(i == 2)),
                                 reads=[wuq, cqn], writes=[ps])
                        xs = xsr.next()
                        copy_out(xs[:, :N], xs, ps[:, :N], ps)
                        store_fm(QPEC + j, rope(xs, 128, r64, 1))

                def post_ckv(ps):
                    xs = xsr.next()
                    P.op("act", lambda e: e.activation(out=xs[:, :N], in_=ps[:, :N], func=AF.Copy), reads=[ps], writes=[xs])
                    sq = sqr.next()
                    P.op("act", lambda e: e.activation(out=sq[:, :N], in_=ps[:, :N], func=AF.Square), reads=[ps], writes=[sq])
                    P.op("pe", lambda e: e.matmul(psR[:, :N], lhsT=ones128[:], rhs=sq[:, :N], start=True, stop=True), reads=[ones128, sq], writes=[psR])
                    m = rstd_from(psR)
                    P.op("dve", lambda e: e.scalar_tensor_tensor(out=ckvn[:, :N], in0=xs[:, :N], scalar=vcol("mla_kvn", l), in1=m[:, :N], op0=ALU.mult, op1=ALU.mult),
                         reads=[xs, m, vecs], writes=[ckvn])

                def mla_kv_tail(_):
                    for h in range(4):
                        ps = vr.next()
                        P.op("pe", lambda e: e.matmul(ps[:, :N], lhsT=wukv[:, h * 128:(h + 1) * 128], rhs=ckvn[:, :N], start=True, stop=True), reads=[wukv, ckvn], writes=[ps])
                        store_fm(KNC + h, evac_bf(ps))
                    for a in range(nt):
                        psv = vr.next()
                        P.op("pe", lambda e: e.matmul(psv[:, :], lhsT=ckvn[:, a * 128:(a + 1) * 128], rhs=wukv[:, 512:1024], start=True, stop=True), reads=[wukv, ckvn], writes=[psv])
                        vob = vobr.next()
                        copy_out(vob[:, :], vob, psv[:, :], psv)
                        P.dma("pool", vtok[VC:VC + 4, t0 + a * 128:t0 + (a + 1) * 128, :].rearrange("h t d -> t h d"),
                              vob[:, :].rearrange("t (h d) -> t h d", h=4), reads=[vob])

                def post_kpe(ps):
                    xs = xsr.next()
                    copy_out(xs[:64, :N], xs, ps[:64, :N], ps)
                    ob = rope(xs, 64, r64, 1)
                    store_fm(KPEC, ob, M=64, prow=0)
                    store_fm(KPEC, ob, M=64, prow=64)

                def addjob(cc, post, M=128):
                    jobs.append(((lambda cc=cc, M=M: proj(cc, M)), post))

                if not skipq:
                    for i in range(3):
                        addjob(20 + i, post_cq(i))
                addjob(23, post_ckv)
                for cc in range(0, 8):
                    if not (skipq and cc < 4):
                        addjob(cc, post_plain(QA + cc))
                    if cc == 4 and not skipq:
                        jobs.append((None, mla_q_tail))
                    if cc == 6:
                        jobs.append((None, mla_kv_tail))
                for i, cc in enumerate(range(12, 18)):
                    if not (skipq and i < 4):
                        addjob(cc, post_rope(QB + i))
                for i, cc in enumerate(range(24, 30)):
                    if not (skipq and i < 4):
                        addjob(cc, post_norm_rope(QD + i, "gqa_qn" if i < 4 else "gqa_kn"))
                addjob(32, post_kpe, M=64)
                pend = []
                for (pj, post) in jobs:
                    ps = pj() if pj is not None else None
                    pend.append((post, ps))
                    if len(pend) > 2:
                        f_, a_ = pend.pop(0)
                        f_(a_)
                for (f_, a_) in pend:
                    f_(a_)
                for a in range(nt):
                    for blk in range(2):
                        psv = vr.next()
                        for kc in range(KC):
                            P.op("pe", lambda e: e.matmul(psv[:, :].rearrange("t (c n) -> t c n", c=4), lhsT=ht[:, kc, a * 128:(a + 1) * 128],
                                                          rhs=wv[:, blk * 4:(blk + 1) * 4, kc, :], start=(kc == 0), stop=(kc == KC - 1)),
                                 reads=[ht, wv], writes=[psv])
                        vob = vobr.next()
                        copy_out(vob[:, :], vob, psv[:, :], psv)
                        tsl = slice(t0 + a * 128, t0 + (a + 1) * 128)
                        if blk == 0:
                            P.dma("pool", vtok[VA:VA + 4, tsl, :].rearrange("h t d -> t h d"), vob[:, :].rearrange("t (h d) -> t h d", h=4), reads=[vob])
                        else:
                            P.dma("pool", vtok[VB:VB + 2, tsl, :].rearrange("h t d -> t h d"), vob[:, 0:256].rearrange("t (h d) -> t h d", h=2), reads=[vob])
                            P.dma("pool", vtok[VD:VD + 2, tsl, :].rearrange("h t d -> t h d"), vob[:, 256:512].rearrange("t (h d) -> t h d", h=2), reads=[vob])
            P.barrier()


    def phase2(l):
        need_ctx = l < L - 1
        qgroups = groups if need_ctx else groups[:8]
        with ExitStack() as st:
            ktr = ring("kt2_", 2, [128, TT], BF16, st)
            vtr = ring("vt2_", 2, [128, NKT, 130], BF16, st)
            kpe = sb("kpe2", [128, TT], BF16, st)
            qr = ring("q2_", 3, [128, 512], BF16, st)
            qpr = ring("qp2_", 3, [128, 512], BF16, st)
            ptr = ring("pt2_", 4, [128, 512], BF16, st)
            onr = ring("on2_", 4, [128, 128], BF16, st)
            mor = ring("mo2_", 3, [128, 512], BF16, st)
            rir = ring("ri2_", 8, [128, 1], F32, st)
            amask = sb("amask", [128, 4, 896], F32, st)
            gtr = ring("gt2_", 2, [128, 896], F32, st)
            ggr = ring("gg2_", 2, [128, 896], F32, st)
            ger = ring("ge2_", 2, [128, 896], F32, st)
            psT = PS[6]
            psT_bf = psT[:, :].bitcast(BF16)
            for vt in vtr.tiles:
                P.op("dve", lambda e: e.memset(vt[:, :, 128:129], 1.0), writes=[vt])
                P.op("dve", lambda e: e.memset(vt[:, :, 129:130], 0.0), writes=[vt])
            P.dma("sp", amask[:], amask_d.rearrange("a p c -> p a c"), writes=[amask])
            P.dma("sp", kpe[:], fmT[KPEC], writes=[kpe])

            def load_kv(kidx, vidx):
                kt = ktr.next()
                P.dma("sp", kt[:], fmT[kidx], writes=[kt])
                vt = vtr.next()
                P.dma("sp", vt[:, :, 0:128], vtok[vidx].rearrange("(kt p) d -> p kt d", p=128), writes=[vt])
                return kt, vt

            def load_q(idx, t0, N, r=None):
                q = (r or qr).next()
                P.dma("sp", q[:, :N], fmT[idx, :, t0:t0 + N], writes=[q])
                return q

            def fin_tile(O, col, sink_ap=None):
                ri = rir.next()
                if sink_ap is not None:
                    P.op("dve", lambda e: e.tensor_scalar(out=ri[:], in0=O[:, 128:129], scalar1=sink_ap, scalar2=None, op0=ALU.add), reads=[O, esink], writes=[ri])
                    P.op("dve", lambda e: e.reciprocal(out=ri[:], in_=ri[:]), reads=[ri], writes=[ri])
                else:
                    P.op("dve", lambda e: e.reciprocal(out=ri[:], in_=O[:, 128:129]), reads=[O], writes=[ri])
                on = onr.next()
                P.op("act", lambda e: e.activation(out=on[:], in_=O[:, 0:128], func=AF.Copy, scale=ri[:]), reads=[O, ri], writes=[on])
                P.op("pe", lambda e: e.transpose(out=psT_bf[:, col:col + 128], in_=on[:], identity=identb[:]), reads=[on, identb], writes=[psT])

            def fin_group(head, t0, N):
                mo = mor.next()
                P.op("dve", lambda e: e.tensor_copy(out=mo[:, :N], in_=psT_bf[:, :N]), reads=[psT], writes=[mo])
                P.dma("pool", mixT[head, :, t0:t0 + N], mo[:, :N], reads=[mo])

            sring = Ring([PS[0], PS[1], PS[7]])
            Od = [PS[2], PS[3], PS[4], PS[5]]

            def dense(ktiles, N, smm, scale, vt):
                nq = N // 128
                n = len(ktiles)
                sbank = [None] * n

                def issue_s(i):
                    sbank[i] = sring.next()
                    smm(ktiles[i], sbank[i])

                issue_s(0)
                if n > 1:
                    issue_s(1)
                for i, ktile in enumerate(ktiles):
                    s_ = sbank[i]
                    pt = ptr.next()
                    P.op("act", lambda e: e.activation(out=pt[:, :N], in_=s_[:, :N], func=AF.Exp, scale=scale), reads=[s_], writes=[pt])
                    if i + 2 < n:
                        issue_s(i + 2)
                    for a in range(nq):
                        P.op("pe", lambda e: e.matmul(Od[a][:, 0:130], lhsT=pt[:, a * 128:(a + 1) * 128], rhs=vt[:, ktile, :], start=(i == 0), stop=(i == n - 1)),
                             reads=[pt, vt], writes=[Od[a]])

            def run_dense(kind):
                heads = range(4)
                kt = vt = None
                for h in heads:
                    if kind == "D":
                        if h % 2 == 0:
                            kt, vt = load_kv(KD + h // 2, VD + h // 2)
                        head = 12 + h
                    else:
                        kt, vt = load_kv(KNC + h, VC + h)
                        head = 8 + h
                    for (t0, N, which) in qgroups:
                        ktiles = list(range(NKT)) if which == 0 else [32, 33]
                        if kind == "D":
                            q = load_q(QD + h, t0, N)

                            def smm(ktile, s_):
                                P.op("pe", lambda e: e.matmul(s_[:, :N], lhsT=kt[:, ktile * 128:(ktile + 1) * 128], rhs=q[:, :N], start=True, stop=True),
                                     reads=[kt, q], writes=[s_])
                            dense(ktiles, N, smm, SC128, vt)
                        else:
                            q = load_q(QNC + h, t0, N)
                            qp = load_q(QPEC + h // 2, t0, N, qpr)
                            lo = (h % 2) * 64

                            def smm(ktile, s_):
                                P.op("pe", lambda e: e.matmul(s_[:, :N], lhsT=kt[:, ktile * 128:(ktile + 1) * 128], rhs=q[:, :N], start=True, stop=False),
                                     reads=[kt, q], writes=[s_])
                                P.op("pe", lambda e: e.matmul(s_[:, :N], lhsT=kpe[lo:lo + 64, ktile * 128:(ktile + 1) * 128], rhs=qp[lo:lo + 64, :N], start=False, stop=True),
                                     reads=[kpe, qp], writes=[s_])
                            dense(ktiles, N, smm, SC192, vt)
                        for a in range(N // 128):
                            fin_tile(Od[a], a * 128)
                        fin_group(head, t0, N)

            ssets = Ring([(PS[0], PS[1]), (PS[2], PS[3])])
            Ow = Ring([PS[4], PS[5], PS[7]])

            wjobs = []

            def window_tile(kt, vt, qf, a, blocks, bias_fn, sink_ap, col, post=None):
                nb = len(blocks)
                nA = min(nb, 4)
                nB = nb - nA

                def stage1():
                    q = qf()
                    bA, bB = ssets.next()
                    for i, ktile in enumerate(blocks):
                        bank, c = (bA, i * 128) if i < 4 else (bB, (i - 4) * 128)
                        P.op("pe", lambda e: e.matmul(bank[:, c:c + 128], lhsT=kt[:, ktile * 128:(ktile + 1) * 128], rhs=q[:, a * 128:(a + 1) * 128], start=True, stop=True),
                             reads=[kt, q], writes=[bank])
                    bias_fn(bA, bB)
                    ptA = ptr.next()
                    P.op("act", lambda e: e.activation(out=ptA[:, :nA * 128], in_=bA[:, :nA * 128], func=AF.Exp, scale=SC128), reads=[bA], writes=[ptA])
                    ptB = None
                    if nB:
                        ptB = ptr.next()
                        P.op("act", lambda e: e.activation(out=ptB[:, :nB * 128], in_=bB[:, :nB * 128], func=AF.Exp, scale=SC128), reads=[bB], writes=[ptB])
                    return ptA, ptB

                def stage2(st_):
                    ptA, ptB = st_
                    O = Ow.next()
                    for i, ktile in enumerate(blocks):
                        pt, c = (ptA, i * 128) if i < 4 else (ptB, (i - 4) * 128)
                        P.op("pe", lambda e: e.matmul(O[:, 0:130], lhsT=pt[:, c:c + 128], rhs=vt[:, ktile, :], start=(i == 0), stop=(i == nb - 1)),
                             reads=[pt, vt], writes=[O])
                    fin_tile(O, col, sink_ap)
                    if post is not None:
                        post()

                wjobs.append((stage1, stage2))

            def run_wjobs():
                prev = None
                for (s1, s2) in wjobs:
                    st_ = s1()
                    if prev is not None:
                        prev[0](prev[1])
                    prev = (s2, st_)
                if prev is not None:
                    prev[0](prev[1])
                del wjobs[:]

            def addbias(bank, c0, c1, tab, tab_ap):
                P.op("dve", lambda e: e.tensor_tensor(out=bank[:, c0:c1], in0=bank[:, c0:c1], in1=tab_ap, op=ALU.add), reads=[bank, tab], writes=[bank])

            def run_A():
                for h in range(4):
                    kt, vt = load_kv(KA + h, VA + h)
                    gt = gtr.next()
                    P.dma("sp", gt[:], gtab_d[l, h], writes=[gt])
                    gg, ge = ggr.next(), ger.next()
                    for (g_, mi) in ((gg, 0), (ge, 2)):
                        P.op("dve", lambda e: e.tensor_tensor(out=g_[:], in0=gt[:], in1=amask[:, mi, :], op=ALU.mult), reads=[gt, amask], writes=[g_])
                        P.op("dve", lambda e: e.tensor_tensor(out=g_[:], in0=g_[:], in1=amask[:, mi + 1, :], op=ALU.add), reads=[g_, amask], writes=[g_])
                    for (t0, N, which) in qgroups:
                        qh_ = {}
                        q = (lambda qh_=qh_, h=h, t0=t0, N=N: qh_["q"] if "q" in qh_ else qh_.setdefault("q", load_q(QA + h, t0, N)))
                        for a in range(N // 128):
                            qt = t0 // 128 + a
                            post = (lambda h=h, t0=t0, N=N: fin_group(h, t0, N)) if a == N // 128 - 1 else None
                            if which == 1:
                                window_tile(kt, vt, q, a, [32, 33], lambda bA, bB: None, None, a * 128, post)
                                continue
                            if 2 <= qt <= 29:
                                d0, nw, tab = 2, 5, gg
                            elif qt == 0:
                                d0, nw, tab = 3, 4, ge
                            elif qt == 1:
                                d0, nw, tab = 2, 4, ge
                            elif qt == 30:
                                d0, nw, tab = 1, 4, ge
                            else:
                                d0, nw, tab = 0, 4, ge
                            m0 = 6 - 2 * d0
                            blocks = [qt + d0 - i for i in range(nw)] + [32, 33]

                            def bias_fn(bA, bB, m0=m0, nw=nw, tab=tab):
                                addbias(bA, 0, 512, tab, tab[:, m0 * 64:m0 * 64 + 512])
                                if nw == 5:
                                    addbias(bB, 0, 128, tab, tab[:, (m0 + 8) * 64:(m0 + 10) * 64])
                            window_tile(kt, vt, q, a, blocks, bias_fn, None, a * 128, post)
                    run_wjobs()

            def run_B():
                kt = vt = None
                for h in range(4):
                    if h % 2 == 0:
                        kt, vt = load_kv(KB + h // 2, VB + h // 2)
                    sink_ap = esink[:, l * 4 + h:l * 4 + h + 1]
                    for (t0, N, which) in qgroups:
                        qh_ = {}
                        q = (lambda qh_=qh_, h=h, t0=t0, N=N: qh_["q"] if "q" in qh_ else qh_.setdefault("q", load_q(QB + h, t0, N)))
                        for a in range(N // 128):
                            qt = t0 // 128 + a
                            post = (lambda h=h, t0=t0, N=N: fin_group(4 + h, t0, N)) if a == N // 128 - 1 else None
                            if which == 1:
                                window_tile(kt, vt, q, a, [32, 33], lambda bA, bB: None, sink_ap, a * 128, post)
                                continue
                            masked = ([qt - 1] if qt > 0 else []) + ([qt + 1] if qt < 31 else [])
                            blocks = masked + [qt, 32, 33]

                            def bias_fn(bA, bB, qt=qt):
                                if 0 < qt < 31:
                                    addbias(bA, 0, 256, trimask, trimask[:, 0:256])
                                elif qt == 0:
                                    addbias(bA, 0, 128, trimask, trimask[:, 128:256])
                                else:
                                    addbias(bA, 0, 128, trimask, trimask[:, 0:128])
                            window_tile(kt, vt, q, a, blocks, bias_fn, sink_ap, a * 128, post)
                    run_wjobs()

            run_A()
            run_B()
            run_dense("C")
            run_dense("D")
            P.barrier()


    def phase3(l):
        grp = groups if l < L - 1 else groups[:8]
        cv_step(len(cvq))
        if l + 1 < L:
            convert_small(l + 1)
            cvq.extend(ffn_pieces(l + 1))
        with ExitStack() as st:
            mr = ring("m3_", 2, [128, KC, 512], BF16, st)
            ztr = ring("z3_", 2, [128, KC, 512], F32, st, nsub=KC)
            wr = ring("w3_", 4, [128, KC, 128], BF16, st)
            hbr = ring("hb3_", 2, [128, KC, 512], BF16, st, nsub=KC)
            pools = (ring("sq3_", 3, [128, 512], BF16, st), ring("zb3_", 3, [128, 512], BF16, st), ring("st3_", 6, [128, 512], F32, st))
            tp = ring("t3_", 3, [128, 512], F32, st)
            mainr = Ring([PS[0], PS[1], PS[2], PS[3]])
            wt_ = wbt[("w_out", l, 0)]
            for (t0, N, which) in grp:
                mt = mr.next()
                P.dma("sp", mt[:, :, :N], mixT[:, :, t0:t0 + N].rearrange("h p t -> p h t"), writes=[mt])
                zt = ztr.next()
                P.dma("sp", zt[:, :, :N], xT[:, :, t0:t0 + N].rearrange("kc p t -> p kc t"), writes=zt.sub)
                for cc in range(KC):
                    w = wr.next()
                    P.dma("sp", w[:], wb["w_out"][l, cc].rearrange("p (kc n) -> p kc n", kc=KC), reads=[wt_], writes=[w])
                    ps = mainr.next()
                    for kc in range(KC):
                        P.op("pe", lambda e: e.matmul(ps[:, :N], lhsT=w[:, kc, :], rhs=mt[:, kc, :N], start=(kc == 0), stop=(kc == KC - 1)),
                             reads=[w, mt], writes=[ps])
                    P.op("dve", lambda e: e.scalar_tensor_tensor(out=zt[:, cc, :N], in0=ps[:, :N], scalar=mod(l, 2, cc, which), in1=zt[:, cc, :N],
                                                                op0=ALU.mult, op1=ALU.add), reads=[ps, zt.sub[cc], modv], writes=[zt.sub[cc]])
                Mt, Rt = ln_stats(zt, N, EPS / ALPHA ** 2, pools, PS[4], PS[5])
                for kc in range(KC):
                    ln_apply(zt, kc, N, Mt, Rt, vcol("ln1_g", l, kc), vcol("ln1_b", l, kc), zt[:, kc, :N], zt.sub[kc], tp)
                P.dma("pool", xT[:, :, t0:t0 + N].rearrange("kc p t -> p kc t"), zt[:, :, :N], reads=zt.sub)
                Mt, Rt = ln_stats(zt, N, EPS, pools, PS[6], PS[7])
                hb = hbr.next()
                for kc in range(KC):
                    ln_apply(zt, kc, N, Mt, Rt, mod(l, 4, kc, which), mod(l, 3, kc, which), hb[:, kc, :N], hb.sub[kc], tp)
                c0 = h2col(t0, which) + 1
                P.dma("pool", h2T[:, :, c0:c0 + N].rearrange("kc p t -> p kc t"), hb[:, :, :N], reads=hb.sub)
            P.barrier()

    def phase4(l):
        last = (l == L - 1)
        grp = [(456 * n, 456, 0) for n in range(8)] + [(3648, 448, 0)]
        if not last:
            grp.append((SEQ, CTX, 1))
        with ExitStack() as st:
            h2 = sb("h4", [128, KC, 512], BF16, st)
            zt = sb("z4", [128, KC, 512], F32, st, nsub=KC)
            act = sb("act4", [128, FC, 512], BF16, st)
            hb = sb("hb4", [128, KC, 512], BF16, st, nsub=KC) if not last else None
            wgr = ring("wg4_", 3, [128, KC, 128], BF16, st)
            wur = ring("wu4_", 3, [128, KC, 128], BF16, st)
            wdr = ring("wd4_", 2, [128, FC, 128], BF16, st)
            gbr = ring("gb4_", 3, [128, 512], F32, st)
            a1r = ring("a14_", 3, [128, 512], F32, st)
            sr = ring("s4_", 2, [128, 512], F32, st)
            pools = (ring("sq4_", 2, [128, 512], BF16, st), ring("zb4_", 2, [128, 512], BF16, st), ring("st4_", 3, [128, 512], F32, st))
            tp = ring("t4_", 3, [128, 512], F32, st)
            ost = sb("ost4", [128, D], F32, st) if last else None
            gr = Ring([PS[0], PS[1]])
            ur = Ring([PS[2], PS[3]])
            dr = Ring([PS[0], PS[1], PS[2], PS[3]])
            o_w, _ = VEC[("conv_w", l)]
            o_b, _ = VEC[("conv_b", l)]

            def make_tail(t0, N, which):
                def tail():
                    if not last:
                        P.dma("pool", xT[:, :, t0:t0 + N].rearrange("kc p t -> p kc t"), zt[:, :, :N], reads=zt.sub)
                        Mt, Rt = ln_stats(zt, N, EPS, pools, PS[4], PS[7])
                        for kc in range(KC):
                            ln_apply(zt, kc, N, Mt, Rt, mod(l + 1, 1, kc, which), mod(l + 1, 0, kc, which), hb[:, kc, :N], hb.sub[kc], tp)
                        P.dma("pool", hT[:, :, t0:t0 + N].rearrange("kc p t -> p kc t"), hb[:, :, :N], reads=hb.sub)
                    else:
                        for c_ in range(0, N, 128):
                            w_ = min(128, N - c_)
                            for k4 in range(4):
                                ps = dr.next()
                                for i in range(4):
                                    kc = k4 * 4 + i
                                    P.op("pe", lambda e: e.transpose(out=ps[0:w_, i * 128:(i + 1) * 128], in_=zt[:, kc, c_:c_ + w_], identity=identf[:]),
                                         reads=[zt.sub[kc], identf], writes=[ps])
                                if k4 % 2 == 0:
                                    P.op("act", lambda e: e.activation(out=ost[0:w_, k4 * 512:(k4 + 1) * 512], in_=ps[0:w_, :], func=AF.Copy), reads=[ps], writes=[ost])
                                else:
                                    P.op("dve", lambda e: e.tensor_copy(out=ost[0:w_, k4 * 512:(k4 + 1) * 512], in_=ps[0:w_, :]), reads=[ps], writes=[ost])
                            P.dma("pool", out_d[t0 + c_:t0 + c_ + w_, :], ost[0:w_, :], reads=[ost])
                return tail

            pending = None
            for (t0, N, which) in grp:
                cv_step(2)
                c0 = h2col(t0, which)
                P.dma("sp", h2[:, :, :N + 2], h2T[:, :, c0:c0 + N + 2].rearrange("kc p t -> p kc t"), writes=[h2])
                if pending is None:
                    P.dma("sp", zt[:, :, :N], xT[:, :, t0:t0 + N].rearrange("kc p t -> p kc t"), writes=zt.sub)
                for j in range(FC):
                    if j == 3 and pending is not None:
                        pending()
                        pending = None
                        P.dma("sp", zt[:, :, :N], xT[:, :, t0:t0 + N].rearrange("kc p t -> p kc t"), writes=zt.sub)
                    wg = wgr.next()
                    P.dma("sp", wg[:], wb["w_gate"][l, j].rearrange("p (kc n) -> p kc n", kc=KC), reads=[wdep("w_gate", l, j)], writes=[wg])
                    wu = wur.next()
                    P.dma("sp", wu[:], wb["w_up"][l, j].rearrange("p (kc n) -> p kc n", kc=KC), reads=[wdep("w_up", l, j)], writes=[wu])
                    psG, psU = gr.next(), ur.next()
                    for kc in range(KC):
                        P.op("pe", lambda e: e.matmul(psG[:, :N + 2], lhsT=wg[:, kc, :], rhs=h2[:, kc, 0:N + 2], start=(kc == 0), stop=(kc == KC - 1)),
                             reads=[wg, h2], writes=[psG])
                    for kc in range(KC):
                        P.op("pe", lambda e: e.matmul(psU[:, :N], lhsT=wu[:, kc, :], rhs=h2[:, kc, 1:N + 1], start=(kc == 0), stop=(kc == KC - 1)),
                             reads=[wu, h2], writes=[psU])
                    gb = gbr.next()
                    P.op("act", lambda e: e.activation(out=gb[:, :N + 2], in_=psG[:, :N + 2], func=AF.Copy), reads=[psG], writes=[gb])
                    a1 = a1r.next()
                    w0 = vecs[:, o_w + j * 3 + 0:o_w + j * 3 + 1]
                    w1 = vecs[:, o_w + j * 3 + 1:o_w + j * 3 + 2]
                    w2 = vecs[:, o_w + j * 3 + 2:o_w + j * 3 + 3]
                    cb = vecs[:, o_b + j:o_b + j + 1]
                    P.op("act", lambda e: e.activation(out=a1[:, :N], in_=gb[:, 1:N + 1], func=AF.Identity, scale=w1, bias=cb), reads=[gb, vecs], writes=[a1])
                    P.op("dve", lambda e: e.scalar_tensor_tensor(out=a1[:, :N], in0=gb[:, 0:N], scalar=w0, in1=a1[:, :N], op0=ALU.mult, op1=ALU.add),
                         reads=[gb, a1, vecs], writes=[a1])
                    P.op("dve", lambda e: e.scalar_tensor_tensor(out=a1[:, :N], in0=gb[:, 2:N + 2], scalar=w2, in1=a1[:, :N], op0=ALU.mult, op1=ALU.add),
                         reads=[gb, a1, vecs], writes=[a1])
                    sg = sr.next()
                    P.op("act", lambda e: e.activation(out=sg[:, :N], in_=a1[:, :N], func=AF.Silu), reads=[a1], writes=[sg])
                    P.op("dve", lambda e: e.tensor_tensor(out=act[:, j, :N], in0=sg[:, :N], in1=psU[:, :N], op=ALU.mult), reads=[sg, psU], writes=[act])
                for cc in range(KC):
                    wd = wdr.next()
                    P.dma("sp", wd[:], wb["w_down"][l, cc].rearrange("p (j n) -> p j n", j=FC), reads=[wdep("w_down", l, cc)], writes=[wd])
                    ps = dr.next()
                    for j in range(FC):
                        P.op("pe", lambda e: e.matmul(ps[:, :N], lhsT=wd[:, j, :], rhs=act[:, j, :N], start=(j == 0), stop=(j == FC - 1)),
                             reads=[wd, act], writes=[ps])
                    P.op("dve", lambda e: e.scalar_tensor_tensor(out=zt[:, cc, :N], in0=ps[:, :N], scalar=mod(l, 5, cc, which), in1=zt[:, cc, :N],
                                                                op0=ALU.mult, op1=ALU.add), reads=[ps, zt.sub[cc], modv], writes=[zt.sub[cc]])
                Mt, Rt = ln_stats(zt, N, EPS / ALPHA ** 2, pools, PS[5], PS[6])
                for kc in range(KC):
                    ln_apply(zt, kc, N, Mt, Rt, vcol("ln2_g", l, kc), vcol("ln2_b", l, kc), zt[:, kc, :N], zt.sub[kc], tp)
                pending = make_tail(t0, N, which)
            pending()
            P.barrier()

    for l in range(L):
        for nm, ph in (("p1", phase1), ("p2", phase2), ("p3", phase3), ("p4", phase4)):
            ph(l)
            if stop_after == "%s_%d" % (nm, l):
                return finish()
    return finish()


_NC_CACHE = {}


def kernel(**inp):
    inp = {k: np.asarray(v) for k, v in inp.items()}
    sh = prep_shared(inp)
    if "nc" not in _NC_CACHE:
        _NC_CACHE["nc"] = build()[0]
    nc = _NC_CACHE["nc"]
    B = inp["x"].shape[0]
    in_maps = []
    for b in range(B):
        m = dict(sh)
        m["xin"] = np.ascontiguousarray(np.concatenate([inp["x"][b], inp["ctx"][b]], 0))
        cv = np.stack([inp["c"][b].reshape(16, 128).T, inp["c_ctx"].reshape(16, 128).T], -1).reshape(128, 32)
        m["cvec"] = np.ascontiguousarray(cv)
        in_maps.append(m)
    res = run_bass_kernel_spmd(nc, in_maps, core_ids=list(range(B)))
    return np.stack([np.asarray(r["out"]) for r in res.results], 0).astype(np.float32)
```

```python
import numpy as np
import ml_dtypes
from contextlib import ExitStack
import concourse.bass as bass
import concourse.mybir as mybir
from concourse.bass_utils import run_bass_kernel_spmd

F32 = mybir.dt.float32
BF16 = mybir.dt.bfloat16
AF = mybir.ActivationFunctionType
ALU = mybir.AluOpType

D = 2048
KC = 16
SEQ = 4096
CTX = 256
TT = SEQ + CTX
NKT = TT // 128
L = 2
DFF = 5632
FC = 44
GRID_W = 64
ALPHA = (2 * L) ** 0.25
EPS = 1e-6
NEG = -1e30
SC128 = 128 ** -0.5
SC192 = 192 ** -0.5
NCORES = 4

QA, KA, QB, KB, QD, KD, QNC, QPEC, KNC, KPEC = 0, 4, 8, 12, 14, 18, 20, 24, 26, 30
NFM = 31
VA, VB, VD, VC = 0, 4, 6, 8
NVT = 12
H2W = SEQ + CTX + 4
H2_CTX0 = SEQ + 2

VEC = {}
_o = 0
for _l in range(L):
    for _n, _w in (("b_ada", 96), ("ln1_g", 16), ("ln1_b", 16), ("ln2_g", 16), ("ln2_b", 16),
                   ("conv_w", FC * 3), ("conv_b", FC), ("gqa_qn", 1), ("gqa_kn", 1),
                   ("mla_qn", 3), ("mla_kvn", 1), ("sink", 4)):
        VEC[(_n, _l)] = (_o, _w)
        _o += _w
NVEC = _o


class Buf:
    __slots__ = ("lw", "rd")

    def __init__(self):
        self.lw = None
        self.rd = {}


class Tile:
    def __init__(self, t, nsub=0):
        self.t = t
        self.b = Buf()
        self.sub = [Tile(t) for _ in range(nsub)]

    def __getitem__(self, idx):
        return self.t[idx]


class Ring:
    def __init__(self, tiles):
        self.tiles = tiles
        self.i = 0

    def next(self):
        t = self.tiles[self.i % len(self.tiles)]
        self.i += 1
        return t


class Prog:
    NDMA = 8
    LIM = 28000

    def __init__(self, nc):
        self.nc = nc
        self.eng = {"pe": nc.tensor, "act": nc.scalar, "dve": nc.vector,
                    "pool": nc.gpsimd, "sp": nc.sync}
        self.sem = {}
        self.cnt = {}
        self.cur = {}
        self.nsem = 0
        for k in ("pe", "act", "dve", "pool"):
            self._fresh(k)
        self.dqn = {"sp": 0, "pool": 0, "act": 0}
        for q in self.dqn:
            for i in range(self.NDMA):
                self._fresh(("dma", q, i))
        self.seen = {k: {} for k in self.eng}
        self.ninst = 0
        self.nwait = 0
        self.cvn = 0

    def _fresh(self, base):
        ep = self.cur.get(base, (None, -1))[1] + 1
        key = (base, ep)
        self.cur[base] = key
        self.sem[key] = self.nc.alloc_semaphore("s%d" % self.nsem)
        self.nsem += 1
        self.cnt[key] = 0
        return key

    def _wait(self, e, ev):
        if ev is None:
            return
        key, c = ev
        if key[0] == e and e == "pe":
            return
        if self.seen[e].get(key, 0) >= c:
            return
        self.seen[e][key] = c
        val = c * 16 if isinstance(key[0], tuple) else c
        self.eng[e].wait_ge(self.sem[key], val)
        self.nwait += 1

    def _deps(self, e, reads, writes):
        for t in reads:
            self._wait(e, t.b.lw)
        for t in writes:
            self._wait(e, t.b.lw)
            for k, c in t.b.rd.items():
                self._wait(e, (k, c))

    def _commit(self, ev, reads, writes):
        k, c = ev
        for t in reads:
            t.b.rd[k] = c
        for t in writes:
            t.b.lw = ev
            t.b.rd = {}

    def op(self, e, fn, reads=(), writes=()):
        key = self.cur[e]
        if self.cnt[key] >= self.LIM:
            key = self._fresh(e)
        self._deps(e, reads, writes)
        ins = fn(self.eng[e])
        self.cnt[key] += 1
        ins.then_inc(self.sem[key], 1)
        self._commit((key, self.cnt[key]), reads, writes)
        self.ninst += 1
        return ins

    def dma(self, q, out, in_, reads=(), writes=(), **kw):
        i = self.dqn[q] % self.NDMA
        self.dqn[q] += 1
        base = ("dma", q, i)
        key = self.cur[base]
        if self.cnt[key] > 0:
            self._wait(q, (key, self.cnt[key]))
        if self.cnt[key] * 16 >= self.LIM:
            key = self._fresh(base)
        self._deps(q, reads, writes)
        ins = self.eng[q].dma_start(out=out, in_=in_, **kw)
        self.cnt[key] += 1
        ins.then_inc(self.sem[key], 16)
        self._commit((key, self.cnt[key]), reads, writes)
        self.ninst += 1
        return ins

    def cvt(self, out, in_, tile):
        key = (("cv", self.cvn), 0)
        self.cvn += 1
        self.sem[key] = self.nc.alloc_semaphore("cv%d" % self.cvn)
        self.nsem += 1
        ins = self.nc.gpsimd.dma_start(out=out, in_=in_)
        ins.then_inc(self.sem[key], 16)
        self.cnt[key] = 1
        tile.b.lw = (key, 1)
        self.ninst += 1

    def barrier(self):
        for e in self.eng:
            for key, c in list(self.cnt.items()):
                if c > 0 and key[0] != e and not (isinstance(key[0], tuple) and key[0][0] == "cv"):
                    self._wait(e, (key, c))


def _rope_tables():
    t = np.arange(SEQ)
    row, col = (t // GRID_W).astype(np.float64), (t % GRID_W).astype(np.float64)

    def tab(dim):
        seg = dim // 2
        half = seg // 2
        C = np.ones((dim, TT))
        S = np.zeros((dim, TT))
        perm = np.zeros(dim, np.int64)
        inv = 10000.0 ** (-np.arange(half, dtype=np.float32) / half)
        for d in range(dim):
            s, e = d // seg, d % seg
            i = e % half
            first = e < half
            pos = row if s == 0 else col
            ang = (pos.astype(np.float32) * np.float32(inv[i])).astype(np.float32)
            C[d, :SEQ] = np.cos(ang)
            S[d, :SEQ] = -np.sin(ang) if first else np.sin(ang)
            perm[d] = d + half if first else d - half
        return C.astype(np.float32), S.astype(np.float32), perm

    C128, S128, p128 = tab(128)
    C64, S64, p64 = tab(64)
    rope128 = np.stack([C128, S128])
    rope64 = np.stack([np.concatenate([C64, C64], 0), np.concatenate([S64, S64], 0)])
    Pm128 = np.zeros((128, 128), np.float32)
    Pm128[p128, np.arange(128)] = 1.0
    Pm64 = np.zeros((128, 128), np.float32)
    for hh in range(2):
        Pm64[hh * 64 + p64, hh * 64 + np.arange(64)] = 1.0
    return rope128, rope64, np.stack([Pm128, Pm64])


def _amask():
    p = np.arange(128)
    a, kcol = p // 64, p % 64
    m = np.arange(14)
    j = 6 - m
    dr = j[None, :] + a[:, None]
    qcol = np.arange(64)
    c0 = np.clip(qcol - 8, 0, 48)
    col_in = (kcol[:, None] >= c0[None, :]) & (kcol[:, None] < c0[None, :] + 16)
    v_edge = np.broadcast_to(col_in[:, None, :], (128, 14, 64))
    v_gen = v_edge & ((dr >= -4) & (dr <= 3))[:, :, None]
    out = []
    for v in (v_gen, v_edge):
        out.append((v / SC128).astype(np.float32).reshape(128, 14 * 64))
        out.append(np.where(v, 0.0, NEG).astype(np.float32).reshape(128, 14 * 64))
    return np.stack(out)


def _gtab(rpb):
    p = np.arange(128)
    a, kcol = p // 64, p % 64
    j = 6 - np.arange(14)
    dr = j[None, :] + a[:, None]
    qcol = np.arange(64)
    dc = np.clip(kcol[:, None] - qcol[None, :] + 15, 0, 30)
    g = rpb[:, :, (dr + 7)[:, :, None], dc[:, None, :]]
    return np.ascontiguousarray(g.reshape(L, 4, 128, 14 * 64))


def _chunk_w(w, kc):
    Lw, K, N = w.shape
    return np.ascontiguousarray(w.reshape(Lw, kc, 128, N // 128, 128).transpose(0, 3, 2, 1, 4).reshape(Lw, N // 128, 128, kc * 128))


_CONST_CACHE = {}


def _consts():
    if not _CONST_CACHE:
        rope128, rope64, perm = _rope_tables()
        k = np.arange(128)
        tri = np.stack([np.where(k[:, None] >= k[None, :], 0.0, NEG), np.where(k[:, None] <= k[None, :], 0.0, NEG)], 1)
        _CONST_CACHE.update(rope128=rope128, rope64=rope64, perm=perm, amask=_amask(),
                            identf=np.eye(128, dtype=np.float32),
                            identb=np.eye(128, dtype=np.float32).astype(ml_dtypes.bfloat16),
                            trimask=np.ascontiguousarray(tri.reshape(128, 256).astype(np.float32)))
    return _CONST_CACHE


def prep_shared(inp):
    sh = {}
    w_in = inp["w_in"]
    sizes = [512, 512, 512, 512, 256, 256, 384, 128, 64, 512, 256, 256]
    offs = np.cumsum([0] + sizes)
    seg = lambda i: w_in[:, :, offs[i]:offs[i + 1]]
    w_in_r = np.concatenate([seg(0), seg(1), seg(2), seg(3), seg(4), seg(5), seg(6), seg(7), seg(9), seg(10), seg(11),
                             seg(8), np.zeros((L, D, 64), np.float32)], axis=2)
    sh["w_in"] = _chunk_w(w_in_r, KC)
    sh["w_out"] = _chunk_w(inp["w_out"], KC)
    sh["w_gate"] = _chunk_w(inp["ffn_w_gate"], KC)
    sh["w_up"] = _chunk_w(inp["ffn_w_up"], KC)
    sh["w_down"] = _chunk_w(inp["ffn_w_down"], FC)
    uq = inp["mla_w_uq"].reshape(L, 384, 4, 192)
    uq = np.concatenate([uq[..., :128].reshape(L, 384, 512), uq[..., 128:].reshape(L, 384, 256)], -1)
    sh["w_uq"] = np.ascontiguousarray(uq.reshape(L, 3, 128, 768).transpose(0, 2, 1, 3).reshape(L, 128, 3 * 768))
    ukv = inp["mla_w_ukv"].reshape(L, 128, 4, 256)
    sh["w_ukv"] = np.ascontiguousarray(np.concatenate([ukv[..., :128].reshape(L, 128, 512), ukv[..., 128:].reshape(L, 128, 512)], -1))
    sh["w_ada"] = np.ascontiguousarray(inp["w_ada"].reshape(L * D, 6 * D))
    vec = np.zeros((128, NVEC), np.float32)

    def put(name, l, arr):
        o, w = VEC[(name, l)]
        vec[:, o:o + w] = arr

    for l in range(L):
        put("b_ada", l, inp["b_ada"][l].reshape(96, 128).T)
        for n in ("ln1_g", "ln1_b", "ln2_g", "ln2_b"):
            put(n, l, inp[n][l].reshape(16, 128).T)
        put("conv_w", l, inp["ffn_conv_w"][l].reshape(3, FC, 128).transpose(2, 1, 0).reshape(128, FC * 3))
        put("conv_b", l, inp["ffn_conv_b"][l].reshape(FC, 128).T)
        put("gqa_qn", l, inp["gqa_q_norm"][l].reshape(128, 1))
        put("gqa_kn", l, inp["gqa_k_norm"][l].reshape(128, 1))
        put("mla_qn", l, inp["mla_q_norm"][l].reshape(3, 128).T)
        put("mla_kvn", l, inp["mla_kv_norm"][l].reshape(128, 1))
        put("sink", l, np.broadcast_to(inp["swa_sink"][l][None, :], (128, 4)))
    sh["vecs"] = vec
    sh["gtab"] = _gtab(inp["na_rpb"])
    sh.update(_consts())
    return sh


def build(dbg=(), stop_after=None):
    nc = bass.Bass("TRN2", target_bir_lowering=False)
    P = Prog(nc)
    es = ExitStack()

    def din(name, shape, dt=F32):
        return nc.dram_tensor(name, list(shape), dt, kind="ExternalInput").ap()

    def dscr(name, shape, dt):
        kind = "ExternalOutput" if name in dbg else "Internal"
        return nc.dram_tensor(name, list(shape), dt, kind=kind).ap()

    xin = din("xin", [TT, D])
    cvec = din("cvec", [128, 32])
    w_ada = din("w_ada", [L * D, 6 * D])
    vecs_d = din("vecs", [128, NVEC])
    w_in_d = din("w_in", [L, 33, 128, 2048])
    w_out_d = din("w_out", [L, 16, 128, 2048])
    w_gate_d = din("w_gate", [L, FC, 128, 2048])
    w_up_d = din("w_up", [L, FC, 128, 2048])
    w_down_d = din("w_down", [L, 16, 128, DFF])
    w_uq_d = din("w_uq", [L, 128, 3 * 768])
    w_ukv_d = din("w_ukv", [L, 128, 1024])
    gtab_d = din("gtab", [L, 4, 128, 14 * 64])
    rope128_d = din("rope128", [2, 128, TT])
    rope64_d = din("rope64", [2, 128, TT])
    perm_d = din("perm", [2, 128, 128])
    amask_d = din("amask", [4, 128, 14 * 64])
    identf_d = din("identf", [128, 128])
    identb_d = din("identb", [128, 128], BF16)
    trimask_d = din("trimask", [128, 256])
    out_d = nc.dram_tensor("out", [SEQ, D], F32, kind="ExternalOutput").ap()

    wb = {}
    wsrc = {"w_in": w_in_d, "w_out": w_out_d, "w_gate": w_gate_d, "w_up": w_up_d, "w_down": w_down_d,
            "w_uq": w_uq_d, "w_ukv": w_ukv_d}
    for n, src in wsrc.items():
        wb[n] = nc.dram_tensor(n + "_bf", list(src.shape), BF16, kind="Internal").ap()
    NPIECE = {"w_in": 1, "w_out": 1, "w_uq": 1, "w_ukv": 1, "w_gate": 4, "w_up": 4, "w_down": 4}
    wbt = {(n, l, p): Tile(None) for n in wsrc for l in range(L) for p in range(NPIECE[n])}

    def wdep(n, l, idx=0):
        per = wsrc[n].shape[1] // NPIECE[n] if NPIECE[n] > 1 else 1
        return wbt[(n, l, idx // per if NPIECE[n] > 1 else 0)]

    xT = dscr("xT", [KC, 128, TT], F32)
    hT = dscr("hT", [KC, 128, TT], BF16)
    h2T = dscr("h2T", [KC, 128, H2W], BF16)
    fmT = dscr("fmT", [NFM, 128, TT], BF16)
    vtok = dscr("vtok", [NVT, TT, 128], BF16)
    mixT = dscr("mixT", [KC, 128, TT], BF16)

    uid = [0]

    def sb(name, shape, dt, stack=None, nsub=0):
        uid[0] += 1
        t = (stack or es).enter_context(nc.sbuf_tensor("sb%d_%s" % (uid[0], name), list(shape), dt))
        return Tile(t, nsub)

    def ring(name, n, shape, dt, stack=None, nsub=0):
        return Ring([sb("%s%d" % (name, i), shape, dt, stack, nsub) for i in range(n)])

    PS = [Tile(es.enter_context(nc.psum_tensor("ps%d" % i, [128, 512], F32))) for i in range(8)]

    def flat2k(ap):
        n = 1
        for s in ap.shape:
            n *= s
        names = " ".join("a%d" % i for i in range(len(ap.shape)))
        return ap.rearrange("%s -> (%s)" % (names, names)).rearrange("(r c) -> r c", c=2048)

    def convert_small(l):
        for n in ("w_in", "w_uq", "w_ukv", "w_out"):
            P.cvt(flat2k(wb[n][l]), flat2k(wsrc[n][l]), wbt[(n, l, 0)])

    def ffn_pieces(l):
        out = []
        for p in range(4):
            for n in ("w_gate", "w_up", "w_down"):
                per = wsrc[n].shape[1] // 4

                def f(n=n, p=p, per=per):
                    P.cvt(flat2k(wb[n][l, p * per:(p + 1) * per]), flat2k(wsrc[n][l, p * per:(p + 1) * per]), wbt[(n, l, p)])
                out.append(f)
        return out

    convert_small(0)
    cvq = ffn_pieces(0)

    def cv_step(n=1):
        for _ in range(n):
            if cvq:
                cvq.pop(0)()

    identf = sb("identf", [128, 128], F32)
    identb = sb("identb", [128, 128], BF16)
    perm = sb("perm", [128, 2, 128], F32)
    trimask = sb("trimask", [128, 256], F32)
    vecs = sb("vecs", [128, NVEC], F32)
    onesb = sb("onesb", [128, 128], BF16)
    ones128 = sb("ones128", [128, 128], BF16)
    ones384 = sb("ones384", [128, 128], BF16)
    modv = sb("modv", [128, L, 96, 2], F32)
    esink = sb("esink", [128, L * 4], F32)
    P.dma("sp", identf[:], identf_d, writes=[identf])
    P.dma("sp", identb[:], identb_d, writes=[identb])
    P.dma("sp", perm[:], perm_d.rearrange("a p c -> p a c"), writes=[perm])
    P.dma("sp", trimask[:], trimask_d, writes=[trimask])
    P.dma("sp", vecs[:], vecs_d, writes=[vecs])
    P.op("dve", lambda e: e.memset(onesb[:], 1.0 / D), writes=[onesb])
    P.op("dve", lambda e: e.memset(ones128[:], 1.0 / 128), writes=[ones128])
    P.op("dve", lambda e: e.memset(ones384[:], 1.0 / 384), writes=[ones384])

    zcol = sb("zcol", [128, KC, 1], BF16)
    P.op("dve", lambda e: e.memset(zcol[:], 0.0), writes=[zcol])
    for c in (0, SEQ + 1, SEQ + 2, H2W - 1):
        P.dma("sp", h2T[:, :, c:c + 1].rearrange("kc p t -> p kc t"), zcol[:], reads=[zcol], allow_slow_non_contiguous=True)

    def vcol(name, l, j=0, w=1):
        o, _ = VEC[(name, l)]
        return vecs[:, o + j:o + j + w]

    def mod(l, s, kc, which):
        return modv[:, l, s * 16 + kc, which:which + 1]

    def ln_stats(zt, N, eps, pools, psM, psV):
        sqp, zbp, stp = pools
        for kc in range(KC):
            sq = sqp.next()
            P.op("act", lambda e: e.activation(out=sq[:, :N], in_=zt[:, kc, :N], func=AF.Square), reads=[zt.sub[kc]], writes=[sq])
            zb = zbp.next()
            P.op("pool" if kc % 2 == 0 else "dve", lambda e: e.tensor_copy(out=zb[:, :N], in_=zt[:, kc, :N]), reads=[zt.sub[kc]], writes=[zb])
            P.op("pe", lambda e: e.matmul(psM[:, :N], lhsT=onesb[:], rhs=zb[:, :N], start=(kc == 0), stop=(kc == KC - 1)),
                 reads=[onesb, zb], writes=[psM])
            P.op("pe", lambda e: e.matmul(psV[:, :N], lhsT=onesb[:], rhs=sq[:, :N], start=(kc == 0), stop=(kc == KC - 1)),
                 reads=[onesb, sq], writes=[psV])
        m2 = stp.next()
        P.op("act", lambda e: e.activation(out=m2[:, :N], in_=psM[:, :N], func=AF.Square), reads=[psM], writes=[m2])
        P.op("dve", lambda e: e.tensor_tensor(out=m2[:, :N], in0=psV[:, :N], in1=m2[:, :N], op=ALU.subtract), reads=[psV, m2], writes=[m2])
        P.op("dve", lambda e: e.tensor_scalar(out=m2[:, :N], in0=m2[:, :N], scalar1=float(eps), scalar2=None, op0=ALU.add), reads=[m2], writes=[m2])
        P.op("act", lambda e: e.activation(out=m2[:, :N], in_=m2[:, :N], func=AF.Sqrt), reads=[m2], writes=[m2])
        P.op("dve", lambda e: e.reciprocal(out=psV[:, :N], in_=m2[:, :N]), reads=[m2], writes=[psV])
        return psM, psV

    def ln_apply(zt, kc, N, Mt, Rt, scale_ap, bias_ap, out_ap, out_tile, tp, extra_reads=()):
        t = tp.next()
        P.op("dve", lambda e: e.tensor_tensor(out=t[:, :N], in0=zt[:, kc, :N], in1=Mt[:, :N], op=ALU.subtract), reads=[zt.sub[kc], Mt], writes=[t])
        P.op("dve", lambda e: e.tensor_tensor(out=t[:, :N], in0=t[:, :N], in1=Rt[:, :N], op=ALU.mult), reads=[t, Rt], writes=[t])
        P.op("act", lambda e: e.activation(out=out_ap, in_=t[:, :N], func=AF.Identity, scale=scale_ap, bias=bias_ap),
             reads=[t, vecs, modv] + list(extra_reads), writes=[out_tile])

    groups = [(g * 512, 512, 0) for g in range(8)] + [(SEQ, CTX, 1)]

    def h2col(t0, which):
        return (t0 if which == 0 else H2_CTX0)

    with ExitStack() as st:
        cv = sb("cv", [128, 32], F32, st)
        scv = sb("scv", [128, 32], F32, st)
        wblk = ring("wblk", 2, [128, KC, 512], F32, st)
        P.dma("sp", cv[:], cvec, writes=[cv])
        P.op("act", lambda e: e.activation(out=scv[:], in_=cv[:], func=AF.Silu), reads=[cv], writes=[scv])
        mrr = ring("mrow", 2, [2, 512], F32, st)
        rowr = Ring([PS[2], PS[3]])
        for l in range(L):
            psA = PS[l]
            psv = psA[:, 0:192].rearrange("p (j w) -> p j w", w=2)
            wl = w_ada[l * D:(l + 1) * D, :].rearrange("(kc p) n -> p kc n", p=128)
            for blk in range(24):
                wt = wblk.next()
                P.dma("sp", wt[:], wl[:, :, blk * 512:(blk + 1) * 512], writes=[wt])
                prow = rowr.next()
                for kc in range(KC):
                    P.op("pe", lambda e: e.matmul(prow[0:2, :], lhsT=scv[:, 2 * kc:2 * kc + 2], rhs=wt[:, kc, :], start=(kc == 0), stop=(kc == KC - 1)),
                         reads=[wt, scv], writes=[prow])
                mr = mrr.next()
                P.op("act", lambda e: e.activation(out=mr[0:2, :], in_=prow[0:2, :], func=AF.Copy), reads=[prow], writes=[mr])
                for j in range(4):
                    P.op("pe", lambda e: e.transpose(out=psv[:, blk * 4 + j, :], in_=mr[0:2, j * 128:(j + 1) * 128], identity=identf[0:2, 0:2]),
                         reads=[mr, identf], writes=[psA])
            o, w = VEC[("b_ada", l)]
            for which in range(2):
                P.op("dve", lambda e: e.tensor_tensor(out=modv[:, l, :, which], in0=psv[:, :, which], in1=vecs[:, o:o + 96], op=ALU.add),
                     reads=[psA, vecs], writes=[modv])
            for s in (1, 4):
                P.op("dve", lambda e: e.tensor_scalar(out=modv[:, l, s * 16:(s + 1) * 16, :], in0=modv[:, l, s * 16:(s + 1) * 16, :],
                                                      scalar1=1.0, scalar2=None, op0=ALU.add), reads=[modv], writes=[modv])
            for s in (2, 5):
                P.op("dve", lambda e: e.tensor_scalar(out=modv[:, l, s * 16:(s + 1) * 16, :], in0=modv[:, l, s * 16:(s + 1) * 16, :],
                                                      scalar1=1.0 / ALPHA, scalar2=None, op0=ALU.mult), reads=[modv], writes=[modv])
            o, w = VEC[("sink", l)]
            P.op("act", lambda e: e.activation(out=esink[:, l * 4:(l + 1) * 4], in_=vecs[:, o:o + 4], func=AF.Exp), reads=[vecs], writes=[esink])
        P.barrier()
    if "modv" in dbg:
        modv_o = nc.dram_tensor("modv_o", [128, L * 96 * 2], F32, kind="ExternalOutput").ap()
        P.dma("sp", modv_o, modv[:].rearrange("p l j w -> p (l j w)"), reads=[modv])

    def finish():
        P.barrier()
        es.close()
        return nc, P

    if stop_after == "ada":
        return finish()

    with ExitStack() as st:
        xr = ring("xin", 2, [128, 4, D], F32, st)
        ztr = ring("zt0", 2, [128, KC, 512], F32, st, nsub=KC)
        hbr = ring("hb0", 2, [128, KC, 512], BF16, st, nsub=KC)
        pools = (ring("sq0", 3, [128, 512], BF16, st), ring("zb0", 3, [128, 512], BF16, st), ring("st0", 6, [128, 512], F32, st))
        tp = ring("t0", 3, [128, 512], F32, st)
        psr = Ring([PS[0], PS[1], PS[2], PS[3]])
        stat_sets = Ring([(PS[4], PS[5]), (PS[6], PS[7])])
        for (t0, N, which) in groups:
            cv_step()
            nt = N // 128
            xt = xr.next()
            P.dma("sp", xt[:, :nt, :], xin[t0:t0 + N, :].rearrange("(a p) d -> p a d", p=128), writes=[xt])
            zt = ztr.next()
            for kc in range(KC):
                ps = psr.next()
                for a in range(nt):
                    P.op("pe", lambda e: e.transpose(out=ps[:, a * 128:(a + 1) * 128], in_=xt[:, a, kc * 128:(kc + 1) * 128], identity=identf[:]),
                         reads=[xt, identf], writes=[ps])
                eng = "act" if kc % 2 == 0 else "dve"
                if eng == "act":
                    P.op("act", lambda e: e.activation(out=zt[:, kc, :N], in_=ps[:, :N], func=AF.Copy), reads=[ps], writes=[zt.sub[kc]])
                else:
                    P.op("dve", lambda e: e.tensor_copy(out=zt[:, kc, :N], in_=ps[:, :N]), reads=[ps], writes=[zt.sub[kc]])
            P.dma("pool", xT[:, :, t0:t0 + N].rearrange("kc p t -> p kc t"), zt[:, :, :N], reads=zt.sub)
            bM, bV = stat_sets.next()
            Mt, Rt = ln_stats(zt, N, EPS, pools, bM, bV)
            hb = hbr.next()
            for kc in range(KC):
                ln_apply(zt, kc, N, Mt, Rt, mod(0, 1, kc, which), mod(0, 0, kc, which), hb[:, kc, :N], hb.sub[kc], tp)
            P.dma("pool", hT[:, :, t0:t0 + N].rearrange("kc p t -> p kc t"), hb[:, :, :N], reads=hb.sub)
        P.barrier()
    if stop_after == "p0":
        return finish()


    def phase1(l):
        with ExitStack() as st:
            hr = ring("h1_", 2, [128, KC, 512], BF16, st)
            wr = ring("w1_", 6, [128, KC, 128], BF16, st)
            wv = sb("wv1", [128, 8, KC, 128], BF16, st)
            wuq = sb("wuq", [128, 3, 768], BF16, st)
            wukv = sb("wukv", [128, 1024], BF16, st)
            rc128 = ring("rc128_", 2, [128, 2, 512], F32, st)
            rc64 = ring("rc64_", 2, [128, 2, 512], F32, st)
            xsr = ring("xs1_", 4, [128, 512], F32, st)
            t1r = ring("t1_", 6, [128, 512], F32, st)
            obr = ring("ob1_", 6, [128, 512], BF16, st)
            sqr = ring("sq1_", 3, [128, 512], BF16, st)
            rsr = ring("rs1_", 4, [128, 512], F32, st)
            vobr = ring("vob1_", 3, [128, 512], BF16, st)
            cqs = sb("cqs", [128, 3, 512], F32, st)
            cqn = sb("cqn", [128, 3, 512], BF16, st)
            ckvn = sb("ckvn", [128, 512], BF16, st)
            mainr = Ring([PS[0], PS[1], PS[2], PS[3]])
            ppr = Ring([PS[4]])
            psR = PS[5]
            vr = Ring([PS[6], PS[7]])
            wt_in = wbt[("w_in", l, 0)]
            wsrc_l = w_in_bf = wb["w_in"]

            def wchunks(c0, n):
                return wb["w_in"][l, c0:c0 + n].rearrange("c p (kc n) -> p c kc n", kc=KC)

            P.dma("sp", wv[:, 0:4], wchunks(8, 4), reads=[wt_in], writes=[wv])
            P.dma("sp", wv[:, 4:6], wchunks(18, 2), reads=[wt_in], writes=[wv])
            P.dma("sp", wv[:, 6:8], wchunks(30, 2), reads=[wt_in], writes=[wv])
            P.dma("sp", wuq[:], wb["w_uq"][l].rearrange("p (kc n) -> p kc n", kc=3), reads=[wbt[("w_uq", l, 0)]], writes=[wuq])
            P.dma("sp", wukv[:], wb["w_ukv"][l], reads=[wbt[("w_ukv", l, 0)]], writes=[wukv])
            cnt = [0]

            for (t0, N, which) in groups:
                cv_step()
                skipq = (l == L - 1 and which == 1)
                nt = N // 128
                ht = hr.next()
                P.dma("sp", ht[:, :, :N], hT[:, :, t0:t0 + N].rearrange("kc p t -> p kc t"), writes=[ht])
                r128 = rc128.next()
                P.dma("sp", r128[:, :, :N], rope128_d[:, :, t0:t0 + N].rearrange("a p t -> p a t"), writes=[r128])
                r64 = rc64.next()
                P.dma("sp", r64[:, :, :N], rope64_d[:, :, t0:t0 + N].rearrange("a p t -> p a t"), writes=[r64])

                def proj(cc, M=128):
                    w = wr.next()
                    P.dma("sp", w[:], wb["w_in"][l, cc].rearrange("p (kc n) -> p kc n", kc=KC), reads=[wt_in], writes=[w])
                    ps = mainr.next()
                    for kc in range(KC):
                        P.op("pe", lambda e: e.matmul(ps[0:M, :N], lhsT=w[:, kc, 0:M], rhs=ht[:, kc, :N], start=(kc == 0), stop=(kc == KC - 1)),
                             reads=[w, ht], writes=[ps])
                    return ps

                def copy_out(dst_ap, dst_tile, src_ap, src_tile):
                    cnt[0] += 1
                    if cnt[0] % 2 == 0:
                        P.op("act", lambda e: e.activation(out=dst_ap, in_=src_ap, func=AF.Copy), reads=[src_tile], writes=[dst_tile])
                    else:
                        P.op("dve", lambda e: e.tensor_copy(out=dst_ap, in_=src_ap), reads=[src_tile], writes=[dst_tile])

                def evac_bf(ps, M=128):
                    ob = obr.next()
                    copy_out(ob[:M, :N], ob, ps[:M, :N], ps)
                    return ob

                def store_fm(idx, ob, M=128, prow=0):
                    P.dma("pool", fmT[idx, prow:prow + M, t0:t0 + N], ob[:M, :N], reads=[ob])

                def rope(xs, M, tab, pmi):
                    pp = ppr.next()
                    P.op("pe", lambda e: e.matmul(pp[:M, :N], lhsT=perm[:M, pmi, :M], rhs=xs[:M, :N], start=True, stop=True),
                         reads=[perm, xs], writes=[pp])
                    t1 = t1r.next()
                    P.op("pool", lambda e: e.tensor_tensor(out=t1[:M, :N], in0=xs[:M, :N], in1=tab[:M, 0, :N], op=ALU.mult), reads=[xs, tab], writes=[t1])
                    t2 = t1r.next()
                    P.op("dve", lambda e: e.tensor_tensor(out=t2[:M, :N], in0=pp[:M, :N], in1=tab[:M, 1, :N], op=ALU.mult), reads=[pp, tab], writes=[t2])
                    ob = obr.next()
                    P.op("dve", lambda e: e.tensor_tensor(out=ob[:M, :N], in0=t1[:M, :N], in1=t2[:M, :N], op=ALU.add), reads=[t1, t2], writes=[ob])
                    return ob

                def rstd_from(psr_tile):
                    m = rsr.next()
                    P.op("dve", lambda e: e.tensor_scalar(out=m[:, :N], in0=psr_tile[:, :N], scalar1=EPS, scalar2=None, op0=ALU.add), reads=[psr_tile], writes=[m])
                    P.op("act", lambda e: e.activation(out=m[:, :N], in_=m[:, :N], func=AF.Sqrt), reads=[m], writes=[m])
                    P.op("dve", lambda e: e.reciprocal(out=m[:, :N], in_=m[:, :N]), reads=[m], writes=[m])
                    return m

                jobs = []
                mcq = []

                def post_plain(idx):
                    return lambda ps: store_fm(idx, evac_bf(ps))

                def post_rope(idx):
                    def f(ps):
                        xs = xsr.next()
                        copy_out(xs[:, :N], xs, ps[:, :N], ps)
                        store_fm(idx, rope(xs, 128, r128, 0))
                    return f

                def post_norm_rope(idx, gname):
                    def f(ps):
                        sq = sqr.next()
                        P.op("act", lambda e: e.activation(out=sq[:, :N], in_=ps[:, :N], func=AF.Square), reads=[ps], writes=[sq])
                        P.op("pe", lambda e: e.matmul(psR[:, :N], lhsT=ones128[:], rhs=sq[:, :N], start=True, stop=True), reads=[ones128, sq], writes=[psR])
                        m = rstd_from(psR)
                        xs = xsr.next()
                        P.op("dve", lambda e: e.scalar_tensor_tensor(out=xs[:, :N], in0=ps[:, :N], scalar=vcol(gname, l), in1=m[:, :N], op0=ALU.mult, op1=ALU.mult),
                             reads=[ps, m, vecs], writes=[xs])
                        store_fm(idx, rope(xs, 128, r128, 0))
                    return f

                def post_cq(i):
                    def f(ps):
                        P.op("act", lambda e: e.activation(out=cqs[:, i, :N], in_=ps[:, :N], func=AF.Copy), reads=[ps], writes=[cqs])
                        sq = sqr.next()
                        P.op("act", lambda e: e.activation(out=sq[:, :N], in_=ps[:, :N], func=AF.Square), reads=[ps], writes=[sq])
                        P.op("pe", lambda e: e.matmul(psR[:, :N], lhsT=ones384[:], rhs=sq[:, :N], start=(i == 0), stop=(i == 2)), reads=[ones384, sq], writes=[psR])
                        if i == 2:
                            mcq.append(rstd_from(psR))
                    return f

                def mla_q_tail(_):
                    m = mcq.pop()
                    for i in range(3):
                        P.op("dve", lambda e: e.scalar_tensor_tensor(out=cqn[:, i, :N], in0=cqs[:, i, :N], scalar=vcol("mla_qn", l, i), in1=m[:, :N],
                                                                    op0=ALU.mult, op1=ALU.mult), reads=[cqs, m, vecs], writes=[cqn])
                    for h in range(4):
                        ps = vr.next()
                        for i in range(3):
                            P.op("pe", lambda e: e.matmul(ps[:, :N], lhsT=wuq[:, i, h * 128:(h + 1) * 128], rhs=cqn[:, i, :N], start=(i == 0), stop=(i == 2)),
                                 reads=[wuq, cqn], writes=[ps])
                        store_fm(QNC + h, evac_bf(ps))
                    for j in range(2):
                        ps = vr.next()
                        for i in range(3):
                            P.op("pe", lambda e: e.matmul(ps[:, :N], lhsT=wuq[:, i, 512 + j * 128:512 + (j + 1) * 128], rhs=cqn[:, i, :N], start=(i == 0), stop=(i == 2)),
                                 reads=[wuq, cqn], writes=[ps])
                        xs = xsr.next()
                        copy_out(xs[:, :N], xs, ps[:, :N], ps)
                        store_fm(QPEC + j, rope(xs, 128, r64, 1))

                def post_ckv(ps):
                    xs = xsr.next()
                    P.op("act", lambda e: e.activation(out=xs[:, :N], in_=ps[:, :N], func=AF.Copy), reads=[ps], writes=[xs])
                    sq = sqr.next()
                    P.op("act", lambda e: e.activation(out=sq[:, :N], in_=ps[:, :N], func=AF.Square), reads=[ps], writes=[sq])
                    P.op("pe", lambda e: e.matmul(psR[:, :N], lhsT=ones128[:], rhs=sq[:, :N], start=True, stop=True), reads=[ones128, sq], writes=[psR])
                    m = rstd_from(psR)
                    P.op("dve", lambda e: e.scalar_tensor_tensor(out=ckvn[:, :N], in0=xs[:, :N], scalar=vcol("mla_kvn", l), in1=m[:, :N], op0=ALU.mult, op1=ALU.mult),
                         reads=[xs, m, vecs], writes=[ckvn])

                def mla_kv_tail(_):
                    for h in range(4):
                        ps = vr.next()
                        P.op("pe", lambda e: e.matmul(ps[:, :N], lhsT=wukv[:, h * 128:(h + 1) * 128], rhs=ckvn[:, :N], start=True, stop=True), reads=[wukv, ckvn], writes=[ps])
                        store_fm(KNC + h, evac_bf(ps))
                    for a in range(nt):
                        psv = vr.next()
                        P.op("pe", lambda e: e.matmul(psv[:, :], lhsT=ckvn[:, a * 128:(a + 1) * 128], rhs=wukv[:, 512:1024], start=True, stop=True), reads=[wukv, ckvn], writes=[psv])
                        vob = vobr.next()
                        copy_out(vob[:, :], vob, psv[:, :], psv)
                        P.dma("pool", vtok[VC:VC + 4, t0 + a * 128:t0 + (a + 1) * 128, :].rearrange("h t d -> t h d"),
                              vob[:, :].rearrange("t (h d) -> t h d", h=4), reads=[vob])

                def post_kpe(ps):
                    xs = xsr.next()
                    copy_out(xs[:64, :N], xs, ps[:64, :N], ps)
                    ob = rope(xs, 64, r64, 1)
                    store_fm(KPEC, ob, M=64, prow=0)
                    store_fm(KPEC, ob, M=64, prow=64)

                def addjob(cc, post, M=128):
                    jobs.append(((lambda cc=cc, M=M: proj(cc, M)), post))

                if not skipq:
                    for i in range(3):
                        addjob(20 + i, post_cq(i))
                addjob(23, post_ckv)
                for cc in range(0, 8):
                    if not (skipq and cc < 4):
                        addjob(cc, post_plain(QA + cc))
                    if cc == 4 and not skipq:
                        jobs.append((None, mla_q_tail))
                    if cc == 6:
                        jobs.append((None, mla_kv_tail))
                for i, cc in enumerate(range(12, 18)):
                    if not (skipq and i < 4):
                        addjob(cc, post_rope(QB + i))
                for i, cc in enumerate(range(24, 30)):
                    if not (skipq and i < 4):
                        addjob(cc, post_norm_rope(QD + i, "gqa_qn" if i < 4 else "gqa_kn"))
                addjob(32, post_kpe, M=64)
                pend = []
                for (pj, post) in jobs:
                    ps = pj() if pj is not None else None
                    pend.append((post, ps))
                    if len(pend) > 2:
                        f_, a_ = pend.pop(0)
                        f_(a_)
                for (f_, a_) in pend:
                    f_(a_)
                for a in range(nt):
                    for blk in range(2):
                        psv = vr.next()
                        for kc in range(KC):
                            P.op("pe", lambda e: e.matmul(psv[:, :].rearrange("t (c n) -> t c n", c=4), lhsT=ht[:, kc, a * 128:(a + 1) * 128],
                                                          rhs=wv[:, blk * 4:(blk + 1) * 4, kc, :], start=(kc == 0), stop=(kc == KC - 1)),
                                 reads=[ht, wv], writes=[psv])
                        vob = vobr.next()
                        copy_out(vob[:, :], vob, psv[:, :], psv)
                        tsl = slice(t0 + a * 128, t0 + (a + 1) * 128)
                        if blk == 0:
                            P.dma("pool", vtok[VA:VA + 4, tsl, :].rearrange("h t d -> t h d"), vob[:, :].rearrange("t (h d) -> t h d", h=4), reads=[vob])
                        else:
                            P.dma("pool", vtok[VB:VB + 2, tsl, :].rearrange("h t d -> t h d"), vob[:, 0:256].rearrange("t (h d) -> t h d", h=2), reads=[vob])
                            P.dma("pool", vtok[VD:VD + 2, tsl, :].rearrange("h t d -> t h d"), vob[:, 256:512].rearrange("t (h d) -> t h d", h=2), reads=[vob])
            P.barrier()


    def phase2(l):
        need_ctx = l < L - 1
        qgroups = groups if need_ctx else groups[:8]
        with ExitStack() as st:
            ktr = ring("kt2_", 2, [128, TT], BF16, st)
            vtr = ring("vt2_", 2, [128, NKT, 130], BF16, st)
            kpeA = sb("kpe2a", [128, TT], BF16, st)
            kpeB = sb("kpe2b", [128, TT], BF16, st)
            qr = ring("q2_", 3, [128, 512], BF16, st)
            qpr = ring("qp2_", 3, [128, 512], BF16, st)
            ptr = ring("pt2_", 4, [128, 512], BF16, st)
            onr = ring("on2_", 4, [128, 128], BF16, st)
            mor = ring("mo2_", 3, [128, 512], BF16, st)
            rir = ring("ri2_", 8, [128, 1], F32, st)
            amask = sb("amask", [128, 4, 896], F32, st)
            gtr = ring("gt2_", 2, [128, 896], F32, st)
            ggr = ring("gg2_", 2, [128, 896], F32, st)
            ger = ring("ge2_", 2, [128, 896], F32, st)
            psT = PS[6]
            psT_bf = psT[:, :].bitcast(BF16)
            for vt in vtr.tiles:
                P.op("dve", lambda e: e.memset(vt[:, :, 128:129], 1.0), writes=[vt])
                P.op("dve", lambda e: e.memset(vt[:, :, 129:130], 0.0), writes=[vt])
            P.dma("sp", amask[:], amask_d.rearrange("a p c -> p a c"), writes=[amask])
            P.op("dve", lambda e: e.memset(kpeA[64:128, :], 0.0), writes=[kpeA])
            P.op("dve", lambda e: e.memset(kpeB[0:64, :], 0.0), writes=[kpeB])
            P.dma("sp", kpeA[0:64, :], fmT[KPEC, 0:64, :], writes=[kpeA])
            P.dma("sp", kpeB[64:128, :], fmT[KPEC, 64:128, :], writes=[kpeB])

            def load_kv(kidx, vidx):
                kt = ktr.next()
                P.dma("sp", kt[:], fmT[kidx], writes=[kt])
                vt = vtr.next()
                P.dma("sp", vt[:, :, 0:128], vtok[vidx].rearrange("(kt p) d -> p kt d", p=128), writes=[vt])
                return kt, vt

            def load_q(idx, t0, N, r=None):
                q = (r or qr).next()
                P.dma("sp", q[:, :N], fmT[idx, :, t0:t0 + N], writes=[q])
                return q

            def fin_tile(O, col, sink_ap=None):
                ri = rir.next()
                if sink_ap is not None:
                    P.op("dve", lambda e: e.tensor_scalar(out=ri[:], in0=O[:, 128:129], scalar1=sink_ap, scalar2=None, op0=ALU.add), reads=[O, esink], writes=[ri])
                    P.op("dve", lambda e: e.reciprocal(out=ri[:], in_=ri[:]), reads=[ri], writes=[ri])
                else:
                    P.op("dve", lambda e: e.reciprocal(out=ri[:], in_=O[:, 128:129]), reads=[O], writes=[ri])
                on = onr.next()
                P.op("act", lambda e: e.activation(out=on[:], in_=O[:, 0:128], func=AF.Copy, scale=ri[:]), reads=[O, ri], writes=[on])
                P.op("pe", lambda e: e.transpose(out=psT_bf[:, col:col + 128], in_=on[:], identity=identb[:]), reads=[on, identb], writes=[psT])

            def fin_group(head, t0, N):
                mo = mor.next()
                P.op("dve", lambda e: e.tensor_copy(out=mo[:, :N], in_=psT_bf[:, :N]), reads=[psT], writes=[mo])
                P.dma("pool", mixT[head, :, t0:t0 + N], mo[:, :N], reads=[mo])

            sring = Ring([PS[0], PS[1], PS[7]])
            Od = [PS[2], PS[3], PS[4], PS[5]]

            def dense(ktiles, N, smm, scale, vt):
                nq = N // 128
                n = len(ktiles)
                sbank = [None] * n

                def issue_s(i):
                    sbank[i] = sring.next()
                    smm(ktiles[i], sbank[i])

                issue_s(0)
                if n > 1:
                    issue_s(1)
                for i, ktile in enumerate(ktiles):
                    s_ = sbank[i]
                    pt = ptr.next()
                    P.op("act", lambda e: e.activation(out=pt[:, :N], in_=s_[:, :N], func=AF.Exp, scale=scale), reads=[s_], writes=[pt])
                    if i + 2 < n:
                        issue_s(i + 2)
                    for a in range(nq):
                        P.op("pe", lambda e: e.matmul(Od[a][:, 0:130], lhsT=pt[:, a * 128:(a + 1) * 128], rhs=vt[:, ktile, :], start=(i == 0), stop=(i == n - 1)),
                             reads=[pt, vt], writes=[Od[a]])

            def run_dense(kind):
                heads = range(4)
                kt = vt = None
                for h in heads:
                    if kind == "D":
                        if h % 2 == 0:
                            kt, vt = load_kv(KD + h // 2, VD + h // 2)
                        head = 12 + h
                    else:
                        kt, vt = load_kv(KNC + h, VC + h)
                        head = 8 + h
                    for (t0, N, which) in qgroups:
                        ktiles = list(range(NKT)) if which == 0 else [32, 33]
                        if kind == "D":
                            q = load_q(QD + h, t0, N)

                            def smm(ktile, s_):
                                P.op("pe", lambda e: e.matmul(s_[:, :N], lhsT=kt[:, ktile * 128:(ktile + 1) * 128], rhs=q[:, :N], start=True, stop=True),
                                     reads=[kt, q], writes=[s_])
                            dense(ktiles, N, smm, SC128, vt)
                        else:
                            q = load_q(QNC + h, t0, N)
                            qp = load_q(QPEC + h // 2, t0, N, qpr)
                            kpe = kpeA if h % 2 == 0 else kpeB

                            def smm(ktile, s_):
                                P.op("pe", lambda e: e.matmul(s_[:, :N], lhsT=kt[:, ktile * 128:(ktile + 1) * 128], rhs=q[:, :N], start=True, stop=False),
                                     reads=[kt, q], writes=[s_])
                                P.op("pe", lambda e: e.matmul(s_[:, :N], lhsT=kpe[:, ktile * 128:(ktile + 1) * 128], rhs=qp[:, :N], start=False, stop=True),
                                     reads=[kpe, qp], writes=[s_])
                            dense(ktiles, N, smm, SC192, vt)
                        for a in range(N // 128):
                            fin_tile(Od[a], a * 128)
                        fin_group(head, t0, N)

            ssets = Ring([(PS[0], PS[1]), (PS[2], PS[3])])
            Ow = Ring([PS[4], PS[5], PS[7]])

            wjobs = []

            def window_tile(kt, vt, qf, a, blocks, bias_fn, sink_ap, col, post=None):
                nb = len(blocks)
                nA = min(nb, 4)
                nB = nb - nA

                def stage1():
                    q = qf()
                    bA, bB = ssets.next()
                    for i, ktile in enumerate(blocks):
                        bank, c = (bA, i * 128) if i < 4 else (bB, (i - 4) * 128)
                        P.op("pe", lambda e: e.matmul(bank[:, c:c + 128], lhsT=kt[:, ktile * 128:(ktile + 1) * 128], rhs=q[:, a * 128:(a + 1) * 128], start=True, stop=True),
                             reads=[kt, q], writes=[bank])
                    bias_fn(bA, bB)
                    ptA = ptr.next()
                    P.op("act", lambda e: e.activation(out=ptA[:, :nA * 128], in_=bA[:, :nA * 128], func=AF.Exp, scale=SC128), reads=[bA], writes=[ptA])
                    ptB = None
                    if nB:
                        ptB = ptr.next()
                        P.op("act", lambda e: e.activation(out=ptB[:, :nB * 128], in_=bB[:, :nB * 128], func=AF.Exp, scale=SC128), reads=[bB], writes=[ptB])
                    return ptA, ptB

                def stage2(st_):
                    ptA, ptB = st_
                    O = Ow.next()
                    for i, ktile in enumerate(blocks):
                        pt, c = (ptA, i * 128) if i < 4 else (ptB, (i - 4) * 128)
                        P.op("pe", lambda e: e.matmul(O[:, 0:130], lhsT=pt[:, c:c + 128], rhs=vt[:, ktile, :], start=(i == 0), stop=(i == nb - 1)),
                             reads=[pt, vt], writes=[O])
                    fin_tile(O, col, sink_ap)
                    if post is not None:
                        post()

                wjobs.append((stage1, stage2))

            def run_wjobs():
                prev = None
                for (s1, s2) in wjobs:
                    st_ = s1()
                    if prev is not None:
                        prev[0](prev[1])
                    prev = (s2, st_)
                if prev is not None:
                    prev[0](prev[1])
                del wjobs[:]

            def addbias(bank, c0, c1, tab, tab_ap):
                P.op("dve", lambda e: e.tensor_tensor(out=bank[:, c0:c1], in0=bank[:, c0:c1], in1=tab_ap, op=ALU.add), reads=[bank, tab], writes=[bank])

            def run_A():
                for h in range(4):
                    kt, vt = load_kv(KA + h, VA + h)
                    gt = gtr.next()
                    P.dma("sp", gt[:], gtab_d[l, h], writes=[gt])
                    gg, ge = ggr.next(), ger.next()
                    for (g_, mi) in ((gg, 0), (ge, 2)):
                        P.op("dve", lambda e: e.tensor_tensor(out=g_[:], in0=gt[:], in1=amask[:, mi, :], op=ALU.mult), reads=[gt, amask], writes=[g_])
                        P.op("dve", lambda e: e.tensor_tensor(out=g_[:], in0=g_[:], in1=amask[:, mi + 1, :], op=ALU.add), reads=[g_, amask], writes=[g_])
                    for (t0, N, which) in qgroups:
                        qh_ = {}
                        q = (lambda qh_=qh_, h=h, t0=t0, N=N: qh_["q"] if "q" in qh_ else qh_.setdefault("q", load_q(QA + h, t0, N)))
                        for a in range(N // 128):
                            qt = t0 // 128 + a
                            post = (lambda h=h, t0=t0, N=N: fin_group(h, t0, N)) if a == N // 128 - 1 else None
                            if which == 1:
                                window_tile(kt, vt, q, a, [32, 33], lambda bA, bB: None, None, a * 128, post)
                                continue
                            if 2 <= qt <= 29:
                                d0, nw, tab = 2, 5, gg
                            elif qt == 0:
                                d0, nw, tab = 3, 4, ge
                            elif qt == 1:
                                d0, nw, tab = 2, 4, ge
                            elif qt == 30:
                                d0, nw, tab = 1, 4, ge
                            else:
                                d0, nw, tab = 0, 4, ge
                            m0 = 6 - 2 * d0
                            blocks = [qt + d0 - i for i in range(nw)] + [32, 33]

                            def bias_fn(bA, bB, m0=m0, nw=nw, tab=tab):
                                addbias(bA, 0, 512, tab, tab[:, m0 * 64:m0 * 64 + 512])
                                if nw == 5:
                                    addbias(bB, 0, 128, tab, tab[:, (m0 + 8) * 64:(m0 + 10) * 64])
                            window_tile(kt, vt, q, a, blocks, bias_fn, None, a * 128, post)
                    run_wjobs()

            def run_B():
                kt = vt = None
                for h in range(4):
                    if h % 2 == 0:
                        kt, vt = load_kv(KB + h // 2, VB + h // 2)
                    sink_ap = esink[:, l * 4 + h:l * 4 + h + 1]
                    for (t0, N, which) in qgroups:
                        qh_ = {}
                        q = (lambda qh_=qh_, h=h, t0=t0, N=N: qh_["q"] if "q" in qh_ else qh_.setdefault("q", load_q(QB + h, t0, N)))
                        for a in range(N // 128):
                            qt = t0 // 128 + a
                            post = (lambda h=h, t0=t0, N=N: fin_group(4 + h, t0, N)) if a == N // 128 - 1 else None
                            if which == 1:
                                window_tile(kt, vt, q, a, [32, 33], lambda bA, bB: None, sink_ap, a * 128, post)
                                continue
                            masked = ([qt - 1] if qt > 0 else []) + ([qt + 1] if qt < 31 else [])
                            blocks = masked + [qt, 32, 33]

                            def bias_fn(bA, bB, qt=qt):
                                if 0 < qt < 31:
                                    addbias(bA, 0, 256, trimask, trimask[:, 0:256])
                                elif qt == 0:
                                    addbias(bA, 0, 128, trimask, trimask[:, 128:256])
                                else:
                                    addbias(bA, 0, 128, trimask, trimask[:, 0:128])
                            window_tile(kt, vt, q, a, blocks, bias_fn, sink_ap, a * 128, post)
                    run_wjobs()

            run_A()
            run_B()
            run_dense("C")
            run_dense("D")
            P.barrier()


    def phase3(l):
        grp = groups if l < L - 1 else groups[:8]
        cv_step(len(cvq))
        if l + 1 < L:
            convert_small(l + 1)
            cvq.extend(ffn_pieces(l + 1))
        with ExitStack() as st:
            mr = ring("m3_", 2, [128, KC, 512], BF16, st)
            ztr = ring("z3_", 2, [128, KC, 512], F32, st, nsub=KC)
            wr = ring("w3_", 4, [128, KC, 128], BF16, st)
            hbr = ring("hb3_", 2, [128, KC, 512], BF16, st, nsub=KC)
            pools = (ring("sq3_", 3, [128, 512], BF16, st), ring("zb3_", 3, [128, 512], BF16, st), ring("st3_", 6, [128, 512], F32, st))
            tp = ring("t3_", 3, [128, 512], F32, st)
            mainr = Ring([PS[0], PS[1], PS[2], PS[3]])
            wt_ = wbt[("w_out", l, 0)]
            for (t0, N, which) in grp:
                mt = mr.next()
                P.dma("sp", mt[:, :, :N], mixT[:, :, t0:t0 + N].rearrange("h p t -> p h t"), writes=[mt])
                zt = ztr.next()
                P.dma("sp", zt[:, :, :N], xT[:, :, t0:t0 + N].rearrange("kc p t -> p kc t"), writes=zt.sub)
                for cc in range(KC):
                    w = wr.next()
                    P.dma("sp", w[:], wb["w_out"][l, cc].rearrange("p (kc n) -> p kc n", kc=KC), reads=[wt_], writes=[w])
                    ps = mainr.next()
                    for kc in range(KC):
                        P.op("pe", lambda e: e.matmul(ps[:, :N], lhsT=w[:, kc, :], rhs=mt[:, kc, :N], start=(kc == 0), stop=(kc == KC - 1)),
                             reads=[w, mt], writes=[ps])
                    P.op("dve", lambda e: e.scalar_tensor_tensor(out=zt[:, cc, :N], in0=ps[:, :N], scalar=mod(l, 2, cc, which), in1=zt[:, cc, :N],
                                                                op0=ALU.mult, op1=ALU.add), reads=[ps, zt.sub[cc], modv], writes=[zt.sub[cc]])
                Mt, Rt = ln_stats(zt, N, EPS / ALPHA ** 2, pools, PS[4], PS[5])
                for kc in range(KC):
                    ln_apply(zt, kc, N, Mt, Rt, vcol("ln1_g", l, kc), vcol("ln1_b", l, kc), zt[:, kc, :N], zt.sub[kc], tp)
                P.dma("pool", xT[:, :, t0:t0 + N].rearrange("kc p t -> p kc t"), zt[:, :, :N], reads=zt.sub)
                Mt, Rt = ln_stats(zt, N, EPS, pools, PS[6], PS[7])
                hb = hbr.next()
                for kc in range(KC):
                    ln_apply(zt, kc, N, Mt, Rt, mod(l, 4, kc, which), mod(l, 3, kc, which), hb[:, kc, :N], hb.sub[kc], tp)
                c0 = h2col(t0, which) + 1
                P.dma("pool", h2T[:, :, c0:c0 + N].rearrange("kc p t -> p kc t"), hb[:, :, :N], reads=hb.sub)
            P.barrier()

    def phase4(l):
        last = (l == L - 1)
        grp = [(456 * n, 456, 0) for n in range(8)] + [(3648, 448, 0)]
        if not last:
            grp.append((SEQ, CTX, 1))
        with ExitStack() as st:
            h2 = sb("h4", [128, KC, 512], BF16, st)
            zt = sb("z4", [128, KC, 512], F32, st, nsub=KC)
            act = sb("act4", [128, FC, 512], BF16, st)
            hb = sb("hb4", [128, KC, 512], BF16, st, nsub=KC) if not last else None
            wgr = ring("wg4_", 3, [128, KC, 128], BF16, st)
            wur = ring("wu4_", 3, [128, KC, 128], BF16, st)
            wdr = ring("wd4_", 2, [128, FC, 128], BF16, st)
            gbr = ring("gb4_", 3, [128, 512], F32, st)
            a1r = ring("a14_", 3, [128, 512], F32, st)
            sr = ring("s4_", 2, [128, 512], F32, st)
            pools = (ring("sq4_", 2, [128, 512], BF16, st), ring("zb4_", 2, [128, 512], BF16, st), ring("st4_", 3, [128, 512], F32, st))
            tp = ring("t4_", 3, [128, 512], F32, st)
            ost = sb("ost4", [128, D], F32, st) if last else None
            gr = Ring([PS[0], PS[1]])
            ur = Ring([PS[2], PS[3]])
            dr = Ring([PS[0], PS[1], PS[2], PS[3]])
            o_w, _ = VEC[("conv_w", l)]
            o_b, _ = VEC[("conv_b", l)]

            def make_tail(t0, N, which):
                def tail():
                    if not last:
                        P.dma("pool", xT[:, :, t0:t0 + N].rearrange("kc p t -> p kc t"), zt[:, :, :N], reads=zt.sub)
                        Mt, Rt = ln_stats(zt, N, EPS, pools, PS[4], PS[7])
                        for kc in range(KC):
                            ln_apply(zt, kc, N, Mt, Rt, mod(l + 1, 1, kc, which), mod(l + 1, 0, kc, which), hb[:, kc, :N], hb.sub[kc], tp)
                        P.dma("pool", hT[:, :, t0:t0 + N].rearrange("kc p t -> p kc t"), hb[:, :, :N], reads=hb.sub)
                    else:
                        for c_ in range(0, N, 128):
                            w_ = min(128, N - c_)
                            for k4 in range(4):
                                ps = dr.next()
                                for i in range(4):
                                    kc = k4 * 4 + i
                                    P.op("pe", lambda e: e.transpose(out=ps[0:w_, i * 128:(i + 1) * 128], in_=zt[:, kc, c_:c_ + w_], identity=identf[:]),
                                         reads=[zt.sub[kc], identf], writes=[ps])
                                if k4 % 2 == 0:
                                    P.op("act", lambda e: e.activation(out=ost[0:w_, k4 * 512:(k4 + 1) * 512], in_=ps[0:w_, :], func=AF.Copy), reads=[ps], writes=[ost])
                                else:
                                    P.op("dve", lambda e: e.tensor_copy(out=ost[0:w_, k4 * 512:(k4 + 1) * 512], in_=ps[0:w_, :]), reads=[ps], writes=[ost])
                            P.dma("pool", out_d[t0 + c_:t0 + c_ + w_, :], ost[0:w_, :], reads=[ost])
                return tail

            pending = None
            for (t0, N, which) in grp:
                cv_step(2)
                c0 = h2col(t0, which)
                P.dma("sp", h2[:, :, :N + 2], h2T[:, :, c0:c0 + N + 2].rearrange("kc p t -> p kc t"), writes=[h2])
                if pending is None:
                    P.dma("sp", zt[:, :, :N], xT[:, :, t0:t0 + N].rearrange("kc p t -> p kc t"), writes=zt.sub)
                for j in range(FC):
                    if j == 3 and pending is not None:
                        pending()
                        pending = None
                        P.dma("sp", zt[:, :, :N], xT[:, :, t0:t0 + N].rearrange("kc p t -> p kc t"), writes=zt.sub)
                    wg = wgr.next()
                    P.dma("sp", wg[:], wb["w_gate"][l, j].rearrange("p (kc n) -> p kc n", kc=KC), reads=[wdep("w_gate", l, j)], writes=[wg])
                    wu = wur.next()
                    P.dma("sp", wu[:], wb["w_up"][l, j].rearrange("p (kc n) -> p kc n", kc=KC), reads=[wdep("w_up", l, j)], writes=[wu])
                    psG, psU = gr.next(), ur.next()
                    for kc in range(KC):
                        P.op("pe", lambda e: e.matmul(psG[:, :N + 2], lhsT=wg[:, kc, :], rhs=h2[:, kc, 0:N + 2], start=(kc == 0), stop=(kc == KC - 1)),
                             reads=[wg, h2], writes=[psG])
                    for kc in range(KC):
                        P.op("pe", lambda e: e.matmul(psU[:, :N], lhsT=wu[:, kc, :], rhs=h2[:, kc, 1:N + 1], start=(kc == 0), stop=(kc == KC - 1)),
                             reads=[wu, h2], writes=[psU])
                    gb = gbr.next()
                    P.op("act", lambda e: e.activation(out=gb[:, :N + 2], in_=psG[:, :N + 2], func=AF.Copy), reads=[psG], writes=[gb])
                    a1 = a1r.next()
                    w0 = vecs[:, o_w + j * 3 + 0:o_w + j * 3 + 1]
                    w1 = vecs[:, o_w + j * 3 + 1:o_w + j * 3 + 2]
                    w2 = vecs[:, o_w + j * 3 + 2:o_w + j * 3 + 3]
                    cb = vecs[:, o_b + j:o_b + j + 1]
                    P.op("act", lambda e: e.activation(out=a1[:, :N], in_=gb[:, 1:N + 1], func=AF.Identity, scale=w1, bias=cb), reads=[gb, vecs], writes=[a1])
                    P.op("dve", lambda e: e.scalar_tensor_tensor(out=a1[:, :N], in0=gb[:, 0:N], scalar=w0, in1=a1[:, :N], op0=ALU.mult, op1=ALU.add),
                         reads=[gb, a1, vecs], writes=[a1])
                    P.op("dve", lambda e: e.scalar_tensor_tensor(out=a1[:, :N], in0=gb[:, 2:N + 2], scalar=w2, in1=a1[:, :N], op0=ALU.mult, op1=ALU.add),
                         reads=[gb, a1, vecs], writes=[a1])
                    sg = sr.next()
                    P.op("act", lambda e: e.activation(out=sg[:, :N], in_=a1[:, :N], func=AF.Silu), reads=[a1], writes=[sg])
                    P.op("dve", lambda e: e.tensor_tensor(out=act[:, j, :N], in0=sg[:, :N], in1=psU[:, :N], op=ALU.mult), reads=[sg, psU], writes=[act])
                for cc in range(KC):
                    wd = wdr.next()
                    P.dma("sp", wd[:], wb["w_down"][l, cc].rearrange("p (j n) -> p j n", j=FC), reads=[wdep("w_down", l, cc)], writes=[wd])
                    ps = dr.next()
                    for j in range(FC):
                        P.op("pe", lambda e: e.matmul(ps[:, :N], lhsT=wd[:, j, :], rhs=act[:, j, :N], start=(j == 0), stop=(j == FC - 1)),
                             reads=[wd, act], writes=[ps])
                    P.op("dve", lambda e: e.scalar_tensor_tensor(out=zt[:, cc, :N], in0=ps[:, :N], scalar=mod(l, 5, cc, which), in1=zt[:, cc, :N],
                                                                op0=ALU.mult, op1=ALU.add), reads=[ps, zt.sub[cc], modv], writes=[zt.sub[cc]])
                Mt, Rt = ln_stats(zt, N, EPS / ALPHA ** 2, pools, PS[5], PS[6])
                for kc in range(KC):
                    ln_apply(zt, kc, N, Mt, Rt, vcol("ln2_g", l, kc), vcol("ln2_b", l, kc), zt[:, kc, :N], zt.sub[kc], tp)
                pending = make_tail(t0, N, which)
            pending()
            P.barrier()

    for l in range(L):
        for nm, ph in (("p1", phase1), ("p2", phase2), ("p3", phase3), ("p4", phase4)):
            ph(l)
            if stop_after == "%s_%d" % (nm, l):
                return finish()
    return finish()


_NC_CACHE = {}


def kernel(**inp):
    inp = {k: np.asarray(v) for k, v in inp.items()}
    sh = prep_shared(inp)
    if "nc" not in _NC_CACHE:
        _NC_CACHE["nc"] = build()[0]
    nc = _NC_CACHE["nc"]
    B = inp["x"].shape[0]
    in_maps = []
    for b in range(B):
        m = dict(sh)
        m["xin"] = np.ascontiguousarray(np.concatenate([inp["x"][b], inp["ctx"][b]], 0))
        cv = np.stack([inp["c"][b].reshape(16, 128).T, inp["c_ctx"].reshape(16, 128).T], -1).reshape(128, 32)
        m["cvec"] = np.ascontiguousarray(cv)
        in_maps.append(m)
    res = run_bass_kernel_spmd(nc, in_maps, core_ids=list(range(B)))
    return np.stack([np.asarray(r["out"]) for r in res.results], 0).astype(np.float32)
```

```python
import numpy as np
import ml_dtypes
from contextlib import ExitStack
import concourse.bass as bass
import concourse.mybir as mybir
from concourse.bass_utils import run_bass_kernel_spmd

F32 = mybir.dt.float32
BF16 = mybir.dt.bfloat16
AF = mybir.ActivationFunctionType
ALU = mybir.AluOpType

D = 2048
KC = 16
SEQ = 4096
CTX = 256
TT = SEQ + CTX
NKT = TT // 128
L = 2
DFF = 5632
FC = 44
GRID_W = 64
ALPHA = (2 * L) ** 0.25
EPS = 1e-6
NEG = -1e30
SC128 = 128 ** -0.5
SC192 = 192 ** -0.5
NCORES = 4

QA, KA, QB, KB, QD, KD, QNC, QPEC, KNC, KPEC = 0, 4, 8, 12, 14, 18, 20, 24, 26, 30
NFM = 31
VA, VB, VD, VC = 0, 4, 6, 8
NVT = 12
H2W = SEQ + CTX + 4
H2_CTX0 = SEQ + 2

VEC = {}
_o = 0
for _l in range(L):
    for _n, _w in (("b_ada", 96), ("ln1_g", 16), ("ln1_b", 16), ("ln2_g", 16), ("ln2_b", 16),
                   ("conv_w", FC * 3), ("conv_b", FC), ("gqa_qn", 1), ("gqa_kn", 1),
                   ("mla_qn", 3), ("mla_kvn", 1), ("sink", 4)):
        VEC[(_n, _l)] = (_o, _w)
        _o += _w
NVEC = _o


class Buf:
    __slots__ = ("lw", "rd")

    def __init__(self):
        self.lw = None
        self.rd = {}


class Tile:
    def __init__(self, t, nsub=0):
        self.t = t
        self.b = Buf()
        self.sub = [Tile(t) for _ in range(nsub)]

    def __getitem__(self, idx):
        return self.t[idx]


class Ring:
    def __init__(self, tiles):
        self.tiles = tiles
        self.i = 0

    def next(self):
        t = self.tiles[self.i % len(self.tiles)]
        self.i += 1
        return t


class Prog:
    NDMA = 8
    LIM = 28000

    def __init__(self, nc):
        self.nc = nc
        self.eng = {"pe": nc.tensor, "act": nc.scalar, "dve": nc.vector,
                    "pool": nc.gpsimd, "sp": nc.sync}
        self.sem = {}
        self.cnt = {}
        self.cur = {}
        self.nsem = 0
        for k in ("pe", "act", "dve", "pool"):
            self._fresh(k)
        self.dqn = {"sp": 0, "pool": 0, "act": 0}
        for q in self.dqn:
            for i in range(self.NDMA):
                self._fresh(("dma", q, i))
        self.seen = {k: {} for k in self.eng}
        self.ninst = 0
        self.nwait = 0
        self.cvn = 0

    def _fresh(self, base):
        ep = self.cur.get(base, (None, -1))[1] + 1
        key = (base, ep)
        self.cur[base] = key
        self.sem[key] = self.nc.alloc_semaphore("s%d" % self.nsem)
        self.nsem += 1
        self.cnt[key] = 0
        return key

    def _wait(self, e, ev):
        if ev is None:
            return
        key, c = ev
        if key[0] == e and e == "pe":
            return
        if self.seen[e].get(key, 0) >= c:
            return
        self.seen[e][key] = c
        val = c * 16 if isinstance(key[0], tuple) else c
        self.eng[e].wait_ge(self.sem[key], val)
        self.nwait += 1

    def _deps(self, e, reads, writes):
        for t in reads:
            self._wait(e, t.b.lw)
        for t in writes:
            self._wait(e, t.b.lw)
            for k, c in t.b.rd.items():
                self._wait(e, (k, c))

    def _commit(self, ev, reads, writes):
        k, c = ev
        for t in reads:
            t.b.rd[k] = c
        for t in writes:
            t.b.lw = ev
            t.b.rd = {}

    def op(self, e, fn, reads=(), writes=()):
        key = self.cur[e]
        if self.cnt[key] >= self.LIM:
            key = self._fresh(e)
        self._deps(e, reads, writes)
        ins = fn(self.eng[e])
        self.cnt[key] += 1
        ins.then_inc(self.sem[key], 1)
        self._commit((key, self.cnt[key]), reads, writes)
        self.ninst += 1
        return ins

    def dma(self, q, out, in_, reads=(), writes=(), **kw):
        i = self.dqn[q] % self.NDMA
        self.dqn[q] += 1
        base = ("dma", q, i)
        key = self.cur[base]
        if self.cnt[key] > 0:
            self._wait(q, (key, self.cnt[key]))
        if self.cnt[key] * 16 >= self.LIM:
            key = self._fresh(base)
        self._deps(q, reads, writes)
        ins = self.eng[q].dma_start(out=out, in_=in_, **kw)
        self.cnt[key] += 1
        ins.then_inc(self.sem[key], 16)
        self._commit((key, self.cnt[key]), reads, writes)
        self.ninst += 1
        return ins

    def cvt(self, out, in_, tile):
        key = (("cv", self.cvn), 0)
        self.cvn += 1
        self.sem[key] = self.nc.alloc_semaphore("cv%d" % self.cvn)
        self.nsem += 1
        ins = self.nc.gpsimd.dma_start(out=out, in_=in_)
        ins.then_inc(self.sem[key], 16)
        self.cnt[key] = 1
        tile.b.lw = (key, 1)
        self.ninst += 1

    def barrier(self):
        for e in self.eng:
            for key, c in list(self.cnt.items()):
                if c > 0 and key[0] != e and not (isinstance(key[0], tuple) and key[0][0] == "cv"):
                    self._wait(e, (key, c))


def _rope_tables():
    t = np.arange(SEQ)
    row, col = (t // GRID_W).astype(np.float64), (t % GRID_W).astype(np.float64)

    def tab(dim):
        seg = dim // 2
        half = seg // 2
        C = np.ones((dim, TT))
        S = np.zeros((dim, TT))
        perm = np.zeros(dim, np.int64)
        inv = 10000.0 ** (-np.arange(half, dtype=np.float32) / half)
        for d in range(dim):
            s, e = d // seg, d % seg
            i = e % half
            first = e < half
            pos = row if s == 0 else col
            ang = (pos.astype(np.float32) * np.float32(inv[i])).astype(np.float32)
            C[d, :SEQ] = np.cos(ang)
            S[d, :SEQ] = -np.sin(ang) if first else np.sin(ang)
            perm[d] = d + half if first else d - half
        return C.astype(np.float32), S.astype(np.float32), perm

    C128, S128, p128 = tab(128)
    C64, S64, p64 = tab(64)
    rope128 = np.stack([C128, S128])
    rope64 = np.stack([np.concatenate([C64, C64], 0), np.concatenate([S64, S64], 0)])
    Pm128 = np.zeros((128, 128), np.float32)
    Pm128[p128, np.arange(128)] = 1.0
    Pm64 = np.zeros((128, 128), np.float32)
    for hh in range(2):
        Pm64[hh * 64 + p64, hh * 64 + np.arange(64)] = 1.0
    return rope128, rope64, np.stack([Pm128, Pm64])


def _amask():
    p = np.arange(128)
    a, kcol = p // 64, p % 64
    m = np.arange(14)
    j = 6 - m
    dr = j[None, :] + a[:, None]
    qcol = np.arange(64)
    c0 = np.clip(qcol - 8, 0, 48)
    col_in = (kcol[:, None] >= c0[None, :]) & (kcol[:, None] < c0[None, :] + 16)
    v_edge = np.broadcast_to(col_in[:, None, :], (128, 14, 64))
    v_gen = v_edge & ((dr >= -4) & (dr <= 3))[:, :, None]
    out = []
    for v in (v_gen, v_edge):
        out.append((v / SC128).astype(np.float32).reshape(128, 14 * 64))
        out.append(np.where(v, 0.0, NEG).astype(np.float32).reshape(128, 14 * 64))
    return np.stack(out)


def _gtab(rpb):
    p = np.arange(128)
    a, kcol = p // 64, p % 64
    j = 6 - np.arange(14)
    dr = j[None, :] + a[:, None]
    qcol = np.arange(64)
    dc = np.clip(kcol[:, None] - qcol[None, :] + 15, 0, 30)
    g = rpb[:, :, (dr + 7)[:, :, None], dc[:, None, :]]
    return np.ascontiguousarray(g.reshape(L, 4, 128, 14 * 64))


def _chunk_w(w, kc):
    Lw, K, N = w.shape
    return np.ascontiguousarray(w.reshape(Lw, kc, 128, N // 128, 128).transpose(0, 3, 2, 1, 4).reshape(Lw, N // 128, 128, kc * 128))


_CONST_CACHE = {}


def _consts():
    if not _CONST_CACHE:
        rope128, rope64, perm = _rope_tables()
        k = np.arange(128)
        tri = np.stack([np.where(k[:, None] >= k[None, :], 0.0, NEG), np.where(k[:, None] <= k[None, :], 0.0, NEG)], 1)
        _CONST_CACHE.update(rope128=rope128, rope64=rope64, perm=perm, amask=_amask(),
                            identf=np.eye(128, dtype=np.float32),
                            identb=np.eye(128, dtype=np.float32).astype(ml_dtypes.bfloat16),
                            trimask=np.ascontiguousarray(tri.reshape(128, 256).astype(np.float32)))
    return _CONST_CACHE


def prep_shared(inp):
    sh = {}
    w_in = inp["w_in"]
    sizes = [512, 512, 512, 512, 256, 256, 384, 128, 64, 512, 256, 256]
    offs = np.cumsum([0] + sizes)
    seg = lambda i: w_in[:, :, offs[i]:offs[i + 1]]
    w_in_r = np.concatenate([seg(0), seg(1), seg(2), seg(3), seg(4), seg(5), seg(6), seg(7), seg(9), seg(10), seg(11),
                             seg(8), np.zeros((L, D, 64), np.float32)], axis=2)
    sh["w_in"] = _chunk_w(w_in_r, KC)
    sh["w_out"] = _chunk_w(inp["w_out"], KC)
    sh["w_gate"] = _chunk_w(inp["ffn_w_gate"], KC)
    sh["w_up"] = _chunk_w(inp["ffn_w_up"], KC)
    sh["w_down"] = _chunk_w(inp["ffn_w_down"], FC)
    uq = inp["mla_w_uq"].reshape(L, 384, 4, 192)
    uq = np.concatenate([uq[..., :128].reshape(L, 384, 512), uq[..., 128:].reshape(L, 384, 256)], -1)
    sh["w_uq"] = np.ascontiguousarray(uq.reshape(L, 3, 128, 768).transpose(0, 2, 1, 3).reshape(L, 128, 3 * 768))
    ukv = inp["mla_w_ukv"].reshape(L, 128, 4, 256)
    sh["w_ukv"] = np.ascontiguousarray(np.concatenate([ukv[..., :128].reshape(L, 128, 512), ukv[..., 128:].reshape(L, 128, 512)], -1))
    sh["w_ada"] = np.ascontiguousarray(inp["w_ada"].reshape(L * D, 6 * D))
    vec = np.zeros((128, NVEC), np.float32)

    def put(name, l, arr):
        o, w = VEC[(name, l)]
        vec[:, o:o + w] = arr

    for l in range(L):
        put("b_ada", l, inp["b_ada"][l].reshape(96, 128).T)
        for n in ("ln1_g", "ln1_b", "ln2_g", "ln2_b"):
            put(n, l, inp[n][l].reshape(16, 128).T)
        put("conv_w", l, inp["ffn_conv_w"][l].reshape(3, FC, 128).transpose(2, 1, 0).reshape(128, FC * 3))
        put("conv_b", l, inp["ffn_conv_b"][l].reshape(FC, 128).T)
        put("gqa_qn", l, inp["gqa_q_norm"][l].reshape(128, 1))
        put("gqa_kn", l, inp["gqa_k_norm"][l].reshape(128, 1))
        put("mla_qn", l, inp["mla_q_norm"][l].reshape(3, 128).T)
        put("mla_kvn", l, inp["mla_kv_norm"][l].reshape(128, 1))
        put("sink", l, np.broadcast_to(inp["swa_sink"][l][None, :], (128, 4)))
    sh["vecs"] = vec
    sh["gtab"] = _gtab(inp["na_rpb"])
    sh.update(_consts())
    return sh


def build(dbg=(), stop_after=None):
    nc = bass.Bass("TRN2", target_bir_lowering=False)
    P = Prog(nc)
    es = ExitStack()

    def din(name, shape, dt=F32):
        return nc.dram_tensor(name, list(shape), dt, kind="ExternalInput").ap()

    def dscr(name, shape, dt):
        kind = "ExternalOutput" if name in dbg else "Internal"
        return nc.dram_tensor(name, list(shape), dt, kind=kind).ap()

    xin = din("xin", [TT, D])
    cvec = din("cvec", [128, 32])
    w_ada = din("w_ada", [L * D, 6 * D])
    vecs_d = din("vecs", [128, NVEC])
    w_in_d = din("w_in", [L, 33, 128, 2048])
    w_out_d = din("w_out", [L, 16, 128, 2048])
    w_gate_d = din("w_gate", [L, FC, 128, 2048])
    w_up_d = din("w_up", [L, FC, 128, 2048])
    w_down_d = din("w_down", [L, 16, 128, DFF])
    w_uq_d = din("w_uq", [L, 128, 3 * 768])
    w_ukv_d = din("w_ukv", [L, 128, 1024])
    gtab_d = din("gtab", [L, 4, 128, 14 * 64])
    rope128_d = din("rope128", [2, 128, TT])
    rope64_d = din("rope64", [2, 128, TT])
    perm_d = din("perm", [2, 128, 128])
    amask_d = din("amask", [4, 128, 14 * 64])
    identf_d = din("identf", [128, 128])
    identb_d = din("identb", [128, 128], BF16)
    trimask_d = din("trimask", [128, 256])
    out_d = nc.dram_tensor("out", [SEQ, D], F32, kind="ExternalOutput").ap()

    wb = {}
    wsrc = {"w_in": w_in_d, "w_out": w_out_d, "w_gate": w_gate_d, "w_up": w_up_d, "w_down": w_down_d,
            "w_uq": w_uq_d, "w_ukv": w_ukv_d}
    for n, src in wsrc.items():
        wb[n] = nc.dram_tensor(n + "_bf", list(src.shape), BF16, kind="Internal").ap()
    NPIECE = {"w_in": 1, "w_out": 1, "w_uq": 1, "w_ukv": 1, "w_gate": 4, "w_up": 4, "w_down": 4}
    wbt = {(n, l, p): Tile(None) for n in wsrc for l in range(L) for p in range(NPIECE[n])}

    def wdep(n, l, idx=0):
        per = wsrc[n].shape[1] // NPIECE[n] if NPIECE[n] > 1 else 1
        return wbt[(n, l, idx // per if NPIECE[n] > 1 else 0)]

    xT = dscr("xT", [KC, 128, TT], F32)
    hT = dscr("hT", [KC, 128, TT], BF16)
    h2T = dscr("h2T", [KC, 128, H2W], BF16)
    fmT = dscr("fmT", [NFM, 128, TT], BF16)
    vtok = dscr("vtok", [NVT, TT, 128], BF16)
    mixT = dscr("mixT", [KC, 128, TT], BF16)

    uid = [0]

    def sb(name, shape, dt, stack=None, nsub=0):
        uid[0] += 1
        t = (stack or es).enter_context(nc.sbuf_tensor("sb%d_%s" % (uid[0], name), list(shape), dt))
        return Tile(t, nsub)

    def ring(name, n, shape, dt, stack=None, nsub=0):
        return Ring([sb("%s%d" % (name, i), shape, dt, stack, nsub) for i in range(n)])

    PS = [Tile(es.enter_context(nc.psum_tensor("ps%d" % i, [128, 512], F32))) for i in range(8)]

    def flat2k(ap):
        n = 1
        for s in ap.shape:
            n *= s
        names = " ".join("a%d" % i for i in range(len(ap.shape)))
        return ap.rearrange("%s -> (%s)" % (names, names)).rearrange("(r c) -> r c", c=2048)

    def convert_small(l):
        for n in ("w_in", "w_uq", "w_ukv", "w_out"):
            P.cvt(flat2k(wb[n][l]), flat2k(wsrc[n][l]), wbt[(n, l, 0)])

    def ffn_pieces(l):
        out = []
        for p in range(4):
            for n in ("w_gate", "w_up", "w_down"):
                per = wsrc[n].shape[1] // 4

                def f(n=n, p=p, per=per):
                    P.cvt(flat2k(wb[n][l, p * per:(p + 1) * per]), flat2k(wsrc[n][l, p * per:(p + 1) * per]), wbt[(n, l, p)])
                out.append(f)
        return out

    convert_small(0)
    cvq = ffn_pieces(0)

    def cv_step(n=1):
        for _ in range(n):
            if cvq:
                cvq.pop(0)()

    identf = sb("identf", [128, 128], F32)
    identb = sb("identb", [128, 128], BF16)
    perm = sb("perm", [128, 2, 128], F32)
    trimask = sb("trimask", [128, 256], F32)
    vecs = sb("vecs", [128, NVEC], F32)
    onesb = sb("onesb", [128, 128], BF16)
    ones128 = sb("ones128", [128, 128], BF16)
    ones384 = sb("ones384", [128, 128], BF16)
    modv = sb("modv", [128, L, 96, 2], F32)
    esink = sb("esink", [128, L * 4], F32)
    P.dma("sp", identf[:], identf_d, writes=[identf])
    P.dma("sp", identb[:], identb_d, writes=[identb])
    P.dma("sp", perm[:], perm_d.rearrange("a p c -> p a c"), writes=[perm])
    P.dma("sp", trimask[:], trimask_d, writes=[trimask])
    P.dma("sp", vecs[:], vecs_d, writes=[vecs])
    P.op("dve", lambda e: e.memset(onesb[:], 1.0 / D), writes=[onesb])
    P.op("dve", lambda e: e.memset(ones128[:], 1.0 / 128), writes=[ones128])
    P.op("dve", lambda e: e.memset(ones384[:], 1.0 / 384), writes=[ones384])

    zcol = sb("zcol", [128, KC, 1], BF16)
    P.op("dve", lambda e: e.memset(zcol[:], 0.0), writes=[zcol])
    for c in (0, SEQ + 1, SEQ + 2, H2W - 1):
        P.dma("sp", h2T[:, :, c:c + 1].rearrange("kc p t -> p kc t"), zcol[:], reads=[zcol], allow_slow_non_contiguous=True)

    def vcol(name, l, j=0, w=1):
        o, _ = VEC[(name, l)]
        return vecs[:, o + j:o + j + w]

    def mod(l, s, kc, which):
        return modv[:, l, s * 16 + kc, which:which + 1]

    def ln_stats(zt, N, eps, pools, psM, psV):
        sqp, zbp, stp = pools
        for kc in range(KC):
            sq = sqp.next()
            P.op("act", lambda e: e.activation(out=sq[:, :N], in_=zt[:, kc, :N], func=AF.Square), reads=[zt.sub[kc]], writes=[sq])
            zb = zbp.next()
            P.op("pool" if kc % 2 == 0 else "dve", lambda e: e.tensor_copy(out=zb[:, :N], in_=zt[:, kc, :N]), reads=[zt.sub[kc]], writes=[zb])
            P.op("pe", lambda e: e.matmul(psM[:, :N], lhsT=onesb[:], rhs=zb[:, :N], start=(kc == 0), stop=(kc == KC - 1)),
                 reads=[onesb, zb], writes=[psM])
            P.op("pe", lambda e: e.matmul(psV[:, :N], lhsT=onesb[:], rhs=sq[:, :N], start=(kc == 0), stop=(kc == KC - 1)),
                 reads=[onesb, sq], writes=[psV])
        m2 = stp.next()
        P.op("act", lambda e: e.activation(out=m2[:, :N], in_=psM[:, :N], func=AF.Square), reads=[psM], writes=[m2])
        P.op("dve", lambda e: e.tensor_tensor(out=m2[:, :N], in0=psV[:, :N], in1=m2[:, :N], op=ALU.subtract), reads=[psV, m2], writes=[m2])
        P.op("dve", lambda e: e.tensor_scalar(out=m2[:, :N], in0=m2[:, :N], scalar1=float(eps), scalar2=None, op0=ALU.add), reads=[m2], writes=[m2])
        P.op("act", lambda e: e.activation(out=m2[:, :N], in_=m2[:, :N], func=AF.Sqrt), reads=[m2], writes=[m2])
        P.op("dve", lambda e: e.reciprocal(out=psV[:, :N], in_=m2[:, :N]), reads=[m2], writes=[psV])
        return psM, psV

    def ln_apply(zt, kc, N, Mt, Rt, scale_ap, bias_ap, out_ap, out_tile, tp, extra_reads=()):
        t = tp.next()
        P.op("dve", lambda e: e.tensor_tensor(out=t[:, :N], in0=zt[:, kc, :N], in1=Mt[:, :N], op=ALU.subtract), reads=[zt.sub[kc], Mt], writes=[t])
        P.op("dve", lambda e: e.tensor_tensor(out=t[:, :N], in0=t[:, :N], in1=Rt[:, :N], op=ALU.mult), reads=[t, Rt], writes=[t])
        P.op("act", lambda e: e.activation(out=out_ap, in_=t[:, :N], func=AF.Identity, scale=scale_ap, bias=bias_ap),
             reads=[t, vecs, modv] + list(extra_reads), writes=[out_tile])

    groups = [(g * 512, 512, 0) for g in range(8)] + [(SEQ, CTX, 1)]

    def h2col(t0, which):
        return (t0 if which == 0 else H2_CTX0)

    with ExitStack() as st:
        cv = sb("cv", [128, 32], F32, st)
        scv = sb("scv", [128, 32], F32, st)
        wblk = ring("wblk", 2, [128, KC, 512], F32, st)
        P.dma("sp", cv[:], cvec, writes=[cv])
        P.op("act", lambda e: e.activation(out=scv[:], in_=cv[:], func=AF.Silu), reads=[cv], writes=[scv])
        mrr = ring("mrow", 2, [2, 512], F32, st)
        rowr = Ring([PS[2], PS[3]])
        for l in range(L):
            psA = PS[l]
            psv = psA[:, 0:192].rearrange("p (j w) -> p j w", w=2)
            wl = w_ada[l * D:(l + 1) * D, :].rearrange("(kc p) n -> p kc n", p=128)
            for blk in range(24):
                wt = wblk.next()
                P.dma("sp", wt[:], wl[:, :, blk * 512:(blk + 1) * 512], writes=[wt])
                prow = rowr.next()
                for kc in range(KC):
                    P.op("pe", lambda e: e.matmul(prow[0:2, :], lhsT=scv[:, 2 * kc:2 * kc + 2], rhs=wt[:, kc, :], start=(kc == 0), stop=(kc == KC - 1)),
                         reads=[wt, scv], writes=[prow])
                mr = mrr.next()
                P.op("act", lambda e: e.activation(out=mr[0:2, :], in_=prow[0:2, :], func=AF.Copy), reads=[prow], writes=[mr])
                for j in range(4):
                    P.op("pe", lambda e: e.transpose(out=psv[:, blk * 4 + j, :], in_=mr[0:2, j * 128:(j + 1) * 128], identity=identf[0:2, 0:2]),
                         reads=[mr, identf], writes=[psA])
            o, w = VEC[("b_ada", l)]
            for which in range(2):
                P.op("dve", lambda e: e.tensor_tensor(out=modv[:, l, :, which], in0=psv[:, :, which], in1=vecs[:, o:o + 96], op=ALU.add),
                     reads=[psA, vecs], writes=[modv])
            for s in (1, 4):
                P.op("dve", lambda e: e.tensor_scalar(out=modv[:, l, s * 16:(s + 1) * 16, :], in0=modv[:, l, s * 16:(s + 1) * 16, :],
                                                      scalar1=1.0, scalar2=None, op0=ALU.add), reads=[modv], writes=[modv])
            for s in (2, 5):
                P.op("dve", lambda e: e.tensor_scalar(out=modv[:, l, s * 16:(s + 1) * 16, :], in0=modv[:, l, s * 16:(s + 1) * 16, :],
                                                      scalar1=1.0 / ALPHA, scalar2=None, op0=ALU.mult), reads=[modv], writes=[modv])
            o, w = VEC[("sink", l)]
            P.op("act", lambda e: e.activation(out=esink[:, l * 4:(l + 1) * 4], in_=vecs[:, o:o + 4], func=AF.Exp), reads=[vecs], writes=[esink])
        P.barrier()
    if "modv" in dbg:
        modv_o = nc.dram_tensor("modv_o", [128, L * 96 * 2], F32, kind="ExternalOutput").ap()
        P.dma("sp", modv_o, modv[:].rearrange("p l j w -> p (l j w)"), reads=[modv])

    def finish():
        P.barrier()
        es.close()
        return nc, P

    if stop_after == "ada":
        return finish()

    with ExitStack() as st:
        xr = ring("xin", 2, [128, 4, D], F32, st)
        ztr = ring("zt0", 2, [128, KC, 512], F32, st, nsub=KC)
        hbr = ring("hb0", 2, [128, KC, 512], BF16, st, nsub=KC)
        pools = (ring("sq0", 3, [128, 512], BF16, st), ring("zb0", 3, [128, 512], BF16, st), ring("st0", 6, [128, 512], F32, st))
        tp = ring("t0", 3, [128, 512], F32, st)
        psr = Ring([PS[0], PS[1], PS[2], PS[3]])
        stat_sets = Ring([(PS[4], PS[5]), (PS[6], PS[7])])
        for (t0, N, which) in groups:
            cv_step()
            nt = N // 128
            xt = xr.next()
            P.dma("sp", xt[:, :nt, :], xin[t0:t0 + N, :].rearrange("(a p) d -> p a d", p=128), writes=[xt])
            zt = ztr.next()
            for kc in range(KC):
                ps = psr.next()
                for a in range(nt):
                    P.op("pe", lambda e: e.transpose(out=ps[:, a * 128:(a + 1) * 128], in_=xt[:, a, kc * 128:(kc + 1) * 128], identity=identf[:]),
                         reads=[xt, identf], writes=[ps])
                eng = "act" if kc % 2 == 0 else "dve"
                if eng == "act":
                    P.op("act", lambda e: e.activation(out=zt[:, kc, :N], in_=ps[:, :N], func=AF.Copy), reads=[ps], writes=[zt.sub[kc]])
                else:
                    P.op("dve", lambda e: e.tensor_copy(out=zt[:, kc, :N], in_=ps[:, :N]), reads=[ps], writes=[zt.sub[kc]])
            P.dma("pool", xT[:, :, t0:t0 + N].rearrange("kc p t -> p kc t"), zt[:, :, :N], reads=zt.sub)
            bM, bV = stat_sets.next()
            Mt, Rt = ln_stats(zt, N, EPS, pools, bM, bV)
            hb = hbr.next()
            for kc in range(KC):
                ln_apply(zt, kc, N, Mt, Rt, mod(0, 1, kc, which), mod(0, 0, kc, which), hb[:, kc, :N], hb.sub[kc], tp)
            P.dma("pool", hT[:, :, t0:t0 + N].rearrange("kc p t -> p kc t"), hb[:, :, :N], reads=hb.sub)
        P.barrier()
    if stop_after == "p0":
        return finish()


    def phase1(l):
        with ExitStack() as st:
            hr = ring("h1_", 2, [128, KC, 512], BF16, st)
            wr = ring("w1_", 6, [128, KC, 128], BF16, st)
            wv = sb("wv1", [128, 8, KC, 128], BF16, st)
            wuq = sb("wuq", [128, 3, 768], BF16, st)
            wukv = sb("wukv", [128, 1024], BF16, st)
            rc128 = ring("rc128_", 2, [128, 2, 512], F32, st)
            rc64 = ring("rc64_", 2, [128, 2, 512], F32, st)
            xsr = ring("xs1_", 4, [128, 512], F32, st)
            t1r = ring("t1_", 6, [128, 512], F32, st)
            obr = ring("ob1_", 6, [128, 512], BF16, st)
            sqr = ring("sq1_", 3, [128, 512], BF16, st)
            rsr = ring("rs1_", 4, [128, 512], F32, st)
            vobr = ring("vob1_", 3, [128, 512], BF16, st)
            cqs = sb("cqs", [128, 3, 512], F32, st)
            cqn = sb("cqn", [128, 3, 512], BF16, st)
            ckvn = sb("ckvn", [128, 512], BF16, st)
            mainr = Ring([PS[0], PS[1], PS[2], PS[3]])
            ppr = Ring([PS[4]])
            psR = PS[5]
            vr = Ring([PS[6], PS[7]])
            wt_in = wbt[("w_in", l, 0)]
            wsrc_l = w_in_bf = wb["w_in"]

            def wchunks(c0, n):
                return wb["w_in"][l, c0:c0 + n].rearrange("c p (kc n) -> p c kc n", kc=KC)

            P.dma("sp", wv[:, 0:4], wchunks(8, 4), reads=[wt_in], writes=[wv])
            P.dma("sp", wv[:, 4:6], wchunks(18, 2), reads=[wt_in], writes=[wv])
            P.dma("sp", wv[:, 6:8], wchunks(30, 2), reads=[wt_in], writes=[wv])
            P.dma("sp", wuq[:], wb["w_uq"][l].rearrange("p (kc n) -> p kc n", kc=3), reads=[wbt[("w_uq", l, 0)]], writes=[wuq])
            P.dma("sp", wukv[:], wb["w_ukv"][l], reads=[wbt[("w_ukv", l, 0)]], writes=[wukv])
            cnt = [0]

            for (t0, N, which) in groups:
                cv_step()
                skipq = (l == L - 1 and which == 1)
                nt = N // 128
                ht = hr.next()
                P.dma("sp", ht[:, :, :N], hT[:, :, t0:t0 + N].rearrange("kc p t -> p kc t"), writes=[ht])
                r128 = rc128.next()
                P.dma("sp", r128[:, :, :N], rope128_d[:, :, t0:t0 + N].rearrange("a p t -> p a t"), writes=[r128])
                r64 = rc64.next()
                P.dma("sp", r64[:, :, :N], rope64_d[:, :, t0:t0 + N].rearrange("a p t -> p a t"), writes=[r64])

                def proj(cc, M=128):
                    w = wr.next()
                    P.dma("sp", w[:], wb["w_in"][l, cc].rearrange("p (kc n) -> p kc n", kc=KC), reads=[wt_in], writes=[w])
                    ps = mainr.next()
                    for kc in range(KC):
                        P.op("pe", lambda e: e.matmul(ps[0:M, :N], lhsT=w[:, kc, 0:M], rhs=ht[:, kc, :N], start=(kc == 0), stop=(kc == KC - 1)),
                             reads=[w, ht], writes=[ps])
                    return ps

                def copy_out(dst_ap, dst_tile, src_ap, src_tile):
                    cnt[0] += 1
                    if cnt[0] % 2 == 0:
                        P.op("act", lambda e: e.activation(out=dst_ap, in_=src_ap, func=AF.Copy), reads=[src_tile], writes=[dst_tile])
                    else:
                        P.op("dve", lambda e: e.tensor_copy(out=dst_ap, in_=src_ap), reads=[src_tile], writes=[dst_tile])

                def evac_bf(ps, M=128):
                    ob = obr.next()
                    copy_out(ob[:M, :N], ob, ps[:M, :N], ps)
                    return ob

                def store_fm(idx, ob, M=128, prow=0):
                    P.dma("pool", fmT[idx, prow:prow + M, t0:t0 + N], ob[:M, :N], reads=[ob])

                def rope(xs, M, tab, pmi):
                    pp = ppr.next()
                    P.op("pe", lambda e: e.matmul(pp[:M, :N], lhsT=perm[:M, pmi, :M], rhs=xs[:M, :N], start=True, stop=True),
                         reads=[perm, xs], writes=[pp])
                    t1 = t1r.next()
                    P.op("pool", lambda e: e.tensor_tensor(out=t1[:M, :N], in0=xs[:M, :N], in1=tab[:M, 0, :N], op=ALU.mult), reads=[xs, tab], writes=[t1])
                    t2 = t1r.next()
                    P.op("dve", lambda e: e.tensor_tensor(out=t2[:M, :N], in0=pp[:M, :N], in1=tab[:M, 1, :N], op=ALU.mult), reads=[pp, tab], writes=[t2])
                    ob = obr.next()
                    P.op("dve", lambda e: e.tensor_tensor(out=ob[:M, :N], in0=t1[:M, :N], in1=t2[:M, :N], op=ALU.add), reads=[t1, t2], writes=[ob])
                    return ob

                def rstd_from(psr_tile):
                    m = rsr.next()
                    P.op("dve", lambda e: e.tensor_scalar(out=m[:, :N], in0=psr_tile[:, :N], scalar1=EPS, scalar2=None, op0=ALU.add), reads=[psr_tile], writes=[m])
                    P.op("act", lambda e: e.activation(out=m[:, :N], in_=m[:, :N], func=AF.Sqrt), reads=[m], writes=[m])
                    P.op("dve", lambda e: e.reciprocal(out=m[:, :N], in_=m[:, :N]), reads=[m], writes=[m])
                    return m

                jobs = []
                mcq = []

                def post_plain(idx):
                    return lambda ps: store_fm(idx, evac_bf(ps))

                def post_rope(idx):
                    def f(ps):
                        xs = xsr.next()
                        copy_out(xs[:, :N], xs, ps[:, :N], ps)
                        store_fm(idx, rope(xs, 128, r128, 0))
                    return f

                def post_norm_rope(idx, gname):
                    def f(ps):
                        sq = sqr.next()
                        P.op("act", lambda e: e.activation(out=sq[:, :N], in_=ps[:, :N], func=AF.Square), reads=[ps], writes=[sq])
                        P.op("pe", lambda e: e.matmul(psR[:, :N], lhsT=ones128[:], rhs=sq[:, :N], start=True, stop=True), reads=[ones128, sq], writes=[psR])
                        m = rstd_from(psR)
                        xs = xsr.next()
                        P.op("dve", lambda e: e.scalar_tensor_tensor(out=xs[:, :N], in0=ps[:, :N], scalar=vcol(gname, l), in1=m[:, :N], op0=ALU.mult, op1=ALU.mult),
                             reads=[ps, m, vecs], writes=[xs])
                        store_fm(idx, rope(xs, 128, r128, 0))
                    return f

                def post_cq(i):
                    def f(ps):
                        P.op("act", lambda e: e.activation(out=cqs[:, i, :N], in_=ps[:, :N], func=AF.Copy), reads=[ps], writes=[cqs])
                        sq = sqr.next()
                        P.op("act", lambda e: e.activation(out=sq[:, :N], in_=ps[:, :N], func=AF.Square), reads=[ps], writes=[sq])
                        P.op("pe", lambda e: e.matmul(psR[:, :N], lhsT=ones384[:], rhs=sq[:, :N], start=(i == 0), stop=(i == 2)), reads=[ones384, sq], writes=[psR])
                        if i == 2:
                            mcq.append(rstd_from(psR))
                    return f

                def mla_q_tail(_):
                    m = mcq.pop()
                    for i in range(3):
                        P.op("dve", lambda e: e.scalar_tensor_tensor(out=cqn[:, i, :N], in0=cqs[:, i, :N], scalar=vcol("mla_qn", l, i), in1=m[:, :N],
                                                                    op0=ALU.mult, op1=ALU.mult), reads=[cqs, m, vecs], writes=[cqn])
                    for h in range(4):
                        ps = vr.next()
                        for i in range(3):
                            P.op("pe", lambda e: e.matmul(ps[:, :N], lhsT=wuq[:, i, h * 128:(h + 1) * 128], rhs=cqn[:, i, :N], start=(i == 0), stop=(i == 2)),
                                 reads=[wuq, cqn], writes=[ps])
                        store_fm(QNC + h, evac_bf(ps))
                    for j in range(2):
                        ps = vr.next()
                        for i in range(3):
                            P.op("pe", lambda e: e.matmul(ps[:, :N], lhsT=wuq[:, i, 512 + j * 128:512 + (j + 1) * 128], rhs=cqn[:, i, :N], start=(i == 0), stop=(i == 2)),
                                 reads=[wuq, cqn], writes=[ps])
                        xs = xsr.next()
                        copy_out(xs[:, :N], xs, ps[:, :N], ps)
                        store_fm(QPEC + j, rope(xs, 128, r64, 1))

                def post_ckv(ps):
                    xs = xsr.next()
                    P.op("act", lambda e: e.activation(out=xs[:, :N], in_=ps[:, :N], func=AF.Copy), reads=[ps], writes=[xs])
                    sq = sqr.next()
                    P.op("act", lambda e: e.activation(out=sq[:, :N], in_=ps[:, :N], func=AF.Square), reads=[ps], writes=[sq])
                    P.op("pe", lambda e: e.matmul(psR[:, :N], lhsT=ones128[:], rhs=sq[:, :N], start=True, stop=True), reads=[ones128, sq], writes=[psR])
                    m = rstd_from(psR)
                    P.op("dve", lambda e: e.scalar_tensor_tensor(out=ckvn[:, :N], in0=xs[:, :N], scalar=vcol("mla_kvn", l), in1=m[:, :N], op0=ALU.mult, op1=ALU.mult),
                         reads=[xs, m, vecs], writes=[ckvn])

                def mla_kv_tail(_):
                    for h in range(4):
                        ps = vr.next()
                        P.op("pe", lambda e: e.matmul(ps[:, :N], lhsT=wukv[:, h * 128:(h + 1) * 128], rhs=ckvn[:, :N], start=True, stop=True), reads=[wukv, ckvn], writes=[ps])
                        store_fm(KNC + h, evac_bf(ps))
                    for a in range(nt):
                        psv = vr.next()
                        P.op("pe", lambda e: e.matmul(psv[:, :], lhsT=ckvn[:, a * 128:(a + 1) * 128], rhs=wukv[:, 512:1024], start=True, stop=True), reads=[wukv, ckvn], writes=[psv])
                        vob = vobr.next()
                        copy_out(vob[:, :], vob, psv[:, :], psv)
                        P.dma("pool", vtok[VC:VC + 4, t0 + a * 128:t0 + (a + 1) * 128, :].rearrange("h t d -> t h d"),
                              vob[:, :].rearrange("t (h d) -> t h d", h=4), reads=[vob])

                def post_kpe(ps):
                    xs = xsr.next()
                    copy_out(xs[:64, :N], xs, ps[:64, :N], ps)
                    ob = rope(xs, 64, r64, 1)
                    store_fm(KPEC, ob, M=64, prow=0)
                    store_fm(KPEC, ob, M=64, prow=64)

                def addjob(cc, post, M=128):
                    jobs.append(((lambda cc=cc, M=M: proj(cc, M)), post))

                if not skipq:
                    for i in range(3):
                        addjob(20 + i, post_cq(i))
                addjob(23, post_ckv)
                for cc in range(0, 8):
                    if not (skipq and cc < 4):
                        addjob(cc, post_plain(QA + cc))
                    if cc == 4 and not skipq:
                        jobs.append((None, mla_q_tail))
                    if cc == 6:
                        jobs.append((None, mla_kv_tail))
                for i, cc in enumerate(range(12, 18)):
                    if not (skipq and i < 4):
                        addjob(cc, post_rope(QB + i))
                for i, cc in enumerate(range(24, 30)):
                    if not (skipq and i < 4):
                        addjob(cc, post_norm_rope(QD + i, "gqa_qn" if i < 4 else "gqa_kn"))
                addjob(32, post_kpe, M=64)
                pend = []
                for (pj, post) in jobs:
                    ps = pj() if pj is not None else None
                    pend.append((post, ps))
                    if len(pend) > 2:
                        f_, a_ = pend.pop(0)
                        f_(a_)
                for (f_, a_) in pend:
                    f_(a_)
                for a in range(nt):
                    for blk in range(2):
                        psv = vr.next()
                        for kc in range(KC):
                            P.op("pe", lambda e: e.matmul(psv[:, :].rearrange("t (c n) -> t c n", c=4), lhsT=ht[:, kc, a * 128:(a + 1) * 128],
                                                          rhs=wv[:, blk * 4:(blk + 1) * 4, kc, :], start=(kc == 0), stop=(kc == KC - 1)),
                                 reads=[ht, wv], writes=[psv])
                        vob = vobr.next()
                        copy_out(vob[:, :], vob, psv[:, :], psv)
                        tsl = slice(t0 + a * 128, t0 + (a + 1) * 128)
                        if blk == 0:
                            P.dma("pool", vtok[VA:VA + 4, tsl, :].rearrange("h t d -> t h d"), vob[:, :].rearrange("t (h d) -> t h d", h=4), reads=[vob])
                        else:
                            P.dma("pool", vtok[VB:VB + 2, tsl, :].rearrange("h t d -> t h d"), vob[:, 0:256].rearrange("t (h d) -> t h d", h=2), reads=[vob])
                            P.dma("pool", vtok[VD:VD + 2, tsl, :].rearrange("h t d -> t h d"), vob[:, 256:512].rearrange("t (h d) -> t h d", h=2), reads=[vob])
            P.barrier()


    def phase2(l):
        need_ctx = l < L - 1
        qgroups = groups if need_ctx else groups[:8]
        with ExitStack() as st:
            ktr = ring("kt2_", 2, [128, TT], BF16, st)
            vtr = ring("vt2_", 2, [128, NKT, 130], BF16, st)
            kpeA = sb("kpe2a", [128, TT], BF16, st)
            kpeB = sb("kpe2b", [128, TT], BF16, st)
            qr = ring("q2_", 3, [128, 512], BF16, st)
            qpr = ring("qp2_", 3, [128, 512], BF16, st)
            ptr = ring("pt2_", 4, [128, 512], BF16, st)
            onr = ring("on2_", 4, [128, 128], BF16, st)
            mor = ring("mo2_", 3, [128, 512], BF16, st)
            rir = ring("ri2_", 8, [128, 1], F32, st)
            amask = sb("amask", [128, 4, 896], F32, st)
            gtr = ring("gt2_", 2, [128, 896], F32, st)
            ggr = ring("gg2_", 2, [128, 896], F32, st)
            ger = ring("ge2_", 2, [128, 896], F32, st)
            psT = PS[6]
            psT_bf = psT[:, :].bitcast(BF16)
            for vt in vtr.tiles:
                P.op("dve", lambda e: e.memset(vt[:, :, 128:129], 1.0), writes=[vt])
                P.op("dve", lambda e: e.memset(vt[:, :, 129:130], 0.0), writes=[vt])
            P.dma("sp", amask[:], amask_d.rearrange("a p c -> p a c"), writes=[amask])
            P.op("dve", lambda e: e.memset(kpeA[64:128, :], 0.0), writes=[kpeA])
            P.op("dve", lambda e: e.memset(kpeB[0:64, :], 0.0), writes=[kpeB])
            P.dma("sp", kpeA[0:64, :], fmT[KPEC, 0:64, :], writes=[kpeA])
            P.dma("sp", kpeB[64:128, :], fmT[KPEC, 64:128, :], writes=[kpeB])

            def load_kv(kidx, vidx):
                kt = ktr.next()
                P.dma("sp", kt[:], fmT[kidx], writes=[kt])
                vt = vtr.next()
                P.dma("sp", vt[:, :, 0:128], vtok[vidx].rearrange("(kt p) d -> p kt d", p=128), writes=[vt])
                return kt, vt

            def load_q(idx, t0, N, r=None):
                q = (r or qr).next()
                P.dma("sp", q[:, :N], fmT[idx, :, t0:t0 + N], writes=[q])
                return q

            def fin_tile(O, col, sink_ap=None, defer=False):
                ri = rir.next()
                if sink_ap is not None:
                    P.op("dve", lambda e: e.tensor_scalar(out=ri[:], in0=O[:, 128:129], scalar1=sink_ap, scalar2=None, op0=ALU.add), reads=[O, esink], writes=[ri])
                    P.op("dve", lambda e: e.reciprocal(out=ri[:], in_=ri[:]), reads=[ri], writes=[ri])
                else:
                    P.op("dve", lambda e: e.reciprocal(out=ri[:], in_=O[:, 128:129]), reads=[O], writes=[ri])
                on = onr.next()
                P.op("act", lambda e: e.activation(out=on[:], in_=O[:, 0:128], func=AF.Copy, scale=ri[:]), reads=[O, ri], writes=[on])

                def tr():
                    P.op("pe", lambda e: e.transpose(out=psT_bf[:, col:col + 128], in_=on[:], identity=identb[:]), reads=[on, identb], writes=[psT])
                if defer:
                    return tr
                tr()

            def fin_group(head, t0, N):
                mo = mor.next()
                P.op("dve", lambda e: e.tensor_copy(out=mo[:, :N], in_=psT_bf[:, :N]), reads=[psT], writes=[mo])
                P.dma("pool", mixT[head, :, t0:t0 + N], mo[:, :N], reads=[mo])

            sring = Ring([PS[0], PS[1], PS[7]])
            Od = [PS[2], PS[3], PS[4], PS[5]]

            dpend = []

            def dense(ktiles, N, smm, scale, vt):
                nq = N // 128
                n = len(ktiles)
                sbank = [None] * n

                def issue_s(i):
                    sbank[i] = sring.next()
                    smm(ktiles[i], sbank[i])

                issue_s(0)
                if n > 1:
                    issue_s(1)
                while dpend:
                    dpend.pop(0)()
                for i, ktile in enumerate(ktiles):
                    s_ = sbank[i]
                    pt = ptr.next()
                    P.op("act", lambda e: e.activation(out=pt[:, :N], in_=s_[:, :N], func=AF.Exp, scale=scale), reads=[s_], writes=[pt])
                    if i + 2 < n:
                        issue_s(i + 2)
                    for a in range(nq):
                        P.op("pe", lambda e: e.matmul(Od[a][:, 0:130], lhsT=pt[:, a * 128:(a + 1) * 128], rhs=vt[:, ktile, :], start=(i == 0), stop=(i == n - 1)),
                             reads=[pt, vt], writes=[Od[a]])

            def run_dense(kind):
                heads = range(4)
                kt = vt = None
                for h in heads:
                    if kind == "D":
                        if h % 2 == 0:
                            kt, vt = load_kv(KD + h // 2, VD + h // 2)
                        head = 12 + h
                    else:
                        kt, vt = load_kv(KNC + h, VC + h)
                        head = 8 + h
                    for (t0, N, which) in qgroups:
                        ktiles = list(range(NKT)) if which == 0 else [32, 33]
                        if kind == "D":
                            q = load_q(QD + h, t0, N)

                            def smm(ktile, s_):
                                P.op("pe", lambda e: e.matmul(s_[:, :N], lhsT=kt[:, ktile * 128:(ktile + 1) * 128], rhs=q[:, :N], start=True, stop=True),
                                     reads=[kt, q], writes=[s_])
                            dense(ktiles, N, smm, SC128, vt)
                        else:
                            q = load_q(QNC + h, t0, N)
                            qp = load_q(QPEC + h // 2, t0, N, qpr)
                            kpe = kpeA if h % 2 == 0 else kpeB

                            def smm(ktile, s_):
                                P.op("pe", lambda e: e.matmul(s_[:, :N], lhsT=kt[:, ktile * 128:(ktile + 1) * 128], rhs=q[:, :N], start=True, stop=False),
                                     reads=[kt, q], writes=[s_])
                                P.op("pe", lambda e: e.matmul(s_[:, :N], lhsT=kpe[:, ktile * 128:(ktile + 1) * 128], rhs=qp[:, :N], start=False, stop=True),
                                     reads=[kpe, qp], writes=[s_])
                            dense(ktiles, N, smm, SC192, vt)
                        trs = [fin_tile(Od[a], a * 128, None, True) for a in range(N // 128)]

                        def fin(trs=trs, head=head, t0=t0, N=N):
                            for f_ in trs:
                                f_()
                            fin_group(head, t0, N)
                        dpend.append(fin)
                while dpend:
                    dpend.pop(0)()

            ssets = Ring([(PS[0], PS[1]), (PS[2], PS[3])])
            Ow = Ring([PS[4], PS[5], PS[7]])

            wjobs = []

            def window_tile(kt, vt, qf, a, blocks, bias_fn, sink_ap, col, post=None):
                nb = len(blocks)
                nA = min(nb, 4)
                nB = nb - nA

                def stage1():
                    q = qf()
                    bA, bB = ssets.next()
                    for i, ktile in enumerate(blocks):
                        bank, c = (bA, i * 128) if i < 4 else (bB, (i - 4) * 128)
                        P.op("pe", lambda e: e.matmul(bank[:, c:c + 128], lhsT=kt[:, ktile * 128:(ktile + 1) * 128], rhs=q[:, a * 128:(a + 1) * 128], start=True, stop=True),
                             reads=[kt, q], writes=[bank])
                    bias_fn(bA, bB)
                    ptA = ptr.next()
                    P.op("act", lambda e: e.activation(out=ptA[:, :nA * 128], in_=bA[:, :nA * 128], func=AF.Exp, scale=SC128), reads=[bA], writes=[ptA])
                    ptB = None
                    if nB:
                        ptB = ptr.next()
                        P.op("act", lambda e: e.activation(out=ptB[:, :nB * 128], in_=bB[:, :nB * 128], func=AF.Exp, scale=SC128), reads=[bB], writes=[ptB])
                    return ptA, ptB

                def stage2(st_):
                    ptA, ptB = st_
                    O = Ow.next()
                    for i, ktile in enumerate(blocks):
                        pt, c = (ptA, i * 128) if i < 4 else (ptB, (i - 4) * 128)
                        P.op("pe", lambda e: e.matmul(O[:, 0:130], lhsT=pt[:, c:c + 128], rhs=vt[:, ktile, :], start=(i == 0), stop=(i == nb - 1)),
                             reads=[pt, vt], writes=[O])
                    fin_tile(O, col, sink_ap)
                    if post is not None:
                        post()

                wjobs.append((stage1, stage2))

            def run_wjobs():
                prev = None
                for (s1, s2) in wjobs:
                    st_ = s1()
                    if prev is not None:
                        prev[0](prev[1])
                    prev = (s2, st_)
                if prev is not None:
                    prev[0](prev[1])
                del wjobs[:]

            def addbias(bank, c0, c1, tab, tab_ap):
                P.op("dve", lambda e: e.tensor_tensor(out=bank[:, c0:c1], in0=bank[:, c0:c1], in1=tab_ap, op=ALU.add), reads=[bank, tab], writes=[bank])

            def run_A():
                for h in range(4):
                    kt, vt = load_kv(KA + h, VA + h)
                    gt = gtr.next()
                    P.dma("sp", gt[:], gtab_d[l, h], writes=[gt])
                    gg, ge = ggr.next(), ger.next()
                    for (g_, mi) in ((gg, 0), (ge, 2)):
                        P.op("dve", lambda e: e.tensor_tensor(out=g_[:], in0=gt[:], in1=amask[:, mi, :], op=ALU.mult), reads=[gt, amask], writes=[g_])
                        P.op("dve", lambda e: e.tensor_tensor(out=g_[:], in0=g_[:], in1=amask[:, mi + 1, :], op=ALU.add), reads=[g_, amask], writes=[g_])
                    for (t0, N, which) in qgroups:
                        qh_ = {}
                        q = (lambda qh_=qh_, h=h, t0=t0, N=N: qh_["q"] if "q" in qh_ else qh_.setdefault("q", load_q(QA + h, t0, N)))
                        for a in range(N // 128):
                            qt = t0 // 128 + a
                            post = (lambda h=h, t0=t0, N=N: fin_group(h, t0, N)) if a == N // 128 - 1 else None
                            if which == 1:
                                window_tile(kt, vt, q, a, [32, 33], lambda bA, bB: None, None, a * 128, post)
                                continue
                            if 2 <= qt <= 29:
                                d0, nw, tab = 2, 5, gg
                            elif qt == 0:
                                d0, nw, tab = 3, 4, ge
                            elif qt == 1:
                                d0, nw, tab = 2, 4, ge
                            elif qt == 30:
                                d0, nw, tab = 1, 4, ge
                            else:
                                d0, nw, tab = 0, 4, ge
                            m0 = 6 - 2 * d0
                            blocks = [qt + d0 - i for i in range(nw)] + [32, 33]

                            def bias_fn(bA, bB, m0=m0, nw=nw, tab=tab):
                                addbias(bA, 0, 512, tab, tab[:, m0 * 64:m0 * 64 + 512])
                                if nw == 5:
                                    addbias(bB, 0, 128, tab, tab[:, (m0 + 8) * 64:(m0 + 10) * 64])
                            window_tile(kt, vt, q, a, blocks, bias_fn, None, a * 128, post)
                    run_wjobs()

            def run_B():
                kt = vt = None
                for h in range(4):
                    if h % 2 == 0:
                        kt, vt = load_kv(KB + h // 2, VB + h // 2)
                    sink_ap = esink[:, l * 4 + h:l * 4 + h + 1]
                    for (t0, N, which) in qgroups:
                        qh_ = {}
                        q = (lambda qh_=qh_, h=h, t0=t0, N=N: qh_["q"] if "q" in qh_ else qh_.setdefault("q", load_q(QB + h, t0, N)))
                        for a in range(N // 128):
                            qt = t0 // 128 + a
                            post = (lambda h=h, t0=t0, N=N: fin_group(4 + h, t0, N)) if a == N // 128 - 1 else None
                            if which == 1:
                                window_tile(kt, vt, q, a, [32, 33], lambda bA, bB: None, sink_ap, a * 128, post)
                                continue
                            masked = ([qt - 1] if qt > 0 else []) + ([qt + 1] if qt < 31 else [])
                            blocks = masked + [qt, 32, 33]

                            def bias_fn(bA, bB, qt=qt):
                                if 0 < qt < 31:
                                    addbias(bA, 0, 256, trimask, trimask[:, 0:256])
                                elif qt == 0:
                                    addbias(bA, 0, 128, trimask, trimask[:, 128:256])
                                else:
                                    addbias(bA, 0, 128, trimask, trimask[:, 0:128])
                            window_tile(kt, vt, q, a, blocks, bias_fn, sink_ap, a * 128, post)
                    run_wjobs()

            run_A()
            run_B()
            run_dense("C")
            run_dense("D")
            P.barrier()


    def phase3(l):
        grp = groups if l < L - 1 else groups[:8]
        cv_step(len(cvq))
        if l + 1 < L:
            convert_small(l + 1)
            cvq.extend(ffn_pieces(l + 1))
        with ExitStack() as st:
            mr = ring("m3_", 2, [128, KC, 512], BF16, st)
            ztr = ring("z3_", 2, [128, KC, 512], F32, st, nsub=KC)
            wr = ring("w3_", 4, [128, KC, 128], BF16, st)
            hbr = ring("hb3_", 2, [128, KC, 512], BF16, st, nsub=KC)
            pools = (ring("sq3_", 3, [128, 512], BF16, st), ring("zb3_", 3, [128, 512], BF16, st), ring("st3_", 6, [128, 512], F32, st))
            tp = ring("t3_", 3, [128, 512], F32, st)
            mainr = Ring([PS[0], PS[1], PS[2], PS[3]])
            wt_ = wbt[("w_out", l, 0)]
            for (t0, N, which) in grp:
                mt = mr.next()
                P.dma("sp", mt[:, :, :N], mixT[:, :, t0:t0 + N].rearrange("h p t -> p h t"), writes=[mt])
                zt = ztr.next()
                P.dma("sp", zt[:, :, :N], xT[:, :, t0:t0 + N].rearrange("kc p t -> p kc t"), writes=zt.sub)
                for cc in range(KC):
                    w = wr.next()
                    P.dma("sp", w[:], wb["w_out"][l, cc].rearrange("p (kc n) -> p kc n", kc=KC), reads=[wt_], writes=[w])
                    ps = mainr.next()
                    for kc in range(KC):
                        P.op("pe", lambda e: e.matmul(ps[:, :N], lhsT=w[:, kc, :], rhs=mt[:, kc, :N], start=(kc == 0), stop=(kc == KC - 1)),
                             reads=[w, mt], writes=[ps])
                    P.op("dve", lambda e: e.scalar_tensor_tensor(out=zt[:, cc, :N], in0=ps[:, :N], scalar=mod(l, 2, cc, which), in1=zt[:, cc, :N],
                                                                op0=ALU.mult, op1=ALU.add), reads=[ps, zt.sub[cc], modv], writes=[zt.sub[cc]])
                Mt, Rt = ln_stats(zt, N, EPS / ALPHA ** 2, pools, PS[4], PS[5])
                for kc in range(KC):
                    ln_apply(zt, kc, N, Mt, Rt, vcol("ln1_g", l, kc), vcol("ln1_b", l, kc), zt[:, kc, :N], zt.sub[kc], tp)
                P.dma("pool", xT[:, :, t0:t0 + N].rearrange("kc p t -> p kc t"), zt[:, :, :N], reads=zt.sub)
                Mt, Rt = ln_stats(zt, N, EPS, pools, PS[6], PS[7])
                hb = hbr.next()
                for kc in range(KC):
                    ln_apply(zt, kc, N, Mt, Rt, mod(l, 4, kc, which), mod(l, 3, kc, which), hb[:, kc, :N], hb.sub[kc], tp)
                c0 = h2col(t0, which) + 1
                P.dma("pool", h2T[:, :, c0:c0 + N].rearrange("kc p t -> p kc t"), hb[:, :, :N], reads=hb.sub)
            P.barrier()

    def phase4(l):
        last = (l == L - 1)
        grp = [(456 * n, 456, 0) for n in range(8)] + [(3648, 448, 0)]
        if not last:
            grp.append((SEQ, CTX, 1))
        with ExitStack() as st:
            h2 = sb("h4", [128, KC, 512], BF16, st)
            zt = sb("z4", [128, KC, 512], F32, st, nsub=KC)
            act = sb("act4", [128, FC, 512], BF16, st)
            hb = sb("hb4", [128, KC, 512], BF16, st, nsub=KC) if not last else None
            wgr = ring("wg4_", 3, [128, KC, 128], BF16, st)
            wur = ring("wu4_", 3, [128, KC, 128], BF16, st)
            wdr = ring("wd4_", 2, [128, FC, 128], BF16, st)
            gbr = ring("gb4_", 3, [128, 512], F32, st)
            a1r = ring("a14_", 3, [128, 512], F32, st)
            sr = ring("s4_", 2, [128, 512], F32, st)
            pools = (ring("sq4_", 2, [128, 512], BF16, st), ring("zb4_", 2, [128, 512], BF16, st), ring("st4_", 3, [128, 512], F32, st))
            tp = ring("t4_", 3, [128, 512], F32, st)
            ost = sb("ost4", [128, D], F32, st) if last else None
            gr = Ring([PS[0], PS[1]])
            ur = Ring([PS[2], PS[3]])
            dr = Ring([PS[0], PS[1], PS[2], PS[3]])
            o_w, _ = VEC[("conv_w", l)]
            o_b, _ = VEC[("conv_b", l)]

            def make_tail(t0, N, which):
                def tail():
                    if not last:
                        P.dma("pool", xT[:, :, t0:t0 + N].rearrange("kc p t -> p kc t"), zt[:, :, :N], reads=zt.sub)
                        Mt, Rt = ln_stats(zt, N, EPS, pools, PS[4], PS[7])
                        for kc in range(KC):
                            ln_apply(zt, kc, N, Mt, Rt, mod(l + 1, 1, kc, which), mod(l + 1, 0, kc, which), hb[:, kc, :N], hb.sub[kc], tp)
                        P.dma("pool", hT[:, :, t0:t0 + N].rearrange("kc p t -> p kc t"), hb[:, :, :N], reads=hb.sub)
                    else:
                        for c_ in range(0, N, 128):
                            w_ = min(128, N - c_)
                            for k4 in range(4):
                                ps = dr.next()
                                for i in range(4):
                                    kc = k4 * 4 + i
                                    P.op("pe", lambda e: e.transpose(out=ps[0:w_, i * 128:(i + 1) * 128], in_=zt[:, kc, c_:c_ + w_], identity=identf[:]),
                                         reads=[zt.sub[kc], identf], writes=[ps])
                                if k4 % 2 == 0:
                                    P.op("act", lambda e: e.activation(out=ost[0:w_, k4 * 512:(k4 + 1) * 512], in_=ps[0:w_, :], func=AF.Copy), reads=[ps], writes=[ost])
                                else:
                                    P.op("dve", lambda e: e.tensor_copy(out=ost[0:w_, k4 * 512:(k4 + 1) * 512], in_=ps[0:w_, :]), reads=[ps], writes=[ost])
                            P.dma("pool", out_d[t0 + c_:t0 + c_ + w_, :], ost[0:w_, :], reads=[ost])
                return tail

            pending = None
            for (t0, N, which) in grp:
                cv_step(2)
                c0 = h2col(t0, which)
                P.dma("sp", h2[:, :, :N + 2], h2T[:, :, c0:c0 + N + 2].rearrange("kc p t -> p kc t"), writes=[h2])
                if pending is None:
                    P.dma("sp", zt[:, :, :N], xT[:, :, t0:t0 + N].rearrange("kc p t -> p kc t"), writes=zt.sub)
                for j in range(FC):
                    if j == 3 and pending is not None:
                        pending()
                        pending = None
                        P.dma("sp", zt[:, :, :N], xT[:, :, t0:t0 + N].rearrange("kc p t -> p kc t"), writes=zt.sub)
                    wg = wgr.next()
                    P.dma("sp", wg[:], wb["w_gate"][l, j].rearrange("p (kc n) -> p kc n", kc=KC), reads=[wdep("w_gate", l, j)], writes=[wg])
                    wu = wur.next()
                    P.dma("sp", wu[:], wb["w_up"][l, j].rearrange("p (kc n) -> p kc n", kc=KC), reads=[wdep("w_up", l, j)], writes=[wu])
                    psG, psU = gr.next(), ur.next()
                    for kc in range(KC):
                        P.op("pe", lambda e: e.matmul(psG[:, :N + 2], lhsT=wg[:, kc, :], rhs=h2[:, kc, 0:N + 2], start=(kc == 0), stop=(kc == KC - 1)),
                             reads=[wg, h2], writes=[psG])
                    for kc in range(KC):
                        P.op("pe", lambda e: e.matmul(psU[:, :N], lhsT=wu[:, kc, :], rhs=h2[:, kc, 1:N + 1], start=(kc == 0), stop=(kc == KC - 1)),
                             reads=[wu, h2], writes=[psU])
                    gb = gbr.next()
                    P.op("act", lambda e: e.activation(out=gb[:, :N + 2], in_=psG[:, :N + 2], func=AF.Copy), reads=[psG], writes=[gb])
                    a1 = a1r.next()
                    w0 = vecs[:, o_w + j * 3 + 0:o_w + j * 3 + 1]
                    w1 = vecs[:, o_w + j * 3 + 1:o_w + j * 3 + 2]
                    w2 = vecs[:, o_w + j * 3 + 2:o_w + j * 3 + 3]
                    cb = vecs[:, o_b + j:o_b + j + 1]
                    P.op("act", lambda e: e.activation(out=a1[:, :N], in_=gb[:, 1:N + 1], func=AF.Identity, scale=w1, bias=cb), reads=[gb, vecs], writes=[a1])
                    P.op("dve", lambda e: e.scalar_tensor_tensor(out=a1[:, :N], in0=gb[:, 0:N], scalar=w0, in1=a1[:, :N], op0=ALU.mult, op1=ALU.add),
                         reads=[gb, a1, vecs], writes=[a1])
                    P.op("dve", lambda e: e.scalar_tensor_tensor(out=a1[:, :N], in0=gb[:, 2:N + 2], scalar=w2, in1=a1[:, :N], op0=ALU.mult, op1=ALU.add),
                         reads=[gb, a1, vecs], writes=[a1])
                    sg = sr.next()
                    P.op("act", lambda e: e.activation(out=sg[:, :N], in_=a1[:, :N], func=AF.Silu), reads=[a1], writes=[sg])
                    P.op("dve", lambda e: e.tensor_tensor(out=act[:, j, :N], in0=sg[:, :N], in1=psU[:, :N], op=ALU.mult), reads=[sg, psU], writes=[act])
                for cc in range(KC):
                    wd = wdr.next()
                    P.dma("sp", wd[:], wb["w_down"][l, cc].rearrange("p (j n) -> p j n", j=FC), reads=[wdep("w_down", l, cc)], writes=[wd])
                    ps = dr.next()
                    for j in range(FC):
                        P.op("pe", lambda e: e.matmul(ps[:, :N], lhsT=wd[:, j, :], rhs=act[:, j, :N], start=(j == 0), stop=(j == FC - 1)),
                             reads=[wd, act], writes=[ps])
                    P.op("dve", lambda e: e.scalar_tensor_tensor(out=zt[:, cc, :N], in0=ps[:, :N], scalar=mod(l, 5, cc, which), in1=zt[:, cc, :N],
                                                                op0=ALU.mult, op1=ALU.add), reads=[ps, zt.sub[cc], modv], writes=[zt.sub[cc]])
                Mt, Rt = ln_stats(zt, N, EPS / ALPHA ** 2, pools, PS[5], PS[6])
                for kc in range(KC):
                    ln_apply(zt, kc, N, Mt, Rt, vcol("ln2_g", l, kc), vcol("ln2_b", l, kc), zt[:, kc, :N], zt.sub[kc], tp)
                pending = make_tail(t0, N, which)
            pending()
            P.barrier()

    for l in range(L):
        for nm, ph in (("p1", phase1), ("p2", phase2), ("p3", phase3), ("p4", phase4)):
            ph(l)
            if stop_after == "%s_%d" % (nm, l):
                return finish()
    return finish()


_NC_CACHE = {}


def kernel(**inp):
    inp = {k: np.asarray(v) for k, v in inp.items()}
    sh = prep_shared(inp)
    if "nc" not in _NC_CACHE:
        _NC_CACHE["nc"] = build()[0]
    nc = _NC_CACHE["nc"]
    B = inp["x"].shape[0]
    in_maps = []
    for b in range(B):
        m = dict(sh)
        m["xin"] = np.ascontiguousarray(np.concatenate([inp["x"][b], inp["ctx"][b]], 0))
        cv = np.stack([inp["c"][b].reshape(16, 128).T, inp["c_ctx"].reshape(16, 128).T], -1).reshape(128, 32)
        m["cvec"] = np.ascontiguousarray(cv)
        in_maps.append(m)
    res = run_bass_kernel_spmd(nc, in_maps, core_ids=list(range(B)))
    return np.stack([np.asarray(r["out"]) for r in res.results], 0).astype(np.float32)
```
